# Optimizing a Trainium2 kernel written in Bass

```python
import math
import jax, jax.numpy as jnp
from jax import lax
import numpy as np

D_MODEL = 1024
BATCH = 32
SEQ = 256
DEPTH = 2
DEC_BATCH = 2
DEC_SEQ = 4096
PAST_LEN = 256

GRID_W = 64
N_HEADS = 8
N_KV_HEADS = 2
HEAD_DIM = 64
ATTN_WIDTH = N_HEADS * HEAD_DIM
KV_WIDTH = N_KV_HEADS * HEAD_DIM
CONV_WIDTH = D_MODEL // 4
CONV_GROUPS = 4
CONV_KSIZE = 31
HYENA_WIDTH = D_MODEL // 4
HYENA_ORDER = 2
HYENA_SHORT = 3
HYENA_POS_BANDS = 16
HYENA_POS_DIM = 1 + 2 * HYENA_POS_BANDS
HYENA_FILTER_HIDDEN = 64
HYENA_MIN_DECAY = math.log(1e-2) / 1.5
HYENA_MAX_DECAY = math.log(1e-2) / 0.3
MIX_WIDTH = ATTN_WIDTH + CONV_WIDTH + HYENA_WIDTH
IN_WIDTH = ATTN_WIDTH + 2 * KV_WIDTH + 2 * CONV_WIDTH + (HYENA_ORDER + 1) * HYENA_WIDTH
IN_SPLITS = (ATTN_WIDTH, ATTN_WIDTH + KV_WIDTH, ATTN_WIDTH + 2 * KV_WIDTH,
             ATTN_WIDTH + 2 * KV_WIDTH + 2 * CONV_WIDTH)
FFN_DIM = 2816
FFN_KSIZE = 3
ROPE_BASE = 10000.0
NORM_EPS = 1e-6
Q_BLOCK = 128

kernel_name = "hybrid_conv_gqa_hyena_prefix_dit_step"


def _rms_norm(x, w):
    xf = x.astype(jnp.float32)
    y = xf * lax.rsqrt(jnp.mean(xf * xf, axis=-1, keepdims=True) + NORM_EPS)
    return (y * w.astype(jnp.float32)).astype(x.dtype)


def _dwconv(x, w, b):
    pad = (w.shape[0] - 1) // 2
    y = lax.conv_general_dilated(
        x, w[:, None, :].astype(x.dtype), window_strides=(1,), padding=[(pad, pad)],
        dimension_numbers=('NWC', 'WIO', 'NWC'), feature_group_count=x.shape[-1])
    return y + b.astype(x.dtype)


def _rope_1d(x, pos):
    d = x.shape[-1]
    inv = ROPE_BASE ** (-jnp.arange(0, d, 2, dtype=jnp.float32) / d)
    ang = pos.astype(jnp.float32)[:, None] * inv[None, :]
    cos = jnp.cos(ang)[:, None, :]
    sin = jnp.sin(ang)[:, None, :]
    xf = x.astype(jnp.float32)
    x1, x2 = xf[..., : d // 2], xf[..., d // 2:]
    return jnp.concatenate([x1 * cos - x2 * sin, x1 * sin + x2 * cos], axis=-1).astype(x.dtype)


def _rope_2d(x, rows, cols):
    half = x.shape[-1] // 2
    return jnp.concatenate([_rope_1d(x[..., :half], rows), _rope_1d(x[..., half:], cols)], axis=-1)


def _grid_positions(length):
    n_rows = length // GRID_W
    r, col = jnp.meshgrid(jnp.arange(n_rows, dtype=jnp.int32), jnp.arange(GRID_W, dtype=jnp.int32), indexing='ij')
    return r.reshape(-1), col.reshape(-1)


def _blocked_attention(q, k, v):
    b, lq, h, d = q.shape
    g = h // N_KV_HEADS
    nb = lq // Q_BLOCK
    qb = q.reshape(b, nb, Q_BLOCK, N_KV_HEADS, g, d).transpose(1, 0, 2, 3, 4, 5)
    scale = d ** -0.5

    def one_block(qi):
        s = jnp.einsum('bqkgd,bskd->bkgqs', qi, k, preferred_element_type=jnp.float32) * scale
        p = jax.nn.softmax(s, axis=-1).astype(v.dtype)
        return jnp.einsum('bkgqs,bskd->bqkgd', p, v)

    o = lax.map(one_block, qb)
    return o.transpose(1, 0, 2, 3, 4, 5).reshape(b, lq, h * d)


def _conformer_conv(z, p):
    a, gate = jnp.split(z, 2, axis=-1)
    u = a * jax.nn.sigmoid(gate)
    u = _dwconv(u, p['conv_dw_w'], p['conv_dw_b'])
    b, L, C = u.shape
    uf = u.astype(jnp.float32).reshape(b, L, CONV_GROUPS, C // CONV_GROUPS)
    mu = jnp.mean(uf, axis=-1, keepdims=True)
    var = jnp.mean(jnp.square(uf - mu), axis=-1, keepdims=True)
    uf = ((uf - mu) * lax.rsqrt(var + NORM_EPS)).reshape(b, L, C)
    u = (uf * p['conv_gn_w'].astype(jnp.float32) + p['conv_gn_b'].astype(jnp.float32)).astype(z.dtype)
    u = jax.nn.silu(u)
    return u @ p['conv_pw_w'] + p['conv_pw_b']


def _hyena_filter_fft(length, p):
    f32 = jnp.float32
    t = jnp.linspace(0.0, 1.0, length, dtype=f32)[:, None]
    w_ang = 2.0 * math.pi * jnp.arange(length, dtype=f32)[:, None] / length
    bands = jnp.linspace(1e-4, HYENA_POS_BANDS - 1, HYENA_POS_BANDS, dtype=f32)[None, :]
    z = jnp.concatenate([t, jnp.cos(bands * w_ang), jnp.sin(-bands * w_ang)], axis=-1)
    fr = p['hy_freq'].astype(f32)
    hid = jnp.sin(fr * (z @ p['hy_f_w1'].astype(f32) + p['hy_f_b1'].astype(f32)))
    hid = jnp.sin(fr * (hid @ p['hy_f_w2'].astype(f32) + p['hy_f_b2'].astype(f32)))
    h = (hid @ p['hy_f_w3'].astype(f32) + p['hy_f_b3'].astype(f32)).reshape(length, HYENA_ORDER, 2, HYENA_WIDTH)
    deltas = jnp.abs(jnp.linspace(HYENA_MIN_DECAY, HYENA_MAX_DECAY, HYENA_WIDTH, dtype=f32))
    h = h * jnp.exp(-t * deltas[None, :])[:, None, None, :]
    h_fwd, h_bwd = h[:, :, 0], h[:, :, 1]
    two_sided = jnp.concatenate([h_fwd, jnp.zeros_like(h_fwd[:1]), h_bwd[1:][::-1]], axis=0)
    return jnp.fft.rfft(two_sided, axis=0)


def _fft_conv(u, kf, bias):
    L = u.shape[1]
    U = jnp.fft.rfft(u, n=2 * L, axis=1)
    y = jnp.fft.irfft(U * kf[None], n=2 * L, axis=1)[:, :L]
    return y + u * bias


def _hyena(z, p):
    z = _dwconv(z, p['hy_short_w'], p['hy_short_b'])
    v, g1, g2 = jnp.split(z.astype(jnp.float32), 3, axis=-1)
    kf = _hyena_filter_fft(z.shape[1], p)
    bias = p['hy_bias'].astype(jnp.float32)
    u = _fft_conv(v, kf[:, 0], bias[0]) * g1
    u = _fft_conv(u, kf[:, 1], bias[1]) * g2
    return u.astype(z.dtype)


def _layer(x, cond, p, pos, ctx_kv):
    b, L, _ = x.shape
    mod = jax.nn.silu(cond) @ p['w_mod'] + p['b_mod']
    sh1, sc1, g1, sh2, sc2, g2 = jnp.split(mod[:, None, :], 6, axis=-1)

    h = _rms_norm(x, p['norm1']) * (1 + sc1) + sh1
    proj = h @ p['w_in']
    q, k, v, zc, zh = jnp.split(proj, IN_SPLITS, axis=-1)
    q = _rms_norm(q.reshape(b, L, N_HEADS, HEAD_DIM), p['q_norm'])
    k = _rms_norm(k.reshape(b, L, N_KV_HEADS, HEAD_DIM), p['k_norm'])
    v = v.reshape(b, L, N_KV_HEADS, HEAD_DIM)
    if pos is None:
        keys, vals = k, v
    else:
        rows, cols = pos
        q = _rope_2d(q, rows, cols)
        k = _rope_2d(k, rows, cols)
        keys = jnp.concatenate([k, ctx_kv[0].astype(k.dtype)], axis=1)
        vals = jnp.concatenate([v, ctx_kv[1].astype(v.dtype)], axis=1)
    attn_out = _blocked_attention(q, keys, vals)
    conv_out = _conformer_conv(zc, p)
    hy_out = _hyena(zh, p)
    mix = jnp.concatenate([attn_out, conv_out, hy_out], axis=-1) @ p['w_o']
    x = x + g1 * mix

    h2 = _rms_norm(x, p['norm2']) * (1 + sc2) + sh2
    u = _dwconv(h2 @ p['ffn_w_up'], p['ffn_dw_w'], p['ffn_dw_b'])
    val, gate = jnp.split(u, 2, axis=-1)
    x = x + g2 * ((jax.nn.silu(gate) * val) @ p['ffn_w_down'])
    return x, k, v


def setup_inputs(seed: int = 0) -> dict:
    key = jax.random.key(seed)
    ks = iter(jax.random.split(key, 40))
    f32 = jnp.float32

    def nrm(shape, std):
        return jax.random.normal(next(ks), shape, f32) * std

    D, FH, C = D_MODEL, HYENA_FILTER_HIDDEN, HYENA_WIDTH
    return {
        'x_prompt': nrm((BATCH, SEQ, D), 1.0),
        'x_sample': nrm((DEC_BATCH, DEC_SEQ, D), 1.0),
        'cache_k': nrm((DEC_BATCH, DEPTH, PAST_LEN, N_KV_HEADS, HEAD_DIM), 1.0),
        'cache_v': nrm((DEC_BATCH, DEPTH, PAST_LEN, N_KV_HEADS, HEAD_DIM), 1.0),
        'c': nrm((DEC_BATCH, D), 1.0),
        'c_ctx': nrm((D,), 1.0),
        'norm1': 1.0 + nrm((DEPTH, D), 0.02),
        'norm2': 1.0 + nrm((DEPTH, D), 0.02),
        'w_mod': nrm((DEPTH, D, 6 * D), 0.5 * D ** -0.5),
        'b_mod': nrm((DEPTH, 6 * D), 0.02),
        'w_in': nrm((DEPTH, D, IN_WIDTH), D ** -0.5),
        'q_norm': 1.0 + nrm((DEPTH, HEAD_DIM), 0.02),
        'k_norm': 1.0 + nrm((DEPTH, HEAD_DIM), 0.02),
        'conv_dw_w': nrm((DEPTH, CONV_KSIZE, CONV_WIDTH), CONV_KSIZE ** -0.5),
        'conv_dw_b': nrm((DEPTH, CONV_WIDTH), 0.02),
        'conv_gn_w': 1.0 + nrm((DEPTH, CONV_WIDTH), 0.02),
        'conv_gn_b': nrm((DEPTH, CONV_WIDTH), 0.02),
        'conv_pw_w': nrm((DEPTH, CONV_WIDTH, CONV_WIDTH), CONV_WIDTH ** -0.5),
        'conv_pw_b': nrm((DEPTH, CONV_WIDTH), 0.02),
        'hy_short_w': nrm((DEPTH, HYENA_SHORT, (HYENA_ORDER + 1) * C), HYENA_SHORT ** -0.5),
        'hy_short_b': nrm((DEPTH, (HYENA_ORDER + 1) * C), 0.02),
        'hy_f_w1': nrm((DEPTH, HYENA_POS_DIM, FH), HYENA_POS_DIM ** -0.5),
        'hy_f_b1': nrm((DEPTH, FH), 0.1),
        'hy_f_w2': nrm((DEPTH, FH, FH), FH ** -0.5),
        'hy_f_b2': nrm((DEPTH, FH), 0.1),
        'hy_f_w3': nrm((DEPTH, FH, HYENA_ORDER * 2 * C), 0.03 * FH ** -0.5),
        'hy_f_b3': nrm((DEPTH, HYENA_ORDER * 2 * C), 0.005),
        'hy_freq': 1.0 + nrm((DEPTH, FH), 0.1),
        'hy_bias': nrm((DEPTH, HYENA_ORDER, C), 0.5),
        'w_o': nrm((DEPTH, MIX_WIDTH, D), MIX_WIDTH ** -0.5),
        'ffn_w_up': nrm((DEPTH, D, 2 * FFN_DIM), D ** -0.5),
        'ffn_dw_w': nrm((DEPTH, FFN_KSIZE, 2 * FFN_DIM), FFN_KSIZE ** -0.5),
        'ffn_dw_b': nrm((DEPTH, 2 * FFN_DIM), 0.02),
        'ffn_w_down': nrm((DEPTH, FFN_DIM, D), FFN_DIM ** -0.5),
        'final_norm': 1.0 + nrm((D,), 0.02),
    }


def reference(x_prompt, x_sample, cache_k, cache_v, c, c_ctx, norm1, norm2, w_mod, b_mod, w_in,
              q_norm, k_norm, conv_dw_w, conv_dw_b, conv_gn_w, conv_gn_b, conv_pw_w, conv_pw_b,
              hy_short_w, hy_short_b, hy_f_w1, hy_f_b1, hy_f_w2, hy_f_b2, hy_f_w3, hy_f_b3,
              hy_freq, hy_bias, w_o, ffn_w_up, ffn_dw_w, ffn_dw_b, ffn_w_down, final_norm):
    pos = _grid_positions(x_sample.shape[1])
    cond_ctx = c_ctx[None, :]
    xp, xs = x_prompt, x_sample
    ks_new, vs_new = [], []
    for l in range(DEPTH):
        p = {
            'norm1': norm1[l], 'norm2': norm2[l], 'w_mod': w_mod[l], 'b_mod': b_mod[l],
            'w_in': w_in[l], 'q_norm': q_norm[l], 'k_norm': k_norm[l],
            'conv_dw_w': conv_dw_w[l], 'conv_dw_b': conv_dw_b[l], 'conv_gn_w': conv_gn_w[l],
            'conv_gn_b': conv_gn_b[l], 'conv_pw_w': conv_pw_w[l], 'conv_pw_b': conv_pw_b[l],
            'hy_short_w': hy_short_w[l], 'hy_short_b': hy_short_b[l],
            'hy_f_w1': hy_f_w1[l], 'hy_f_b1': hy_f_b1[l], 'hy_f_w2': hy_f_w2[l], 'hy_f_b2': hy_f_b2[l],
            'hy_f_w3': hy_f_w3[l], 'hy_f_b3': hy_f_b3[l], 'hy_freq': hy_freq[l], 'hy_bias': hy_bias[l],
            'w_o': w_o[l], 'ffn_w_up': ffn_w_up[l], 'ffn_dw_w': ffn_dw_w[l], 'ffn_dw_b': ffn_dw_b[l],
            'ffn_w_down': ffn_w_down[l],
        }
        xp, k_ctx, v_ctx = _layer(xp, cond_ctx, p, None, None)
        ks_new.append(k_ctx)
        vs_new.append(v_ctx)
        xs, _, _ = _layer(xs, c, p, pos, (cache_k[:, l], cache_v[:, l]))
    y_prompt = _rms_norm(xp, final_norm)
    y_sample = _rms_norm(xs, final_norm)
    new_k = jnp.stack(ks_new, axis=1)
    new_v = jnp.stack(vs_new, axis=1)
    return (y_prompt, y_sample, new_k, new_v)
```

```python
import contextlib
import os
import math
import numpy as np
import ml_dtypes
import concourse.bass as bass
import concourse.mybir as mybir
from concourse.bass_utils import run_bass_kernel_spmd

F32 = mybir.dt.float32
BF16 = mybir.dt.bfloat16
ALU = mybir.AluOpType
AF = mybir.ActivationFunctionType

D = 1024
DEPTH = 2
NCORE = 8
EPS = 1e-6
FFN = 2816
NJ = 22
L_S = 4096
L_P = 256
PAYROWS = 1032
NOHY = bool(os.environ.get('NOHY'))
BF16_CONSTS = ['fW1', 'fGc', 'fGs', 'fE1', 'fE2', 'fVc', 'fVsn', 'dftC', 'dftS', 'dftSn', 'idftC', 'idftSn', 'selM', 'onesm', 'blk64', 'ropeRT', 'identb']


class Sched:
    ENGS = ("pe", "act", "dve", "pool", "sp")
    HND = dict(pe="tensor", act="scalar", dve="vector", pool="gpsimd", sp="sync")
    UID = 0
    GLOBAL = {}

    def __init__(self, nc):
        self.nc = nc
        self.ops = []
        self.last_w = {}
        self.readers = {}
        self.dma_keys = {}

    def _deps(self, reads, writes):
        deps = set()
        for r in reads:
            lw = self.last_w.get(r)
            if lw is not None:
                deps.add(lw)
        for w in writes:
            lw = self.last_w.get(w)
            if lw is not None:
                deps.add(lw)
            for rd in self.readers.get(w, ()):
                deps.add(rd)
        return deps

    def _commit(self, oid, reads, writes):
        for w in writes:
            self.last_w[w] = oid
            self.readers[w] = []
        for r in reads:
            if r in writes:
                continue
            self.readers.setdefault(r, []).append(oid)

    def op(self, eng, fn, reads=(), writes=()):
        reads = tuple(reads); writes = tuple(writes)
        writes = writes + tuple(r for r in reads if r.startswith("ps") and r not in writes)
        deps = self._deps(reads, writes)
        oid = len(self.ops)
        self.ops.append(dict(id=oid, eng=eng, fn=fn, deps=deps, kind="c", sig=False))
        self._commit(oid, reads, writes)
        return oid

    def dma(self, queue, out, in_, key, reads=(), writes=(), **kw):
        reads = tuple(reads); writes = tuple(writes)
        deps = self._deps(reads, writes)
        for w in writes:
            lw = self.last_w.get(w)
            if lw is not None and self.ops[lw]["kind"] == "d" and self.ops[lw]["key"] == key and self.ops[lw]["eng"] == queue:
                deps.discard(lw)
        oid = len(self.ops)
        cnt = self.dma_keys.setdefault(key, [0, "d"])
        cnt[0] += 1
        self.ops.append(dict(id=oid, eng=queue, deps=deps, kind="d", key=key,
                             cum=16 * cnt[0], out=out, in_=in_, kw=kw, sig=False))
        self._commit(oid, reads, writes)
        return oid

    def cc(self, fn, key, reads=(), writes=()):
        reads = tuple(reads); writes = tuple(writes)
        deps = self._deps(reads, writes)
        oid = len(self.ops)
        cnt = self.dma_keys.setdefault(key, [0, "cc"])
        cnt[0] += 1
        self.ops.append(dict(id=oid, eng="pool", deps=deps, kind="cc", key=key,
                             cum=cnt[0], fn=fn, sig=False))
        self._commit(oid, reads, writes)
        return oid

    def emit(self):
        nc = self.nc
        ops = self.ops
        for o in ops:
            for d in o["deps"]:
                Dd = ops[d]
                if Dd["kind"] == "c" and Dd["eng"] != o["eng"]:
                    Dd["sig"] = True
        last = {}
        for o in ops:
            if o["kind"] == "c":
                last[o["eng"]] = o
        for o in last.values():
            o["sig"] = True
        cnt = {e: 0 for e in self.ENGS}
        for o in ops:
            if o["kind"] == "c" and o["sig"]:
                cnt[o["eng"]] += 1
                o["sigval"] = cnt[o["eng"]]
        per_eng = {e: [] for e in self.ENGS}
        for o in ops:
            per_eng[o["eng"]].append(o)
        with contextlib.ExitStack() as st:
            G = Sched.GLOBAL
            if G.get("nc") is not nc:
                G.clear()
                G.update(nc=nc, esem={e: nc.alloc_semaphore(name="ge_%s" % e) for e in self.ENGS},
                         ecount={e: 0 for e in self.ENGS}, kpool=[], kcount=[])
            while len(G["kpool"]) < len(self.dma_keys):
                G["kpool"].append(nc.alloc_semaphore(name="gk_%d" % len(G["kpool"])))
                G["kcount"].append(0)
            esem = G["esem"]
            ksem = {}
            kbase = {}
            for i, k in enumerate(self.dma_keys):
                ksem[k] = G["kpool"][i]
                kbase[k] = G["kcount"][i]
            for o in ops:
                if o["kind"] == "c" and o["sig"]:
                    o["sigval"] += G["ecount"][o["eng"]]
                elif o["kind"] in ("d", "cc"):
                    o["cum"] += kbase[o["key"]]
            for e in self.ENGS:
                G["ecount"][e] += cnt[e]
            for i, (k, c) in enumerate(self.dma_keys.items()):
                G["kcount"][i] += (16 if c[1] == "d" else 1) * c[0]
            block = st.enter_context(nc.Block())

            def body_for(ename):
                def body(eng):
                    waited = {}

                    def need(sem, val, tag):
                        if waited.get(tag, 0) >= val:
                            return
                        eng.wait_ge(sem, val)
                        waited[tag] = val

                    for o in per_eng[ename]:
                        for d in sorted(o["deps"]):
                            Dd = ops[d]
                            if Dd["kind"] == "c":
                                if Dd["eng"] == ename:
                                    continue
                                need(esem[Dd["eng"]], Dd["sigval"], "e_" + Dd["eng"])
                            else:
                                need(ksem[Dd["key"]], Dd["cum"], ("k", Dd["key"]))
                        if o["kind"] == "c":
                            ins = o["fn"](eng)
                            if o["sig"]:
                                ins.then_inc(esem[ename], 1)
                        elif o["kind"] == "d":
                            eng.dma_start(out=o["out"], in_=o["in_"], **o["kw"]).then_inc(ksem[o["key"]], 16)
                        else:
                            o["fn"](eng).then_inc(ksem[o["key"]], 1)
                    if ename == "sp":
                        for k, c in self.dma_keys.items():
                            need(ksem[k], kbase[k] + (16 if c[1] == "d" else 1) * c[0], ("k", k))
                        for e2, o2 in last.items():
                            if e2 != ename:
                                need(esem[e2], o2["sigval"], "e_" + e2)
                return body

            for ename in self.ENGS:
                if per_eng[ename] or ename == "sp":
                    getattr(block, self.HND[ename])(body_for(ename))


class Phase:
    UID = 0

    def __init__(self, nc, name):
        self.nc = nc
        self.name = name
        self.st = contextlib.ExitStack()
        self.S = Sched(nc)
        self._n = 0
        self._rr = 0

    def __enter__(self):
        self.st.__enter__()
        return self

    def __exit__(self, *a):
        if a[0] is None:
            self.S.emit()
        return self.st.__exit__(*a)

    def sb(self, shape, dt, name=None):
        self._n += 1
        Phase.UID += 1
        return self.st.enter_context(self.nc.sbuf_tensor("%s_%s%d" % (self.name, name or "t", Phase.UID), list(shape), dt))

    def ps(self, shape=(128, 512), dt=F32):
        self._n += 1
        Phase.UID += 1
        return self.st.enter_context(self.nc.psum_tensor("%s_p%d" % (self.name, Phase.UID), list(shape), dt))

    def mm(self, out, lhsT, rhs, start, stop, reads, writes, **kw):
        return self.S.op("pe", lambda e: e.matmul(out, lhsT=lhsT, rhs=rhs, start=start, stop=stop, **kw), reads, writes)

    def tr(self, out, in_, ident, reads, writes):
        return self.S.op("pe", lambda e: e.transpose(out, in_, ident), reads, writes)

    def act(self, out, in_, func, reads, writes, bias=None, scale=None, eng="act"):
        kw = {}
        if bias is not None:
            kw["bias"] = bias
        if scale is not None:
            kw["scale"] = scale
        return self.S.op("act", lambda e: e.activation(out=out, in_=in_, func=func, **kw), reads, writes)

    def tt(self, eng, out, in0, in1, op, reads, writes):
        return self.S.op(eng, lambda e: e.tensor_tensor(out=out, in0=in0, in1=in1, op=op), reads, writes)

    def ts(self, eng, out, in0, s1, op0, reads, writes, s2=None, op1=None):
        if op1 is None:
            return self.S.op(eng, lambda e: e.tensor_scalar(out=out, in0=in0, scalar1=s1, scalar2=None, op0=op0), reads, writes)
        return self.S.op(eng, lambda e: e.tensor_scalar(out=out, in0=in0, scalar1=s1, scalar2=s2, op0=op0, op1=op1), reads, writes)

    def stt(self, out, in0, scalar, in1, op0, op1, reads, writes):
        return self.S.op("dve", lambda e: e.scalar_tensor_tensor(out=out, in0=in0, scalar=scalar, in1=in1, op0=op0, op1=op1), reads, writes)

    def cp(self, eng, out, in_, reads, writes):
        if eng == "act":
            return self.S.op("act", lambda e: e.activation(out=out, in_=in_, func=AF.Copy), reads, writes)
        return self.S.op(eng, lambda e: e.tensor_copy(out=out, in_=in_), reads, writes)

    def memset(self, eng, ap, val, writes):
        return self.S.op(eng, lambda e: e.memset(ap, val), (), writes)

    def dma(self, q, out, in_, key, reads=(), writes=(), **kw):
        return self.S.dma(q, out, in_, key, reads, writes, **kw)

    def rsqrt(self, out, in_, reads, writes, eps_ap):
        self.act(out, in_, AF.Sqrt, reads, writes, bias=eps_ap)
        return self.S.op("dve", lambda e: e.reciprocal(out=out, in_=out), writes, writes)


def _pt(v, n):
    return np.ascontiguousarray(np.asarray(v, np.float32).reshape(n, 128).T)


class Pack:
    def __init__(self):
        self.cols = []
        self.off = {}
        self.n = 0

    def add(self, name, arr):
        arr = np.asarray(arr, np.float32)
        if arr.ndim == 1:
            arr = arr[:, None]
        arr = arr.reshape(arr.shape[0], -1)
        if arr.shape[0] < 128:
            arr = np.concatenate([arr, np.zeros((128 - arr.shape[0], arr.shape[1]), np.float32)], 0)
        self.off[name] = (self.n, arr.shape[1])
        self.cols.append(arr)
        self.n += arr.shape[1]

    def build(self):
        return np.ascontiguousarray(np.concatenate(self.cols, 1))


def _rope_tables(tok):
    rows = tok // 64
    cols = tok % 64
    inv = 10000.0 ** (-np.arange(0, 32, 2, dtype=np.float32) / 32.0)
    cosT = np.zeros((128, len(tok)), np.float32)
    sinT = np.zeros((128, len(tok)), np.float32)
    for p in range(128):
        dd = p % 64
        pos = rows if dd < 32 else cols
        ang = pos.astype(np.float32) * inv[dd % 16]
        cosT[p] = np.cos(ang)
        sinT[p] = np.sin(ang)
    return cosT, sinT


def _consts():
    c = {}
    c["ident"] = np.eye(128, dtype=np.float32)
    c["onesm"] = np.full((128, 128), 1.0 / 1024.0, np.float32)
    b = np.zeros((128, 128), np.float32)
    b[:64, :64] = 1.0 / 64.0
    b[64:, 64:] = 1.0 / 64.0
    c["blk64"] = b
    rt = np.zeros((128, 128), np.float32)
    for m in range(128):
        if (m % 32) < 16:
            rt[m + 16, m] = -1.0
        else:
            rt[m - 16, m] = 1.0
    c["ropeRT"] = rt
    L = 256
    t = np.linspace(0.0, 1.0, L, dtype=np.float32)[:, None]
    w_ang = (2.0 * math.pi * np.arange(L, dtype=np.float32)[:, None] / L).astype(np.float32)
    bands = np.linspace(1e-4, 15, 16, dtype=np.float32)[None, :]
    z = np.concatenate([t, np.cos(bands * w_ang), np.sin(-bands * w_ang)], -1).astype(np.float32)
    c["zT256"] = np.ascontiguousarray(z.T)
    deltas = np.abs(np.linspace(math.log(1e-2) / 1.5, math.log(1e-2) / 0.3, 256, dtype=np.float32))
    dec = np.exp(-t * deltas[None, :]).astype(np.float32)
    c["decayP"] = np.ascontiguousarray(dec.reshape(2, 128, 256).transpose(1, 0, 2))
    tt = np.arange(256, dtype=np.float64)[:, None]
    ff = np.arange(512, dtype=np.float64)[None, :]
    ang = 2.0 * math.pi * tt * ff / 512.0
    cm, sm = np.cos(ang), np.sin(ang)
    lay_t = lambda a: np.ascontiguousarray(a.reshape(2, 128, 512).transpose(1, 0, 2).astype(np.float32))
    c["dftC"] = lay_t(cm); c["dftS"] = lay_t(sm); c["dftSn"] = lay_t(-sm)
    lay_f = lambda a: np.ascontiguousarray(a.T.reshape(4, 128, 256).transpose(1, 0, 2).astype(np.float32))
    c["idftC"] = lay_f(cm / 512.0); c["idftSn"] = lay_f(-sm / 512.0)
    L = 4096
    t = np.linspace(0.0, 1.0, L, dtype=np.float32)[:, None]
    w_ang = (2.0 * math.pi * np.arange(L, dtype=np.float32)[:, None] / L).astype(np.float32)
    z = np.concatenate([t, np.cos(bands * w_ang), np.sin(-bands * w_ang)], -1).astype(np.float32)
    c["zT4096"] = np.ascontiguousarray(z.T)
    t1 = np.arange(32, dtype=np.float64)[:, None]
    f1 = np.arange(64, dtype=np.float64)[None, :]
    a1 = 2.0 * math.pi * t1 * f1 / 64.0
    w1_ = np.zeros((128, 192), np.float32)
    w1_[:32] = np.concatenate([np.cos(a1), -np.sin(a1), -np.cos(a1)], 1)
    c["fW1"] = w1_
    t2 = np.arange(128, dtype=np.float64)[:, None, None]
    f1_ = np.arange(64, dtype=np.float64)[None, :, None]
    f2 = np.arange(128, dtype=np.float64)[None, None, :]
    ag = 2.0 * math.pi * t2 * (f1_ + 64.0 * f2) / 8192.0
    c["fGc"] = np.cos(ag).astype(np.float32).reshape(128, 8192)
    c["fGs"] = np.sin(ag).astype(np.float32).reshape(128, 8192)
    f2c = np.arange(128, dtype=np.float64)[:, None]
    t2r = np.arange(128, dtype=np.float64)[None, :]
    ae = 2.0 * math.pi * f2c * t2r / 128.0
    c["fE1"] = np.concatenate([np.cos(ae), np.sin(ae)], 1).astype(np.float32)
    c["fE2"] = np.concatenate([-np.sin(ae), np.cos(ae)], 1).astype(np.float32)
    f1c = np.arange(64, dtype=np.float64)[:, None, None]
    t2m = np.arange(128, dtype=np.float64)[None, :, None]
    t1m = np.arange(32, dtype=np.float64)[None, None, :]
    av = 2.0 * math.pi * f1c * (128.0 * t1m + t2m) / 8192.0
    c["fVc"] = (np.cos(av) / 8192.0).astype(np.float32).reshape(64, 4096)
    c["fVsn"] = (-np.sin(av) / 8192.0).astype(np.float32).reshape(64, 4096)
    c["_deltas"] = deltas
    return c


def host_prep(inp):
    cs = _consts()
    maps = []
    for i in range(NCORE):
        b, q = i // 4, i % 4
        m = {}
        m["x_tok"] = np.ascontiguousarray(np.concatenate(
            [inp["x_prompt"][4 * i:4 * i + 4].reshape(1024, D), inp["x_sample"][b, 1024 * q:1024 * q + 1024]], 0).astype(np.float32))
        cond2 = np.stack([inp["c_ctx"], inp["c"][b]], 0).astype(np.float32)
        m["condT"] = np.ascontiguousarray(cond2.reshape(2, 8, 128).transpose(2, 1, 0))
        m["cache_k"] = np.ascontiguousarray(inp["cache_k"][b].reshape(DEPTH, 256, 128).astype(np.float32))
        m["cache_v"] = np.ascontiguousarray(inp["cache_v"][b].reshape(DEPTH, 256, 128).astype(np.float32))
        for nm in ["w_mod", "w_in", "w_o", "ffn_w_up", "ffn_w_down", "conv_pw_w", "hy_f_w1", "hy_f_w2", "hy_f_w3"]:
            m[nm] = np.ascontiguousarray(inp[nm].astype(np.float32))
        m["hy_f_b3r"] = np.ascontiguousarray(inp["hy_f_b3"].reshape(DEPTH, 1, 1024).astype(np.float32))
        m["hy_biasr"] = np.ascontiguousarray(inp["hy_bias"].reshape(DEPTH, 1, 512).astype(np.float32))
        pk = Pack()
        for l in range(DEPTH):
            pk.add("norm1_%d" % l, _pt(inp["norm1"][l], 8))
            pk.add("norm2_%d" % l, _pt(inp["norm2"][l], 8))
            pk.add("bmod_%d" % l, _pt(inp["b_mod"][l], 48))
            pk.add("qw_%d" % l, np.tile(inp["q_norm"][l], 2))
            pk.add("kw_%d" % l, np.tile(inp["k_norm"][l], 2))
            pk.add("cdw_%d" % l, inp["conv_dw_w"][l].T.reshape(2, 128, 31).transpose(1, 0, 2))
            pk.add("cdb_%d" % l, _pt(inp["conv_dw_b"][l], 2))
            pk.add("gnw_%d" % l, _pt(inp["conv_gn_w"][l], 2))
            pk.add("gnb_%d" % l, _pt(inp["conv_gn_b"][l], 2))
            pk.add("pwb_%d" % l, _pt(inp["conv_pw_b"][l], 2))
            pk.add("fdw_%d" % l, inp["ffn_dw_w"][l].T.reshape(44, 128, 3).transpose(1, 0, 2))
            pk.add("fdb_%d" % l, _pt(inp["ffn_dw_b"][l], 44))
            pk.add("hsw_%d" % l, inp["hy_short_w"][l].T.reshape(6, 128, 3).transpose(1, 0, 2))
            pk.add("hsb_%d" % l, _pt(inp["hy_short_b"][l], 6))
            pk.add("hfr_%d" % l, inp["hy_freq"][l])
            pk.add("hfb1_%d" % l, inp["hy_f_b1"][l])
            pk.add("hfb2_%d" % l, inp["hy_f_b2"][l])
        pk.add("fnorm", _pt(inp["final_norm"], 8))
        eq = np.zeros((128, 4), np.float32); eq[:, q] = 1.0
        el = np.zeros((128, 4), np.float32)
        er = np.zeros((128, 4), np.float32)
        if q > 0:
            el[:, q - 1] = 1.0
        if q < 3:
            er[:, q + 1] = 1.0
        pk.add("eq", eq); pk.add("el", el); pk.add("er", er)
        chs = 64 * q + np.arange(64)
        for l in range(DEPTH):
            hw = inp["hy_short_w"][l]
            pk.add("hswm_%d" % l, np.stack([hw[:, blk * 256 + chs].T for blk in range(3)], 1))
            pk.add("hsbm_%d" % l, np.stack([inp["hy_short_b"][l][blk * 256 + chs] for blk in range(3)], 1))
            b3 = inp["hy_f_b3"][l].reshape(2, 2, 256)
            pk.add("hb3m_%d" % l, np.stack([np.concatenate([b3[o, 0, chs], b3[o, 1, chs]]) for o in range(2)], 1))
            pk.add("hbm_%d" % l, inp["hy_bias"][l][:, chs].T)
        pk.add("eps", np.full((128, 1), EPS, np.float32))
        m["pp"] = pk.build()
        tok = np.arange(1024 * q, 1024 * q + 1024)
        cosT, sinT = _rope_tables(tok)
        m["ropec"] = cosT
        m["ropes"] = sinT
        sel = np.zeros((128, 2, 64), np.float32)
        for mm_ in range(64):
            ch = 64 * q + mm_
            sel[ch % 128, ch // 128, mm_] = 1.0
        m["selM"] = sel
        w3 = inp["hy_f_w3"].reshape(DEPTH, 64, 2, 2, 256)
        m["hy_f_w3m"] = np.ascontiguousarray(w3[:, :, :, :, chs].reshape(DEPTH, 64, 256).astype(np.float32))
        tS = np.linspace(0.0, 1.0, 4096, dtype=np.float32)[None, :]
        dS = np.exp(-tS * cs["_deltas"][chs][:, None]).astype(np.float32)
        m["decayS"] = np.ascontiguousarray(np.concatenate([dS, dS], 0))
        for k, v in cs.items():
            if not k.startswith("_"):
                m[k] = v
        m["identb"] = cs["ident"]
        for k in BF16_CONSTS:
            m[k] = np.ascontiguousarray(m[k]).astype(ml_dtypes.bfloat16)
        maps.append(m)
    return maps, pk.off


class Ctx:
    pass


def build(ppoff, npp, stop_after=None, dbg=None, extra_shapes=None, only=None):
    nc = bass.Bass("TRN2", target_bir_lowering=False)
    C = Ctx()
    C.nc = nc
    C.ppoff = ppoff
    do = lambda n, s, dt=F32: nc.dram_tensor(n, list(s), dt, kind="ExternalOutput").ap()
    C.declared = {}
    shapes = dict(x_tok=[2048, D], condT=[128, 8, 2], cache_k=[DEPTH, 256, 128], cache_v=[DEPTH, 256, 128],
                  w_mod=[DEPTH, D, 6 * D], w_in=[DEPTH, D, 2048], w_o=[DEPTH, D, D], ffn_w_up=[DEPTH, D, 2 * FFN],
                  ffn_w_down=[DEPTH, FFN, D], conv_pw_w=[DEPTH, 256, 256], pp=[128, npp], ropec=[128, 1024],
                  ropes=[128, 1024], ident=[128, 128], identb=[128, 128], onesm=[128, 128], blk64=[128, 128], ropeRT=[128, 128],
                  hy_f_w1=[DEPTH, 33, 64], hy_f_w2=[DEPTH, 64, 64], hy_f_w3=[DEPTH, 64, 1024], hy_f_b3r=[DEPTH, 1, 1024],
                  hy_biasr=[DEPTH, 1, 512], zT256=[33, 256], decayP=[128, 2, 256], dftC=[128, 2, 512], dftS=[128, 2, 512],
                  dftSn=[128, 2, 512], idftC=[128, 4, 256], idftSn=[128, 4, 256],
                  selM=[128, 2, 64], hy_f_w3m=[DEPTH, 64, 256], decayS=[128, 4096], zT4096=[33, 4096], fW1=[128, 192],
                  fGc=[128, 8192], fGs=[128, 8192], fE1=[128, 256], fE2=[128, 256], fVc=[64, 4096], fVsn=[64, 4096])
    shapes.update(extra_shapes or {})

    def inp(name):
        if name not in C.declared:
            C.declared[name] = nc.dram_tensor(name, list(shapes[name]), BF16 if name in BF16_CONSTS else F32,
                                              kind="ExternalInput").ap()
        return C.declared[name]
    C.inp = inp
    C.y_tok = do("y_tok", [2048, D])
    C.new_k = do("new_k", [4, DEPTH, 256, 128])
    C.new_v = do("new_v", [4, DEPTH, 256, 128])
    C.dbg = do("dbg", dbg[1]) if dbg else None
    C.dbgname = dbg[0] if dbg else None
    dt_ = lambda n, s, dt=BF16: nc.dram_tensor(n, list(s), dt).ap()
    C.qT_d = [dt_("qT_d%d" % g, [512, 1024]) for g in range(2)]
    C.uT_d = [dt_("uT_d%d" % g, [256, 1024]) for g in range(2)]
    C.kT_d = dt_("kT_d", [128, 1024])
    C.vtok_d = dt_("vtok_d", [1024, 128])
    C.zhT_d = dt_("zhT_d", [768, 1024])
    C.payA = nc.dram_tensor("payA", [264, 1024], BF16)
    C.gatA = nc.dram_tensor("gatA", [4 * 264, 1024], BF16)
    C.payB = nc.dram_tensor("payB", [384, 1024], BF16)
    C.gatB = nc.dram_tensor("gatB", [4 * 384, 1024], BF16)
    C.payC = nc.dram_tensor("payC", [384, 1024], BF16)
    C.gatC = nc.dram_tensor("gatC", [4 * 384, 1024], BF16)
    C.pay_kT = C.payA.ap()[0:128, :]
    C.pay_V = C.payA.ap()[128:256, :].rearrange("r (a f) -> (r a) f", f=128)
    C.pay_ub = C.payA.ap()[256:264, :].rearrange("r (a f) -> (r a) f", f=32)
    C.pay_zh = lambda zc: (C.payB if zc < 3 else C.payC).ap()[(zc % 3) * 128:(zc % 3) * 128 + 128, :]
    C.gat_kT = lambda r: C.gatA.ap()[r * 264:r * 264 + 128, :]
    C.gat_V = lambda r: C.gatA.ap()[r * 264 + 128:r * 264 + 256, :].rearrange("r (a f) -> (r a) f", f=128)
    C.gat_ub = lambda r: C.gatA.ap()[r * 264 + 256:r * 264 + 264, :].rearrange("r (a f) -> (r a) f", f=32)
    C.gat_zh = lambda r, zc: (C.gatB if zc < 3 else C.gatC).ap()[r * 384 + (zc % 3) * 128:r * 384 + (zc % 3) * 128 + 128, :]
    C.pay3 = nc.dram_tensor("pay3", [128, 1024], BF16)
    C.hyz_d = dt_("hyz_d", [3, 64, 4096], F32)
    C.hyf_d = dt_("hyf_d", [2, 128, 4096], F32)
    C.Ksp_d = dt_("Ksp_d", [2, 2, 128, 4096], BF16)
    C.Y_d = dt_("Y_d", [2, 128, 4096], BF16)
    C.X2_d = dt_("X2_d", [32, 8192], BF16)
    C.payH = nc.dram_tensor("payH", [32, 8192], BF16)
    C.gatH = nc.dram_tensor("gatH", [128, 8192], BF16)
    C.gat3 = nc.dram_tensor("gat3", [4 * 128, 1024], BF16)

    with contextlib.ExitStack() as gst:
        gsb = lambda n, s, dt: gst.enter_context(nc.sbuf_tensor("g_" + n, list(s), dt))
        C.xT = gsb("xT", [128, 8, 2048], F32)
        C.hT = gsb("hT", [128, 8, 1026], BF16)
        C.mixT = C.hT
        C.pp = gsb("pp", [128, npp], F32)
        C.modT = gsb("modT", [128, DEPTH, 48, 2], F32)
        C.A1 = gsb("A1", [128, DEPTH, 8, 2], F32)
        C.A2 = gsb("A2", [128, DEPTH, 8, 2], F32)
        C.ident_f = gsb("ident_f", [128, 128], F32)
        C.ident_b = gsb("ident_b", [128, 128], BF16)
        C.onesm = gsb("onesm", [128, 128], BF16)
        C.blk64 = gsb("blk64", [128, 128], BF16)
        C.ropeRT = gsb("ropeRT", [128, 128], BF16)

        phases = []
        phases.append(("init", lambda: ph_init(C)))
        phases.append(("xin", lambda: ph_xin(C)))
        for l in range(DEPTH):
            for g in (1, 0):
                phases.append(("norm1_%d_%d" % (l, g), lambda l=l, g=g: ph_norm(C, l, g, 1)))
                phases.append(("proj_%d_%d" % (l, g), lambda l=l, g=g: ph_proj(C, l, g)))
            for g in (0, 1):
                if g == 0:
                    phases.append(("attn_%d_%d" % (l, g), lambda l=l, g=g: ph_attn(C, l, g)))
                else:
                    phases.append(("attn_%d_%d" % (l, g), lambda l=l, g=g: ph_attn(C, l, g)))
                phases.append(("conv_%d_%d" % (l, g), lambda l=l, g=g: ph_conv(C, l, g)))
                if NOHY:
                    phases.append(("wo_%d_%d" % (l, g), lambda l=l, g=g: ph_wo(C, l, g, zero_hy=True)))
                else:
                    phases.append(("hy_%d_%d" % (l, g), lambda l=l, g=g: ph_hyena(C, l, g)))
                    phases.append(("wo_%d_%d" % (l, g), lambda l=l, g=g: ph_wo(C, l, g)))
                if g == 1:
                    phases.append(("halo3_%d" % l, lambda l=l: ph_halo3(C, l)))
                phases.append(("ffn_%d_%d" % (l, g), lambda l=l, g=g: ph_ffn(C, l, g)))
        phases.append(("final", lambda: ph_final(C)))
        if only is not None:
            with Phase(nc, "ld") as P:
                P.dma("sp", C.pp[:], C.inp("pp"), key="pp", writes=["pp"])
                P.memset("dve", C.hT[:], 0.0, ["h"])
                P.memset("dve", C.xT[:], 0.0, ["x"])
                P.memset("dve", C.modT[:], 0.0, ["m"])
            only(C)
            phases = []
        for name, fn in phases:
            fn()
            if stop_after == name:
                break
        if C.dbg is not None:
            ph_dbg(C)
    C.ncobj = nc
    return nc, C


def ppa(C, name):
    o, n = C.ppoff[name]
    return C.pp[:, o:o + n]


def ph_dbg(C):
    with Phase(C.nc, "dbg") as P:
        src = getattr(C, C.dbgname)
        if hasattr(src, "ap") and not isinstance(src, bass.AP):
            src = src.ap()
        P.dma("pool", C.dbg, src, key="dbg")


def ph_init(C):
    nc = C.nc
    with Phase(nc, "init") as P:
        P.dma("sp", C.pp[:], C.inp("pp"), key="pp", writes=["pp"])
        P.dma("sp", C.ident_f[:], C.inp("ident"), key="idf", writes=["idf"])
        P.dma("sp", C.ident_b[:], C.inp("identb"), key="idb", writes=["idb"])
        P.dma("sp", C.onesm[:], C.inp("onesm"), key="onesm", writes=["onesm"])
        P.dma("sp", C.blk64[:], C.inp("blk64"), key="blk64", writes=["blk64"])
        P.dma("sp", C.ropeRT[:], C.inp("ropeRT"), key="ropeRT", writes=["ropeRT"])
        cond = P.sb([128, 8, 2], F32)
        scb = P.sb([128, 8, 2], BF16)
        P.dma("sp", cond[:], C.inp("condT"), key="cond", writes=["cond"])
        P.act(scb[:], cond[:], AF.Silu, ["cond"], ["scb"])
        wm = [P.sb([128, 8, 512], BF16, "wm") for _ in range(2)]
        psm = P.ps([128, 512])
        for l in range(DEPTH):
            for blk in range(12):
                s = blk % 2
                src = C.inp("w_mod")[l][:, blk * 512:(blk + 1) * 512].rearrange("(kc p) n -> p kc n", p=128)
                P.dma("pool", wm[s][:], src, key="wm%d" % s, writes=["wm%d" % s])
                for mcl in range(4):
                    mc = blk * 4 + mcl
                    for kc in range(8):
                        P.mm(psm[:, mc * 2:mc * 2 + 2], wm[s][:, kc, mcl * 128:(mcl + 1) * 128], scb[:, kc, :],
                             kc == 0, kc == 7, ["wm%d" % s, "scb"], ["psm"])
            bm = ppa(C, "bmod_%d" % l)
            for j in range(2):
                P.tt("dve", C.modT[:, l, :, j], psm[:, 0:96].rearrange("p (m j) -> p m j", j=2)[:, :, j], bm, ALU.add, ["psm", "pp"], ["modT"])
            n1 = ppa(C, "norm1_%d" % l)
            n2 = ppa(C, "norm2_%d" % l)
            for j in range(2):
                P.stt(C.A1[:, l, :, j], C.modT[:, l, 8:16, j], 1.0, n1, ALU.add, ALU.mult, ["modT", "pp"], ["A1"])
                P.stt(C.A2[:, l, :, j], C.modT[:, l, 32:40, j], 1.0, n2, ALU.add, ALU.mult, ["modT", "pp"], ["A2"])


def ph_xin(C):
    nc = C.nc
    with Phase(nc, "xin") as P:
        xin = [P.sb([128, D], F32, "xin") for _ in range(2)]
        pst = [P.ps([128, 512]) for _ in range(4)]
        n = 0
        P.dma("sp", xin[0][:], C.inp("x_tok")[0:128, :], key="xin0", writes=["xin0"])
        for tt in range(16):
            s = tt % 2
            if tt + 1 < 16:
                P.dma("sp", xin[1 - s][:], C.inp("x_tok")[(tt + 1) * 128:(tt + 2) * 128, :], key="xin%d" % (1 - s),
                      writes=["xin%d" % (1 - s)])
            for hh in range(2):
                pb = n % 4
                n += 1
                for kk in range(4):
                    kc = hh * 4 + kk
                    P.tr(pst[pb][:, kk * 128:(kk + 1) * 128], xin[s][:, kc * 128:(kc + 1) * 128], C.ident_f[:],
                         ["xin%d" % s], ["pst%d" % pb])
                out = C.xT[:, hh * 4:hh * 4 + 4, tt * 128:(tt + 1) * 128]
                src = pst[pb][:].rearrange("p (k t) -> p k t", k=4)
                if n % 2:
                    P.cp("act", out, src, ["pst%d" % pb], ["xw%d_%d" % (tt, hh)])
                else:
                    P.cp("dve", out, src, ["pst%d" % pb], ["xw%d_%d" % (tt, hh)])


def ph_norm(C, l, g, which):
    nc = C.nc
    j = g
    with Phase(nc, "nrm") as P:
        sq = [P.sb([128, 8, 512], BF16, "sq") for _ in range(2)]
        tn = [P.sb([128, 8, 512], F32, "tn") for _ in range(2)]
        rs = [P.sb([128, 512], F32, "rs") for _ in range(2)]
        pss = [P.ps([128, 512]) for _ in range(2)]
        eps = ppa(C, "eps")
        for tb in range(2):
            t0 = g * 1024 + tb * 512
            xk = "xT%d" % (t0 // 512)
            P.act(sq[tb][:], C.xT[:, :, t0:t0 + 512], AF.Square, [xk], ["sq%d" % tb])
            for kc in range(8):
                P.mm(pss[tb][:], C.onesm[:], sq[tb][:, kc, :], kc == 0, kc == 7, ["sq%d" % tb], ["pss%d" % tb])
            P.rsqrt(rs[tb][:], pss[tb][:], ["pss%d" % tb], ["rs%d" % tb], eps)
            if which == 1:
                A, B = C.A1, C.modT[:, l, 0:8, :]
            elif which == 2:
                A, B = C.A2, C.modT[:, l, 24:32, :]
            for kc in range(8):
                P.stt(tn[tb][:, kc, :], C.xT[:, kc, t0:t0 + 512], A[:, l, kc, j:j + 1], rs[tb][:], ALU.mult, ALU.mult,
                      [xk, "rs%d" % tb, "A"], ["tn%d_%d" % (tb, kc)])
                P.act(C.hT[:, kc, tb * 512:(tb + 1) * 512], tn[tb][:, kc, :], AF.Identity, ["tn%d_%d" % (tb, kc)],
                      ["hT%d" % tb], bias=B[:, kc, j:j + 1])


def ph_proj(C, l, g):
    nc = C.nc
    with Phase(nc, "prj") as P:
        if g == 0:
            for i_, (src_, dst_) in enumerate(((C.payA, C.gatA), (C.payB, C.gatB), (C.payC, C.gatC))):
                P.S.cc(lambda e, src_=src_, dst_=dst_: e.collective_compute(
                    "AllGather", ALU.bypass, replica_groups=GROUPS4, ins=[src_.ap().opt()], outs=[dst_.ap().opt()]), key="cc%d" % i_)
        wb = [P.sb([128, 8, 512], BF16, "wb") for _ in range(2)]
        ps = [P.ps([128, 512]) for _ in range(8)]
        sq = [P.sb([128, 512], BF16, "sq") for _ in range(2)]
        rs = [P.sb([128, 512], F32, "rs") for _ in range(2)]
        qn = [P.sb([128, 512], F32, "qn") for _ in range(2)]
        qb = [P.sb([128, 512], BF16, "qb") for _ in range(2)]
        o1 = [P.sb([128, 512], F32, "o1") for _ in range(2)]
        stg = [P.sb([128, 512], BF16, "stg") for _ in range(4)]
        sig = P.sb([128, 2, 1024], BF16, "sig")
        vf = [P.sb([128, 512], F32, "vf") for _ in range(2)]
        tokf = [P.sb([128, 4, 128], F32, "tokf") for _ in range(2)]
        tokb = [P.sb([128, 4, 128], BF16, "tokb") for _ in range(2)]
        eps = ppa(C, "eps")
        if g == 1:
            rc = P.sb([128, 1024], F32, "rc")
            rsn = P.sb([128, 1024], F32, "rsn")
            P.dma("sp", rc[:], C.inp("ropec"), key="rc", writes=["rc"])
            P.dma("sp", rsn[:], C.inp("ropes"), key="rsn", writes=["rsn"])
        W = C.inp("w_in")[l]
        cnt = dict(ps=0, stg=0, w=0, k=0)

        def load_block(cols, permq=False):
            s = cnt["w"] % 2
            cnt["w"] += 1
            key = "wb%d" % s
            if permq:
                for hh in range(2):
                    for c in range(4):
                        h0 = (hh * 4 + c) * 64
                        src = W[:, h0:h0 + 64].rearrange("(kc p) d -> p kc d", p=128)
                        dst = wb[s][:, :, c * 128 + hh * 64:c * 128 + hh * 64 + 64]
                        P.dma("pool", dst, src, key=key, writes=[key])
            else:
                n = cols[1] - cols[0]
                src = W[:, cols[0]:cols[1]].rearrange("(kc p) n -> p kc n", p=128)
                P.dma("pool", wb[s][:, :, 0:n], src, key=key, writes=[key])
            return s

        def matmuls(s, cl):
            res = []
            for tb in range(2):
                pi = cnt["ps"] % 4
                cnt["ps"] += 1
                for kc in range(8):
                    P.mm(ps[pi][:], wb[s][:, kc, cl * 128:(cl + 1) * 128], C.hT[:, kc, tb * 512:(tb + 1) * 512],
                         kc == 0, kc == 7, ["wb%d" % s, "hT%d" % tb], ["ps%d" % pi])
                res.append(pi)
            return res

        def stage_out(src_psum_or_none, dst_dram, producer):
            si = cnt["stg"] % 4
            cnt["stg"] += 1
            producer(stg[si][:], "stg%d" % si)
            P.dma("sp", dst_dram, stg[si][:], key="stg%d" % si, reads=["stg%d" % si])

        def headnorm(pi, tb, wname, rope, dst_dram, fp32_out=None):
            k = cnt["k"] % 2
            cnt["k"] += 1
            pk = "ps%d" % pi
            P.act(sq[k][:], ps[pi][:], AF.Square, [pk], ["sq%d" % k])
            p2 = 4 + (cnt["ps"] % 2)
            P.mm(ps[p2][:], C.blk64[:], sq[k][:], True, True, ["sq%d" % k], ["ps%d" % p2])
            P.rsqrt(rs[k][:], ps[p2][:], ["ps%d" % p2], ["rs%d" % k], eps)
            w_ap = ppa(C, wname)
            if not rope:
                if fp32_out is not None:
                    P.stt(fp32_out, ps[pi][:], w_ap, rs[k][:], ALU.mult, ALU.mult, [pk, "rs%d" % k], ["qn%d" % k])
                    stage_out(None, dst_dram, lambda ap, key: P.cp("act", ap, fp32_out, ["qn%d" % k], [key]))
                else:
                    stage_out(None, dst_dram, lambda ap, key: P.stt(ap, ps[pi][:], w_ap, rs[k][:], ALU.mult, ALU.mult,
                                                                      [pk, "rs%d" % k], [key]))
            else:
                P.stt(qn[k][:], ps[pi][:], w_ap, rs[k][:], ALU.mult, ALU.mult, [pk, "rs%d" % k], ["qn%d" % k])
                P.cp("act", qb[k][:], qn[k][:], ["qn%d" % k], ["qb%d" % k])
                p3 = 6 + (cnt["ps"] % 2)
                P.mm(ps[p3][:], C.ropeRT[:], qb[k][:], True, True, ["qb%d" % k], ["ps%d" % p3])
                P.tt("pool", o1[k][:], qn[k][:], rc[:, tb * 512:(tb + 1) * 512], ALU.mult, ["qn%d" % k, "rc"], ["o1%d" % k])
                P.tt("dve", qn[k][:], ps[p3][:], rsn[:, tb * 512:(tb + 1) * 512], ALU.mult, ["ps%d" % p3, "rsn"], ["qn%d" % k])
                stage_out(None, dst_dram, lambda ap, key: P.tt("dve", ap, qn[k][:], o1[k][:], ALU.add,
                                                                 ["qn%d" % k, "o1%d" % k], [key]))

        def to_tokmajor(src_f32, srckey, tb, out_dram_f32, out_dram_bf, tag="v"):
            k = cnt["k"] % 2
            cnt["k"] += 1
            p3 = 6 + k
            for i in range(4):
                P.tr(ps[p3][:, i * 128:(i + 1) * 128], src_f32[:, i * 128:(i + 1) * 128], C.ident_f[:], [srckey], ["ps%d" % p3])
            pv = ps[p3][:].rearrange("p (i f) -> p i f", i=4)
            if out_dram_f32 is not None and os.environ.get("SKIP_NK", "") != tag:
                P.cp("act", tokf[k][:], pv, ["ps%d" % p3], ["tokf%d" % k])
                for si, dd in enumerate(out_dram_f32):
                    P.dma("sp", dd, tokf[k][:, 2 * si:2 * si + 2, :], key="tokf%d" % k, reads=["tokf%d" % k])
            if out_dram_bf is not None:
                P.cp("dve", tokb[k][:], pv, ["ps%d" % p3], ["tokb%d" % k])
                P.dma("sp", out_dram_bf, tokb[k][:], key="tokb%d" % k, reads=["tokb%d" % k])

        s = load_block(None, permq=True)
        s_nxt = load_block((1024, 1536))
        for c in range(4):
            pis = matmuls(s, c)
            for tb, pi in enumerate(pis):
                headnorm(pi, tb, "qw_%d" % l, g == 1, C.qT_d[g][c * 128:(c + 1) * 128, tb * 512:(tb + 1) * 512])
        s = s_nxt
        s_nxt = load_block((512, 1024))
        for cl in range(4):
            pis = matmuls(s, cl)
            for tb, pi in enumerate(pis):
                pk = "ps%d" % pi
                if cl < 2:
                    P.act(sig[:, cl, tb * 512:(tb + 1) * 512], ps[pi][:], AF.Sigmoid, [pk], ["sig"])
                else:
                    zc = cl - 2
                    if g == 0:
                        dst = C.zhT_d[zc * 128:(zc + 1) * 128, tb * 512:(tb + 1) * 512]
                    else:
                        dst = C.pay_zh(zc)[:, tb * 512:(tb + 1) * 512]
                    stage_out(None, dst, lambda ap, key, pi=pi, pk=pk: P.cp("act", ap, ps[pi][:], [pk], [key]))
        s = s_nxt
        s_nxt = load_block((1536, 2048))
        for cl in range(4):
            pis = matmuls(s, cl)
            for tb, pi in enumerate(pis):
                pk = "ps%d" % pi
                if cl == 0:
                    if g == 0:
                        kk = cnt["k"] % 2
                        headnorm(pi, tb, "kw_%d" % l, False, C.kT_d[:, tb * 512:(tb + 1) * 512], fp32_out=qn[kk][:])
                        for sq_ in range(2):
                            pass
                        dst = [C.new_k[2 * tb + si, l].rearrange("(i p) f -> p i f", p=128) for si in range(2)]
                        to_tokmajor(qn[kk], "qn%d" % kk, tb, dst, None, tag="k")
                    else:
                        headnorm(pi, tb, "kw_%d" % l, True, C.pay_kT[:, tb * 512:(tb + 1) * 512])
                elif cl == 1:
                    kk = cnt["k"] % 2
                    P.cp("act", vf[kk][:], ps[pi][:], [pk], ["vf%d" % kk])
                    if g == 0:
                        dstf = [C.new_v[2 * tb + si, l].rearrange("(i p) f -> p i f", p=128) for si in range(2)]
                        dstb = C.vtok_d[tb * 512:(tb + 1) * 512, :].rearrange("(i p) f -> p i f", p=128)
                        to_tokmajor(vf[kk], "vf%d" % kk, tb, dstf, dstb)
                    else:
                        dstb = C.pay_V[tb * 512:(tb + 1) * 512, :].rearrange("(i p) f -> p i f", p=128)
                        to_tokmajor(vf[kk], "vf%d" % kk, tb, None, dstb)
                else:
                    cc = cl - 2
                    dst = C.uT_d[g][cc * 128:(cc + 1) * 128, tb * 512:(tb + 1) * 512]
                    si_ = cnt["stg"] % 4
                    stage_out(None, dst, lambda ap, key, pi=pi, pk=pk, cc=cc, tb=tb: P.tt(
                        "dve", ap, ps[pi][:], sig[:, cc, tb * 512:(tb + 1) * 512], ALU.mult, [pk, "sig"], [key]))
                    if g == 1:
                        ub = C.pay_ub
                        if tb == 0:
                            P.dma("sp", ub[cc * 128:(cc + 1) * 128, 0:16], stg[si_][:, 0:16], key="stg%d" % si_, reads=["stg%d" % si_])
                        else:
                            P.dma("sp", ub[cc * 128:(cc + 1) * 128, 16:32], stg[si_][:, 496:512], key="stg%d" % si_, reads=["stg%d" % si_])
        s = s_nxt
        for cl in range(4):
            pis = matmuls(s, cl)
            for tb, pi in enumerate(pis):
                pk = "ps%d" % pi
                zc = 2 + cl
                if g == 0:
                    dst = C.zhT_d[zc * 128:(zc + 1) * 128, tb * 512:(tb + 1) * 512]
                else:
                    dst = C.pay_zh(zc)[:, tb * 512:(tb + 1) * 512]
                stage_out(None, dst, lambda ap, key, pi=pi, pk=pk: P.cp("act", ap, ps[pi][:], [pk], [key]))


GROUPS4 = [[0, 1, 2, 3], [4, 5, 6, 7]]


def ph_ag(C, src, dst, name):
    with Phase(C.nc, name) as P:
        P.S.cc(lambda e: e.collective_compute("AllGather", ALU.bypass, replica_groups=GROUPS4,
                                              ins=[src.ap().opt()], outs=[dst.ap().opt()]), key="cc")


def ph_attn(C, l, g, heads=range(8), with_ag=False):
    nc = C.nc
    with Phase(nc, "att") as P:
        nkt = 8 if g == 0 else 34
        psS = [P.ps([128, 512]) for _ in range(6)]
        psO = [P.ps([128, 512]) for _ in range(2)]
        if with_ag:
            for i_, (src_, dst_) in enumerate(((C.payA, C.gatA), (C.payB, C.gatB), (C.payC, C.gatC))):
                P.S.cc(lambda e, src_=src_, dst_=dst_: e.collective_compute(
                    "AllGather", ALU.bypass, replica_groups=GROUPS4, ins=[src_.ap().opt()], outs=[dst_.ap().opt()]), key="cc%d" % i_)
        qT = P.sb([128, 4, 1024], BF16, "qT")
        qz = [P.sb([128, 4, 1024], BF16, "qz") for _ in range(2)]
        kT = P.sb([128, nkt * 128], BF16, "kT")
        va = P.sb([128, nkt, 2, 128], BF16, "va")
        P.dma("sp", qT[:], C.qT_d[g].rearrange("(c p) t -> p c t", p=128), key="qT", writes=["qT"])
        P.memset("dve", qz[0][64:128].rearrange("p a b -> p (a b)"), 0.0, ["qz0"])
        P.memset("dve", qz[1][0:64].rearrange("p a b -> p (a b)"), 0.0, ["qz1"])
        P.cp("dve", qz[0][0:64], qT[0:64], ["qT"], ["qz0b"])
        P.cp("act", qz[1][64:128], qT[64:128], ["qT"], ["qz1b"])
        for t0_ in range(0, nkt, 8):
            t1_ = min(nkt, t0_ + 8)
            P.memset("dve", va[:, t0_:t1_].rearrange("p a b c -> p (a b c)"), 1.0, ["va"])
        if g == 0:
            P.dma("sp", kT[:], C.kT_d, key="kT", writes=["kT"])
            for kv in range(2):
                P.dma("sp", va[:, :, kv, 0:64], C.vtok_d[:, kv * 64:(kv + 1) * 64].rearrange("(i p) d -> p i d", p=128),
                      key="va", writes=["va"])
        else:
            for r in range(4):
                P.dma("sp", kT[:, r * 1024:(r + 1) * 1024], C.gat_kT(r), key="kT", writes=["kT"])
            for r in range(4):
                vreg = C.gat_V(r)
                for kv in range(2):
                    P.dma("sp", va[:, r * 8:(r + 1) * 8, kv, 0:64],
                          vreg[:, kv * 64:(kv + 1) * 64].rearrange("(i p) d -> p i d", p=128), key="va", writes=["va"])
            ck = P.sb([128, 2, 128], F32, "ck")
            P.dma("sp", ck[:], C.inp("cache_k")[l].rearrange("(i p) f -> p i f", p=128), key="ck", writes=["ck"])
            pck = psO[0]
            for i in range(2):
                P.tr(pck[:, i * 128:(i + 1) * 128], ck[:, i, :], C.ident_f[:], ["ck"], ["psO0"])
            P.cp("dve", kT[:, 4096:4352], pck[:, 0:256], ["psO0"], ["kT"])
            for kv in range(2):
                P.dma("pool", va[:, 32:34, kv, 0:64],
                      C.inp("cache_v")[l][:, kv * 64:(kv + 1) * 64].rearrange("(i p) d -> p i d", p=128), key="va2", writes=["va"])
        pT = [P.sb([128, 512], BF16, "pT") for _ in range(6)]
        dn = [P.sb([64, 512], F32, "dn") for _ in range(2)]
        it = dict(s=0, o=0, p=0, b=0)

        NB = 3

        def one(h, qsl, n, ktiles, dst):
            c, kvh = h % 4, h // 4
            r0 = kvh * 64
            oi = it["o"] % 2
            it["o"] += 1
            batches = [ktiles[i:i + NB] for i in range(0, len(ktiles), NB)]

            def emit_s_batch(bi):
                res = []
                par = it["b"] % 2
                it["b"] += 1
                for kt in batches[bi]:
                    si = it["s"] % 6
                    it["s"] += 1
                    pi = it["p"] % 6
                    it["p"] += 1
                    P.mm(psS[si][:, 0:n], kT[:, kt * 128:(kt + 1) * 128], qz[kvh][:, c, qsl], True, True,
                         ["kT", "qz0", "qz1", "qz0b", "qz1b"], ["psS%d" % si])
                    res.append((si, pi, kt))
                for (si, pi, kt) in res:
                    P.act(pT[pi][:, 0:n], psS[si][:, 0:n], AF.Exp, ["psS%d" % si], ["pT%d" % pi, "pTb%d" % par], scale=0.125)
                return res, par
            pend = emit_s_batch(0)
            for bi in range(len(batches)):
                nxt = emit_s_batch(bi + 1) if bi + 1 < len(batches) else None
                res, par = pend
                for j, (si, pi, kt) in enumerate(res):
                    first = (bi == 0 and j == 0)
                    last = (bi == len(batches) - 1 and j == len(res) - 1)
                    P.mm(psO[oi][:, 0:n], va[:, kt, kvh, :], pT[pi][:, 0:n], first, last,
                         ["va", "pT%d" % pi, "pTb%d" % par], ["psO%d" % oi])
                pend = nxt
            P.cp("dve", dn[oi][:, 0:n], psO[oi][64:128, 0:n], ["psO%d" % oi], ["dn%d" % oi])
            P.act(dn[oi][:, 0:n], dn[oi][:, 0:n], AF.Ln, ["dn%d" % oi], ["dn%d" % oi])
            P.act(dn[oi][:, 0:n], dn[oi][:, 0:n], AF.Exp, ["dn%d" % oi], ["dn%d" % oi], scale=-1.0)
            P.tt("dve", dst, psO[oi][0:64, 0:n], dn[oi][:, 0:n], ALU.mult, ["psO%d" % oi, "dn%d" % oi], ["mixA"])

        for h in heads:
            c, kvh = h % 4, h // 4
            r0 = kvh * 64
            if g == 0:
                for sq_ in range(4):
                    one(h, slice(sq_ * 256, (sq_ + 1) * 256), 256, [sq_ * 2, sq_ * 2 + 1],
                        C.mixT[r0:r0 + 64, c, sq_ * 256:(sq_ + 1) * 256])
            else:
                for qb in range(2):
                    one(h, slice(qb * 512, (qb + 1) * 512), 512, list(range(34)),
                        C.mixT[r0:r0 + 64, c, qb * 512:(qb + 1) * 512])


def ph_conv(C, l, g):
    nc = C.nc
    with Phase(nc, "cnv") as P:
        o, n = C.ppoff["cdw_%d" % l]
        dg = P.sb([128, 2, 31, 128], BF16, "dg")
        for cc in range(2):
            for k in range(31):
                P.ts("dve", dg[:, cc, k, :], C.ident_b[:], C.pp[:, o + cc * 31 + k:o + cc * 31 + k + 1], ALU.mult, [], ["dg"])
        cdb = ppa(C, "cdb_%d" % l); gnw = ppa(C, "gnw_%d" % l); gnb = ppa(C, "gnb_%d" % l); pwb = ppa(C, "pwb_%d" % l)
        eps = ppa(C, "eps")
        pww = P.sb([128, 2, 256], BF16, "pww")
        P.dma("pool", pww[:], C.inp("conv_pw_w")[l].rearrange("(kc p) n -> p kc n", p=128), key="pww", writes=["pww"])
        if g == 0:
            up = P.sb([128, 2, 4, 286], BF16, "up")
            P.memset("pool", up[:], 0.0, ["up"])
            for cc in range(2):
                P.dma("sp", up[:, cc, :, 15:271], C.uT_d[0][cc * 128:(cc + 1) * 128, :].rearrange("p (s t) -> p s t", s=4),
                      key="up", writes=["up"])
        else:
            up = P.sb([128, 2, 1054], BF16, "up")
            P.memset("pool", up[:], 0.0, ["up"])
            for cc in range(2):
                P.dma("sp", up[:, cc, 15:1039], C.uT_d[1][cc * 128:(cc + 1) * 128, :], key="up", writes=["up"])
            ubg = P.sb([128, 2, 4, 32], BF16, "ubg")
            for r in range(4):
                ub = C.gat_ub(r)
                for cc in range(2):
                    P.dma("sp", ubg[:, cc, r, :], ub[cc * 128:(cc + 1) * 128, :], key="ubg", writes=["ubg"])
            el = ppa(C, "el"); er = ppa(C, "er")
            for r in range(4):
                for (dst, src, ee) in ((up[:, :, 0:15], ubg[:, :, r, 17:32], el), (up[:, :, 1039:1054], ubg[:, :, r, 0:15], er)):
                    P.stt(dst, src, ee[:, r:r + 1], dst, ALU.mult, ALU.add, ["ubg", "up"], ["up"])
        sbf = P.sb([128, 2, 1024], BF16, "sbf")
        pc = [P.ps([128, 512]) for _ in range(2)]
        pm = [P.ps([128, 512]) for _ in range(4)]
        yf = [P.sb([128, 512], F32, "yf") for _ in range(2)]
        yb = [P.sb([128, 512], BF16, "yb") for _ in range(2)]
        ysq = [P.sb([128, 512], BF16, "ysq") for _ in range(2)]
        msq = [P.sb([128, 512], F32, "msq") for _ in range(2)]
        var = [P.sb([128, 512], F32, "var") for _ in range(2)]
        n_ = 0
        for cc in range(2):
            for blk in range(2):
                b = n_ % 2
                n_ += 1
                if g == 0:
                    for half in range(2):
                        sq_ = blk * 2 + half
                        for k in range(31):
                            P.mm(pc[b][:, half * 256:(half + 1) * 256], dg[:, cc, k, :], up[:, cc, sq_, k:k + 256], k == 0, k == 30,
                                 ["dg", "up"], ["psc%d" % b])
                else:
                    for k in range(31):
                        P.mm(pc[b][:], dg[:, cc, k, :], up[:, cc, blk * 512 + k:blk * 512 + k + 512], k == 0, k == 30,
                             ["dg", "up"], ["psc%d" % b])
                P.act(yf[b][:], pc[b][:], AF.Identity, ["psc%d" % b], ["yf%d" % b], bias=cdb[:, cc:cc + 1])
                P.cp("dve", yb[b][:], yf[b][:], ["yf%d" % b], ["yb%d" % b])
                P.act(ysq[b][:], yf[b][:], AF.Square, ["yf%d" % b], ["ysq%d" % b])
                P.mm(pm[2 * b][:], C.blk64[:], yb[b][:], True, True, ["yb%d" % b], ["psm%d" % (2 * b)])
                P.mm(pm[2 * b + 1][:], C.blk64[:], ysq[b][:], True, True, ["ysq%d" % b], ["psm%d" % (2 * b + 1)])
                P.act(msq[b][:], pm[2 * b][:], AF.Square, ["psm%d" % (2 * b)], ["msq%d" % b])
                P.tt("dve", var[b][:], pm[2 * b + 1][:], msq[b][:], ALU.subtract, ["psm%d" % (2 * b + 1), "msq%d" % b], ["var%d" % b])
                P.rsqrt(var[b][:], var[b][:], ["var%d" % b], ["var%d" % b], eps)
                P.tt("dve", yf[b][:], yf[b][:], pm[2 * b][:], ALU.subtract, ["yf%d" % b, "psm%d" % (2 * b)], ["yf%d" % b])
                P.tt("pool", yf[b][:], yf[b][:], var[b][:], ALU.mult, ["yf%d" % b, "var%d" % b], ["yf%d" % b])
                P.act(sbf[:, cc, blk * 512:(blk + 1) * 512], yf[b][:], AF.Silu, ["yf%d" % b], ["sbf"],
                      scale=gnw[:, cc:cc + 1], bias=gnb[:, cc:cc + 1])
        for m in range(2):
            for blk in range(2):
                b = n_ % 2
                n_ += 1
                for cc in range(2):
                    P.mm(pc[b][:], pww[:, cc, m * 128:(m + 1) * 128], sbf[:, cc, blk * 512:(blk + 1) * 512], cc == 0, cc == 1,
                         ["pww", "sbf"], ["psc%d" % b])
                P.act(C.mixT[:, 4 + m, blk * 512:(blk + 1) * 512], pc[b][:], AF.Identity, ["psc%d" % b], ["mixC"],
                      bias=pwb[:, m:m + 1])


def ph_wo(C, l, g, zero_hy=False, with_norm2=True):
    nc = C.nc
    j = g
    with Phase(nc, "wo") as P:
        wo = P.sb([128, 8, 1024], BF16, "wo")
        W = C.inp("w_o")[l]
        for cb in range(2):
            cs_ = slice(cb * 512, (cb + 1) * 512)
            for c in range(4):
                for hh in range(2):
                    P.dma("pool", wo[hh * 64:(hh + 1) * 64, c, cs_], W[(hh * 4 + c) * 64:(hh * 4 + c) * 64 + 64, cs_],
                          key="wo%d" % cb, writes=["wo%d" % cb])
            P.dma("pool", wo[:, 4:8, cs_], W[512:1024, cs_].rearrange("(kc p) n -> p kc n", p=128), key="wo%d" % cb, writes=["wo%d" % cb])
        if zero_hy:
            P.memset("dve", C.mixT[:, 6:8, :], 0.0, ["mix"])
        ps = [P.ps([128, 512]) for _ in range(4)]
        sq = [P.sb([128, 8, 512], BF16, "sq") for _ in range(2)]
        tn = [P.sb([128, 8, 512], F32, "tn") for _ in range(2)]
        rs = [P.sb([128, 512], F32, "rs") for _ in range(2)]
        pss = [P.ps([128, 512]) for _ in range(2)]
        eps = ppa(C, "eps")
        n_ = 0
        for tb in range(2):
            t0 = g * 1024 + tb * 512
            xk = "xT%d" % (t0 // 512)
            for m in range(8):
                b = n_ % 4
                n_ += 1
                for kc in range(8):
                    P.mm(ps[b][:], wo[:, kc, m * 128:(m + 1) * 128], C.mixT[:, kc, tb * 512:(tb + 1) * 512], kc == 0, kc == 7,
                         ["wo%d" % (m // 4), "mix"], ["ps%d" % b])
                xs = C.xT[:, m, t0:t0 + 512]
                P.stt(xs, ps[b][:], C.modT[:, l, 16 + m, j:j + 1], xs, ALU.mult, ALU.add, ["ps%d" % b], [xk])
            if with_norm2:
                P.act(sq[tb][:], C.xT[:, :, t0:t0 + 512], AF.Square, [xk], ["sq%d" % tb])
                for kc in range(8):
                    P.mm(pss[tb][:], C.onesm[:], sq[tb][:, kc, :], kc == 0, kc == 7, ["sq%d" % tb], ["pss%d" % tb])
                P.rsqrt(rs[tb][:], pss[tb][:], ["pss%d" % tb], ["rs%d" % tb], eps)
                for kc in range(8):
                    P.stt(tn[tb][:, kc, :], C.xT[:, kc, t0:t0 + 512], C.A2[:, l, kc, j:j + 1], rs[tb][:], ALU.mult, ALU.mult,
                          [xk, "rs%d" % tb], ["tn%d_%d" % (tb, kc)])
                    P.act(C.hT[:, kc, tb * 512:(tb + 1) * 512], tn[tb][:, kc, :], AF.Identity, ["tn%d_%d" % (tb, kc)],
                          ["hT%d" % tb], bias=C.modT[:, l, 24 + kc, j:j + 1])


def ph_halo3(C, l):
    nc = C.nc
    with Phase(nc, "h3a") as P:
        hb = P.sb([128, 1024], BF16, "hb")
        P.memset("dve", hb[:], 0.0, ["hb"])
        P.cp("dve", hb[:, 0:8], C.hT[:, :, 0:1].rearrange("p k e -> p (k e)"), ["hb"], ["hb"])
        P.cp("dve", hb[:, 8:16], C.hT[:, :, 1023:1024].rearrange("p k e -> p (k e)"), ["hb"], ["hb"])
        P.dma("sp", C.pay3.ap(), hb[:], key="hb", reads=["hb"])
    ph_ag(C, C.pay3, C.gat3, "ag3")
    with Phase(nc, "h3b") as P:
        hbg = P.sb([128, 4, 16], F32, "hbg")
        for r in range(4):
            P.dma("pool", hbg[:, r, :], C.gat3.ap()[r * 128:(r + 1) * 128, 0:16], key="hbg%d" % r, writes=["hbg"])
        el = ppa(C, "el"); er = ppa(C, "er")
        acc = P.sb([128, 2, 8], F32, "acc")
        P.memset("dve", acc[:], 0.0, ["acc"])
        for r in range(4):
            P.stt(acc[:, 0, :], hbg[:, r, 8:16], el[:, r:r + 1], acc[:, 0, :], ALU.mult, ALU.add, ["hbg", "acc"], ["acc"])
            P.stt(acc[:, 1, :], hbg[:, r, 0:8], er[:, r:r + 1], acc[:, 1, :], ALU.mult, ALU.add, ["hbg", "acc"], ["acc"])
        P.cp("dve", C.hT[:, :, 1024:1025].rearrange("p k e -> p (k e)"), acc[:, 0, :], ["acc"], ["hTh"])
        P.cp("dve", C.hT[:, :, 1025:1026].rearrange("p k e -> p (k e)"), acc[:, 1, :], ["acc"], ["hTh"])


def ph_ffn(C, l, g):
    nc = C.nc
    j = g
    with Phase(nc, "ffn") as P:
        o_w, _ = C.ppoff["fdw_%d" % l]
        o_b, _ = C.ppoff["fdb_%d" % l]
        wsc = lambda ch, k: C.pp[:, o_w + ch * 3 + k:o_w + ch * 3 + k + 1]
        bsc = lambda ch: C.pp[:, o_b + ch:o_b + ch + 1]
        Wu = C.inp("ffn_w_up")[l]
        Wd = C.inp("ffn_w_down")[l]
        gT = P.sb([128, 11, 1024], BF16, "gT")
        wdn = P.sb([128, 11, 1024], BF16, "wdn")
        wupV = [P.sb([128, 8, 512], BF16, "wupV") for _ in range(2)]
        wupG = [P.sb([128, 8, 512], BF16, "wupG") for _ in range(2)]
        cv = [P.sb([128, 1024], F32, "cv") for _ in range(2)]
        cg = [P.sb([128, 1024], F32, "cg") for _ in range(2)]
        sg = [P.sb([128, 1024], F32, "sg") for _ in range(2)]
        pt = [[P.ps([128, 512]) for _ in range(2)] for _ in range(3)]
        ph = [P.ps([128, 512]) for _ in range(2)]
        rot = 0
        for half in range(2):
            for jj in range(11):
                jf = half * 11 + jj
                s = jf % 2
                if jj == 1:
                    P.dma("pool", wdn[:], Wd[half * 1408:(half + 1) * 1408, :].rearrange("(j p) n -> p j n", p=128), key="wdn",
                          writes=["wdn"])
                if jj in (0, 4, 8):
                    bi = half * 3 + jj // 4
                    wb_ = bi % 2

                    def issue_block(bk):
                        h_, q_ = bk // 3, bk % 3
                        j0 = h_ * 11 + q_ * 4
                        nj = 4 if q_ < 2 else 3
                        sl_ = bk % 2
                        P.dma("pool", wupV[sl_][:, :, 0:nj * 128], Wu[:, j0 * 128:(j0 + nj) * 128].rearrange("(kc p) n -> p kc n", p=128),
                              key="wupV%d" % sl_, writes=["wup%d" % sl_])
                        P.dma("pool", wupG[sl_][:, :, 0:nj * 128],
                              Wu[:, FFN + j0 * 128:FFN + (j0 + nj) * 128].rearrange("(kc p) n -> p kc n", p=128),
                              key="wupG%d" % sl_, writes=["wup%d" % sl_])
                    if bi == 0:
                        issue_block(0)
                    if bi + 1 < 6:
                        issue_block(bi + 1)
                wcol = (jj % 4) * 128 if jj < 8 else (jj - 8) * 128
                wkey = "wup%d" % wb_
                tv = rot % 3
                tg = (rot + 1) % 3
                rot += 2
                for (tt_, wsrc) in ((tv, wupV[wb_]), (tg, wupG[wb_])):
                    for tb in range(2):
                        for kc in range(8):
                            P.mm(pt[tt_][tb][:], wsrc[:, kc, wcol:wcol + 128], C.hT[:, kc, tb * 512:(tb + 1) * 512],
                                 kc == 0, kc == 7, [wkey, "hT0", "hT1"], ["pst%d" % tt_])
                if g == 1:
                    for (wsrc, col) in ((wupV[wb_], 0), (wupG[wb_], 2)):
                        for kc in range(8):
                            P.mm(ph[s][:, col:col + 2], wsrc[:, kc, wcol:wcol + 128], C.hT[:, kc, 1024:1026], kc == 0, kc == 7,
                                 [wkey, "hTh"], ["psh%d" % s])
                for (tt_, cbuf, ch, hc) in ((tv, cv[s], jf, 0), (tg, cg[s], NJ + jf, 2)):
                    ck = "c%d_%d" % (hc, s)
                    pk = "pst%d" % tt_
                    for tb in range(2):
                        cb_ = cbuf[:, tb * 512:(tb + 1) * 512]
                        pb_ = pt[tt_][tb][:]
                        P.act(cb_, pb_, AF.Identity, [pk], [ck], scale=wsc(ch, 1), bias=bsc(ch))
                        if g == 0:
                            c3 = cb_.rearrange("p (s t) -> p s t", s=2)
                            p3 = pb_.rearrange("p (s t) -> p s t", s=2)
                            P.stt(c3[:, :, 1:256], p3[:, :, 0:255], wsc(ch, 0), c3[:, :, 1:256], ALU.mult, ALU.add, [pk, ck], [ck])
                            P.stt(c3[:, :, 0:255], p3[:, :, 1:256], wsc(ch, 2), c3[:, :, 0:255], ALU.mult, ALU.add, [pk, ck], [ck])
                        else:
                            P.stt(cb_[:, 1:512], pb_[:, 0:511], wsc(ch, 0), cb_[:, 1:512], ALU.mult, ALU.add, [pk, ck], [ck])
                            P.stt(cb_[:, 0:511], pb_[:, 1:512], wsc(ch, 2), cb_[:, 0:511], ALU.mult, ALU.add, [pk, ck], [ck])
                    if g == 1:
                        P.stt(cbuf[:, 512:513], pt[tt_][0][:, 511:512], wsc(ch, 0), cbuf[:, 512:513], ALU.mult, ALU.add, [pk, ck], [ck])
                        P.stt(cbuf[:, 511:512], pt[tt_][1][:, 0:1], wsc(ch, 2), cbuf[:, 511:512], ALU.mult, ALU.add, [pk, ck], [ck])
                        P.stt(cbuf[:, 0:1], ph[s][:, hc:hc + 1], wsc(ch, 0), cbuf[:, 0:1], ALU.mult, ALU.add, ["psh%d" % s, ck], [ck])
                        P.stt(cbuf[:, 1023:1024], ph[s][:, hc + 1:hc + 2], wsc(ch, 2), cbuf[:, 1023:1024], ALU.mult, ALU.add,
                              ["psh%d" % s, ck], [ck])
                P.act(sg[s][:], cg[s][:], AF.Silu, ["c2_%d" % s], ["sg%d" % s])
                P.tt("dve", gT[:, jj, :], cv[s][:], sg[s][:], ALU.mult, ["c0_%d" % s, "sg%d" % s], ["gT"])
            n_ = 0
            for m in range(8):
                for tb in range(2):
                    tt_ = n_ % 3
                    n_ += 1
                    for jj in range(11):
                        P.mm(pt[tt_][0][:], wdn[:, jj, m * 128:(m + 1) * 128], gT[:, jj, tb * 512:(tb + 1) * 512], jj == 0, jj == 10,
                             ["wdn", "gT"], ["pst%d" % tt_])
                    t0 = g * 1024 + tb * 512
                    xs = C.xT[:, m, t0:t0 + 512]
                    P.stt(xs, pt[tt_][0][:], C.modT[:, l, 40 + m, j:j + 1], xs, ALU.mult, ALU.add, ["pst%d" % tt_],
                          ["xT%d" % (t0 // 512)])


def ph_final(C):
    nc = C.nc
    with Phase(nc, "fin") as P:
        sq = [P.sb([128, 8, 512], BF16, "sq") for _ in range(2)]
        yT = [P.sb([128, 8, 512], F32, "yT") for _ in range(2)]
        rs = [P.sb([128, 512], F32, "rs") for _ in range(2)]
        yo = [P.sb([128, 1024], F32, "yo") for _ in range(2)]
        pss = [P.ps([128, 512]) for _ in range(2)]
        ptr = [P.ps([128, 512]) for _ in range(4)]
        eps = ppa(C, "eps")
        fn = ppa(C, "fnorm")
        n_ = 0
        for tb4 in range(4):
            b = tb4 % 2
            t0 = tb4 * 512
            xk = "xT%d" % tb4
            P.act(sq[b][:], C.xT[:, :, t0:t0 + 512], AF.Square, [xk], ["sq%d" % b])
            for kc in range(8):
                P.mm(pss[b][:], C.onesm[:], sq[b][:, kc, :], kc == 0, kc == 7, ["sq%d" % b], ["pss%d" % b])
            P.rsqrt(rs[b][:], pss[b][:], ["pss%d" % b], ["rs%d" % b], eps)
            for kc in range(8):
                P.stt(yT[b][:, kc, :], C.xT[:, kc, t0:t0 + 512], fn[:, kc:kc + 1], rs[b][:], ALU.mult, ALU.mult,
                      [xk, "rs%d" % b], ["yT%d" % b])
            for ti in range(4):
                yb_ = n_ % 2
                for hh in range(2):
                    pb = n_ % 4 if False else (2 * yb_ + hh)
                    for kk in range(4):
                        kc = hh * 4 + kk
                        P.tr(ptr[pb][:, kk * 128:(kk + 1) * 128], yT[b][:, kc, ti * 128:(ti + 1) * 128], C.ident_f[:],
                             ["yT%d" % b], ["pstr%d" % pb])
                    if hh == 0:
                        P.cp("act", yo[yb_][:, 0:512], ptr[pb][:], ["pstr%d" % pb], ["yo%d" % yb_])
                    else:
                        P.cp("dve", yo[yb_][:, 512:1024], ptr[pb][:], ["pstr%d" % pb], ["yo%d" % yb_])
                r0 = t0 + ti * 128
                P.dma("sp", C.y_tok[r0:r0 + 128, :], yo[yb_][:], key="yo%d" % yb_, reads=["yo%d" % yb_])
                n_ += 1


TWO_PI = 2.0 * math.pi
MAGIC = 12582912.0


def filt_mlp(P, C, l, zT, L, hid2):
    w1 = P.sb([33, 64], F32, "w1")
    w2 = P.sb([64, 64], F32, "w2")
    P.dma("sp", w1[:], C.inp("hy_f_w1")[l], key="w1", writes=["w1"])
    P.dma("sp", w2[:], C.inp("hy_f_w2")[l], key="w2", writes=["w2"])
    fr = ppa(C, "hfr_%d" % l)[0:64, :]
    fb = P.sb([64, 2], F32, "fb")
    P.tt("dve", fb[:, 0:1], ppa(C, "hfb1_%d" % l)[0:64, :], fr, ALU.mult, [], ["fb"])
    P.tt("dve", fb[:, 1:2], ppa(C, "hfb2_%d" % l)[0:64, :], fr, ALU.mult, [], ["fb"])
    hid1 = P.sb([64, L], F32, "hid1")
    a = P.sb([64, 512], F32, "fa")
    t1 = P.sb([64, 512], F32, "ft1")
    pf = P.ps([128, 512])
    for (wt, src, dst, col) in ((w1, zT, hid1, 0), (w2, hid1, hid2, 1)):
        kk = 33 if col == 0 else 64
        for b0 in range(0, L, 512):
            n = min(512, L - b0)
            P.mm(pf[0:64, 0:n], wt[0:kk, :], src[0:kk, b0:b0 + n], True, True, ["w1", "w2", "zT", "hid1"], ["psf"])
            P.ts("dve", a[:, 0:n], pf[0:64, 0:n], fr, ALU.mult, ["psf", "fb"], ["fa"], s2=fb[:, col:col + 1], op1=ALU.add)
            P.ts("dve", t1[:, 0:n], a[:, 0:n], 1.0 / TWO_PI, ALU.mult, ["fa"], ["ft1"], s2=MAGIC, op1=ALU.add)
            P.ts("dve", t1[:, 0:n], t1[:, 0:n], MAGIC, ALU.subtract, ["ft1"], ["ft1"])
            P.stt(a[:, 0:n], t1[:, 0:n], -TWO_PI, a[:, 0:n], ALU.mult, ALU.add, ["ft1", "fa"], ["fa"])
            P.ts("dve", a[:, 0:n], a[:, 0:n], -3.14159, ALU.max, ["fa"], ["fa"], s2=3.14159, op1=ALU.min)
            P.act(dst[:, b0:b0 + n], a[:, 0:n], AF.Sin, ["fa"], ["hid1" if col == 0 else "hid2"])


def ph_hyena(C, l, g):
    if g == 0:
        ph_hyena_p(C, l)
    else:
        ph_hyena_s(C, l)


def ph_hyena_p(C, l):
    nc = C.nc
    with Phase(nc, "hyp") as P:
        dC = P.sb([128, 2, 512], BF16, "dC"); dS = P.sb([128, 2, 512], BF16, "dS"); dSn = P.sb([128, 2, 512], BF16, "dSn")
        iC = P.sb([128, 4, 256], BF16, "iC"); iSn = P.sb([128, 4, 256], BF16, "iSn")
        for t_, nm in ((dC, "dftC"), (dS, "dftS"), (dSn, "dftSn"), (iC, "idftC"), (iSn, "idftSn")):
            P.dma("sp", t_[:], C.inp(nm), key="cst", writes=["cst"])
        zT = P.sb([33, 256], F32, "zT")
        P.dma("sp", zT[:], C.inp("zT256"), key="zT", writes=["zT"])
        hid2 = P.sb([64, 256], F32, "hid2")
        filt_mlp(P, C, l, zT, 256, hid2)
        w3 = P.sb([64, 1024], F32, "w3")
        b3 = P.sb([1, 1024], F32, "b3")
        onesr = P.sb([1, 128], F32, "onesr")
        brow = P.sb([1, 512], F32, "brow")
        dec = P.sb([128, 2, 256], F32, "dec")
        P.dma("sp", w3[:], C.inp("hy_f_w3")[l], key="w3", writes=["w3"])
        P.dma("sp", b3[:], C.inp("hy_f_b3r")[l], key="b3", writes=["b3"])
        P.dma("sp", brow[:], C.inp("hy_biasr")[l], key="brow", writes=["brow"])
        P.dma("sp", dec[:], C.inp("decayP"), key="dec", writes=["dec"])
        P.memset("dve", onesr[:], 1.0, ["onesr"])
        hP = P.sb([128, 2, 1024], F32, "hP")
        pA = [P.ps([128, 512]) for _ in range(2)]
        for ti in range(2):
            for cb in range(2):
                pb = cb
                P.mm(pA[pb][:], hid2[:, ti * 128:(ti + 1) * 128], w3[:, cb * 512:(cb + 1) * 512], True, False, ["hid2", "w3"], ["psA%d" % pb])
                P.mm(pA[pb][:], onesr[:], b3[:, cb * 512:(cb + 1) * 512], False, True, ["onesr", "b3"], ["psA%d" % pb])
                for od in range(2):
                    P.tt("dve", hP[:, ti, cb * 512 + od * 256:cb * 512 + (od + 1) * 256], pA[pb][:, od * 256:(od + 1) * 256], dec[:, ti, :],
                         ALU.mult, ["psA%d" % pb, "dec"], ["hP"])
        hs = P.sb([128, 2, 512], BF16, "hs")
        hd = P.sb([128, 2, 512], BF16, "hd")
        for o in range(2):
            P.memset("dve", hP[0:1, 0, o * 512 + 256:o * 512 + 512], 0.0, ["hP"])
            P.tt("dve", hP[0:1, 0, o * 512:o * 512 + 256], hP[0:1, 0, o * 512:o * 512 + 256], brow[:, o * 256:(o + 1) * 256], ALU.add,
                 ["hP", "brow"], ["hP"])
        for o in range(2):
            fw = hP[:, :, o * 512:o * 512 + 256]
            bw = hP[:, :, o * 512 + 256:o * 512 + 512]
            P.tt("dve", hs[:, :, o * 256:(o + 1) * 256], fw, bw, ALU.add, ["hP"], ["hs"])
            P.tt("dve", hd[:, :, o * 256:(o + 1) * 256], bw, fw, ALU.subtract, ["hP"], ["hd"])
        Kr = P.sb([128, 4, 512], F32, "Kr")
        Ki = P.sb([128, 4, 512], F32, "Ki")
        for fch in range(4):
            for (dst, mat, src, nm) in ((Kr, dC, hs, "hs"), (Ki, dS, hd, "hd")):
                pb = fch % 2
                for ti in range(2):
                    P.mm(pA[pb][:], mat[:, ti, fch * 128:(fch + 1) * 128], src[:, ti, :], ti == 0, ti == 1, ["cst", nm], ["psA%d" % pb])
                P.cp("act", dst[:, fch, :], pA[pb][:], ["psA%d" % pb], ["K"])
        zh = P.sb([128, 6, 1024], BF16, "zh")
        P.dma("sp", zh[:], C.zhT_d.rearrange("(c p) t -> p c t", p=128), key="zh", writes=["zh"])
        o_w, _ = C.ppoff["hsw_%d" % l]
        o_b, _ = C.ppoff["hsb_%d" % l]
        wsc = lambda ch, k: C.pp[:, o_w + ch * 3 + k:o_w + ch * 3 + k + 1]
        zc = P.sb([128, 6, 1024], BF16, "zc")
        ztmp = [P.sb([128, 1024], F32, "ztmp") for _ in range(2)]
        for ch in range(6):
            b = ch % 2
            P.act(ztmp[b][:], zh[:, ch, :], AF.Identity, ["zh"], ["ztmp%d" % b], scale=wsc(ch, 1), bias=C.pp[:, o_b + ch:o_b + ch + 1])
            c3 = ztmp[b][:].rearrange("p (s t) -> p s t", s=4)
            z3 = zh[:, ch, :].rearrange("p (s t) -> p s t", s=4)
            P.stt(c3[:, :, 1:256], z3[:, :, 0:255], wsc(ch, 0), c3[:, :, 1:256], ALU.mult, ALU.add, ["zh", "ztmp%d" % b], ["ztmp%d" % b])
            P.stt(c3[:, :, 0:255], z3[:, :, 1:256], wsc(ch, 2), c3[:, :, 0:255], ALU.mult, ALU.add, ["zh", "ztmp%d" % b], ["ztmp%d" % b])
            P.cp("pool", zc[:, ch, :], ztmp[b][:], ["ztmp%d" % b], ["zc"])
        vtok = P.sb([128, 2, 4, 256], BF16, "vtok")
        Yr = P.sb([128, 4, 1024], BF16, "Yr")
        Yi = P.sb([128, 4, 1024], BF16, "Yi")
        u1 = P.sb([128, 2, 1024], BF16, "u1")
        ta = [P.sb([128, 512], F32, "ta") for _ in range(2)]
        tb_ = [P.sb([128, 512], F32, "tb") for _ in range(2)]
        pT = P.ps([128, 1024], BF16)
        pU = [P.ps([128, 512]) for _ in range(4)]
        for o in range(2):
            src = zc if o == 0 else None
            for cc in range(2):
                for sq_ in range(4):
                    for ti in range(2):
                        col = (sq_ * 2 + ti) * 128
                        inp_ = (zc[:, cc, col:col + 128] if o == 0 else u1[:, cc, col:col + 128])
                        P.tr(pT[:, (sq_ * 2 + ti) * 128:(sq_ * 2 + ti + 1) * 128], inp_, C.ident_b[:], ["zc", "u1"], ["psT"])
                pv = pT[:].rearrange("p (s t c) -> p t s c", s=4, t=2)
                P.cp("dve", vtok[:, :, :, cc * 128:(cc + 1) * 128], pv, ["psT"], ["vtok"])
            n_ = 0
            for fch in range(4):
                for cb in range(2):
                    pr, pi_ = pU[(n_ % 2) * 2], pU[(n_ % 2) * 2 + 1]
                    kr_, ki_ = "psU%d" % ((n_ % 2) * 2), "psU%d" % ((n_ % 2) * 2 + 1)
                    n_ += 1
                    rhs = lambda ti: vtok[:, ti, cb * 2:cb * 2 + 2, :].rearrange("p s c -> p (s c)")
                    for ti in range(2):
                        P.mm(pr[:], dC[:, ti, fch * 128:(fch + 1) * 128], rhs(ti), ti == 0, ti == 1, ["cst", "vtok"], [kr_])
                    for ti in range(2):
                        P.mm(pi_[:], dSn[:, ti, fch * 128:(fch + 1) * 128], rhs(ti), ti == 0, ti == 1, ["cst", "vtok"], [ki_])
                    b = n_ % 2
                    for bh in range(2):
                        sl = slice(bh * 256, (bh + 1) * 256)
                        kr = Kr[:, fch, o * 256:(o + 1) * 256]
                        ki = Ki[:, fch, o * 256:(o + 1) * 256]
                        osl = slice(cb * 512 + bh * 256, cb * 512 + (bh + 1) * 256)
                        P.tt("dve", ta[b][:, sl], pr[:, sl], kr, ALU.mult, [kr_, "K"], ["ta%d" % b])
                        P.tt("dve", tb_[b][:, sl], pi_[:, sl], ki, ALU.mult, [ki_, "K"], ["tb%d" % b])
                        P.tt("pool", Yr[:, fch, osl], ta[b][:, sl], tb_[b][:, sl], ALU.subtract, ["ta%d" % b, "tb%d" % b], ["Y"])
                        P.tt("dve", ta[b][:, sl], pr[:, sl], ki, ALU.mult, [kr_, "K"], ["ta%d" % b])
                        P.tt("dve", tb_[b][:, sl], pi_[:, sl], kr, ALU.mult, [ki_, "K"], ["tb%d" % b])
                        P.tt("pool", Yi[:, fch, osl], ta[b][:, sl], tb_[b][:, sl], ALU.add, ["ta%d" % b, "tb%d" % b], ["Y"])
            n_ = 0
            for sq_ in range(4):
                for cc in range(2):
                    pb = n_ % 4
                    n_ += 1
                    cs = slice(sq_ * 256 + cc * 128, sq_ * 256 + (cc + 1) * 128)
                    for fch in range(4):
                        P.mm(pU[pb][:, 0:256], Yr[:, fch, cs], iC[:, fch, :], fch == 0, False, ["Y", "cst"], ["psU%d" % pb])
                        P.mm(pU[pb][:, 0:256], Yi[:, fch, cs], iSn[:, fch, :], False, fch == 3, ["Y", "cst"], ["psU%d" % pb])
                    gate = zc[:, (2 if o == 0 else 4) + cc, sq_ * 256:(sq_ + 1) * 256]
                    if o == 0:
                        dst = u1[:, cc, sq_ * 256:(sq_ + 1) * 256]
                        P.tt("dve", dst, pU[pb][:, 0:256], gate, ALU.mult, ["psU%d" % pb, "zc"], ["u1"])
                    else:
                        dst = C.mixT[:, 6 + cc, sq_ * 256:(sq_ + 1) * 256]
                        P.tt("dve", dst, pU[pb][:, 0:256], gate, ALU.mult, ["psU%d" % pb, "zc"], ["mixH"])


def ph_hyena_s(C, l):
    hs_select(C, l)
    hs_filter(C, l)
    hs_kspec(C, l)
    for o in range(2):
        hs_fwd(C, l, o)
        hs_inv(C, l, o)
    ph_ag(C, C.payH, C.gatH, "agH")
    hs_gather(C, l)


def hs_select(C, l):
    nc = C.nc
    with Phase(nc, "hsa") as P:
        selb = P.sb([128, 2, 64], BF16, "selb")
        P.dma("sp", selb[:], C.inp("selM"), key="sel", writes=["sel"])
        zgb = [P.sb([128, 2, 4096], BF16, "zgb") for _ in range(2)]
        zraw = P.sb([64, 4098], F32, "zraw")
        cbuf = [P.sb([64, 4096], F32, "cbuf") for _ in range(2)]
        ps = [P.ps([128, 512]) for _ in range(4)]
        o_w, _ = C.ppoff["hswm_%d" % l]
        o_b, _ = C.ppoff["hsbm_%d" % l]
        P.memset("dve", zraw[:, 0:1], 0.0, ["zraw"])
        P.memset("dve", zraw[:, 4097:4098], 0.0, ["zraw"])
        n_ = 0
        for blk in range(3):
            zb = blk % 2
            for cc in range(2):
                for r in range(4):
                    P.dma("sp", zgb[zb][:, cc, r * 1024:(r + 1) * 1024], C.gat_zh(r, 2 * blk + cc), key="zgb%d" % zb, writes=["zgb%d" % zb])
            for tb in range(8):
                pb = n_ % 4
                n_ += 1
                for cc in range(2):
                    P.mm(ps[pb][0:64, :], selb[:, cc, :], zgb[zb][:, cc, tb * 512:(tb + 1) * 512], cc == 0, cc == 1,
                         ["sel", "zgb%d" % zb], ["ps%d" % pb])
                if tb % 2:
                    P.cp("act", zraw[:, 1 + tb * 512:1 + (tb + 1) * 512], ps[pb][0:64, :], ["ps%d" % pb], ["zraw%d" % tb])
                else:
                    P.cp("dve", zraw[:, 1 + tb * 512:1 + (tb + 1) * 512], ps[pb][0:64, :], ["ps%d" % pb], ["zraw%d" % tb])
            w = lambda k: C.pp[0:64, o_w + blk * 3 + k:o_w + blk * 3 + k + 1]
            cb = cbuf[blk % 2]
            ck = "cbuf%d" % (blk % 2)
            zk = ["zraw"] + ["zraw%d" % i for i in range(8)]
            P.act(cb[:], zraw[:, 1:4097], AF.Identity, zk, [ck], scale=w(1), bias=C.pp[0:64, o_b + blk:o_b + blk + 1])
            P.stt(cb[:], zraw[:, 0:4096], w(0), cb[:], ALU.mult, ALU.add, zk + [ck], [ck])
            P.stt(cb[:], zraw[:, 2:4098], w(2), cb[:], ALU.mult, ALU.add, zk + [ck], [ck])
            P.dma("sp", C.hyz_d[blk], cb[:], key=ck, reads=[ck])


def hs_filter(C, l):
    nc = C.nc
    with Phase(nc, "hsf") as P:
        zT = P.sb([33, 4096], F32, "zT")
        P.dma("sp", zT[:], C.inp("zT4096"), key="zT", writes=["zT"])
        hid2 = P.sb([64, 4096], F32, "hid2")
        filt_mlp(P, C, l, zT, 4096, hid2)
        w3 = P.sb([64, 256], F32, "w3")
        P.dma("sp", w3[:], C.inp("hy_f_w3m")[l], key="w3", writes=["w3"])
        dec = P.sb([128, 4096], F32, "dec")
        P.dma("sp", dec[:], C.inp("decayS"), key="dec", writes=["dec"])
        hT = [P.sb([128, 4096], F32, "hT") for _ in range(2)]
        pA = [P.ps([128, 512]) for _ in range(2)]
        b3 = ppa(C, "hb3m_%d" % l)
        bm = ppa(C, "hbm_%d" % l)
        n_ = 0
        for o in range(2):
            for tb in range(8):
                pb = n_ % 2
                n_ += 1
                P.mm(pA[pb][:], w3[:, o * 128:(o + 1) * 128], hid2[:, tb * 512:(tb + 1) * 512], True, True, ["w3", "hid2"], ["psA%d" % pb])
                P.stt(hT[o][:, tb * 512:(tb + 1) * 512], pA[pb][:], b3[:, o:o + 1], dec[:, tb * 512:(tb + 1) * 512], ALU.add, ALU.mult,
                      ["psA%d" % pb, "dec"], ["hT%d" % o])
            P.memset("dve", hT[o][64:128, 0:1], 0.0, ["hT%d" % o])
            P.tt("dve", hT[o][0:64, 0:1], hT[o][0:64, 0:1], bm[0:64, o:o + 1], ALU.add, ["hT%d" % o], ["hT%d" % o])
            P.dma("sp", C.hyf_d[o], hT[o][:], key="hTo%d" % o, reads=["hT%d" % o])


def fft_tables(P, C):
    W1 = P.sb([128, 192], BF16, "W1")
    Gc = P.sb([128, 64, 128], BF16, "Gc")
    Gs = P.sb([128, 64, 128], BF16, "Gs")
    P.dma("sp", W1[:], C.inp("fW1"), key="ftabw", writes=["ftabW"])
    P.dma("sp", Gc[:].rearrange("p a b -> p (a b)"), C.inp("fGc"), key="ftabc", writes=["ftab"])
    P.dma("sp", Gs[:].rearrange("p a b -> p (a b)"), C.inp("fGs"), key="ftabs", writes=["ftabS"])
    return W1, Gc, Gs


def fft_fwd(P, X, W1, Gc, Gs, A, psA, psU, handler):
    for c2 in range(32):
        pb = c2 % 2
        for i in range(2):
            P.mm(psA[pb][:, i * 192:(i + 1) * 192], X[:, 2 * c2 + i, :], W1[:], True, True, ["X", "Xz", "Xz2", "ftabW"], ["psA%d" % pb])
        src = psA[pb][:, 0:384].rearrange("p (i f) -> p i f", i=2)
        if c2 % 2:
            P.cp("act", A[:, 2 * c2:2 * c2 + 2, :], src, ["psA%d" % pb], ["A%d" % c2])
        else:
            P.cp("dve", A[:, 2 * c2:2 * c2 + 2, :], src, ["psA%d" % pb], ["A%d" % c2])
    for grp in range(8):
        b = grp % 2
        pre, pim = psU[2 * b], psU[2 * b + 1]
        kre, kim = "psU%d" % (2 * b), "psU%d" % (2 * b + 1)
        for fi in range(8):
            f1 = grp * 8 + fi
            sl = slice(fi * 64, (fi + 1) * 64)
            akeys = ["ftab", "ftabS"] + ["A%d" % i for i in range(32)]
            P.mm(pre[:, sl], Gc[:, f1, :], A[:, :, f1], True, False, akeys, [kre])
            P.mm(pre[:, sl], Gs[:, f1, :], A[:, :, 64 + f1], False, True, akeys, [kre])
            P.mm(pim[:, sl], Gc[:, f1, :], A[:, :, 64 + f1], True, False, akeys, [kim])
            P.mm(pim[:, sl], Gs[:, f1, :], A[:, :, 128 + f1], False, True, akeys, [kim])
        handler(grp, pre, pim, kre, kim)


def hs_kspec(C, l):
    nc = C.nc
    with Phase(nc, "hsk") as P:
        _kspec_body(C, l, P)


def _kspec_body(C, l, P):
    X = P.sb([128, 64, 128], BF16, "X")
    P.memset("dve", X[32:64].rearrange("p a b -> p (a b)"), 0.0, ["Xz"])
    P.memset("act" if False else "dve", X[64:128].rearrange("p a b -> p (a b)"), 0.0, ["Xz2"])
    P.dma("pool", X[0:32], C.hyf_d[0][0:64, :].rearrange("c (a b) -> a c b", b=128), key="X", writes=["X"])
    W1, Gc, Gs = fft_tables(P, C)
    A = P.sb([128, 64, 192], BF16, "A")
    KA = [P.sb([128, 4096], F32, "KA") for _ in range(2)]
    Ko = [P.sb([128, 512], BF16, "Ko") for _ in range(4)]
    psA = [P.ps([128, 512]) for _ in range(2)]
    psU = [P.ps([128, 512]) for _ in range(4)]
    cnt = [0]
    for o in range(2):
        for dd in range(2):
            if (o, dd) != (0, 0):
                P.dma("pool", X[0:32], C.hyf_d[o][dd * 64:(dd + 1) * 64, :].rearrange("c (a b) -> a c b", b=128), key="X", writes=["X"])

            def handler(grp, pre, pim, kre, kim, dd=dd, o=o):
                sl = slice(grp * 512, (grp + 1) * 512)
                if dd == 0:
                    P.cp("act", KA[0][:, sl], pre[:], [kre], ["KAr"])
                    P.cp("dve", KA[1][:, sl], pim[:], [kim], ["KAi"])
                else:
                    for ri, (ps_, kk, op, kak) in enumerate(((pre, kre, ALU.add, "KAr"), (pim, kim, ALU.subtract, "KAi"))):
                        si = cnt[0] % 4
                        cnt[0] += 1
                        P.tt("dve", Ko[si][:], KA[ri][:, sl], ps_[:], op, [kak, kk], ["Ko%d" % si])
                        P.dma("sp", C.Ksp_d[o, ri][:, sl], Ko[si][:], key="Ko%d" % si, reads=["Ko%d" % si])
            fft_fwd(P, X, W1, Gc, Gs, A, psA, psU, handler)


def hs_fwd(C, l, o):
    nc = C.nc
    with Phase(nc, "hsc") as P:
        X = P.sb([128, 64, 128], BF16, "X")
        P.memset("dve", X[32:64].rearrange("p a b -> p (a b)"), 0.0, ["Xz"])
        P.memset("dve", X[64:128].rearrange("p a b -> p (a b)"), 0.0, ["Xz2"])
        if o == 0:
            P.dma("pool", X[0:32], C.hyz_d[0].rearrange("c (a b) -> a c b", b=128), key="X", writes=["X"])
        else:
            P.dma("sp", X[0:32].rearrange("p c t -> p (c t)"), C.X2_d, key="X", writes=["X"])
        W1, Gc, Gs = fft_tables(P, C)
        A = P.sb([128, 64, 192], BF16, "A")
        Ks = [P.sb([128, 4096], BF16, "Ks") for _ in range(2)]
        for ri in range(2):
            P.dma("sp", Ks[ri][:], C.Ksp_d[o, ri], key="Ks", writes=["Ks"])
        Ys = [P.sb([128, 4096], BF16, "Ys") for _ in range(2)]
        ta = [P.sb([128, 512], F32, "ta") for _ in range(2)]
        tb_ = [P.sb([128, 512], F32, "tb") for _ in range(2)]
        psA = [P.ps([128, 512]) for _ in range(2)]
        psU = [P.ps([128, 512]) for _ in range(4)]

        def handler(grp, pre, pim, kre, kim):
            sl = slice(grp * 512, (grp + 1) * 512)
            b = grp % 2
            P.tt("dve", ta[b][:], pre[:], Ks[0][:, sl], ALU.mult, [kre, "Ks"], ["ta%d" % b])
            P.tt("dve", tb_[b][:], pim[:], Ks[1][:, sl], ALU.mult, [kim, "Ks"], ["tb%d" % b])
            P.tt("pool", Ys[0][:, sl], ta[b][:], tb_[b][:], ALU.subtract, ["ta%d" % b, "tb%d" % b], ["Ys"])
            P.tt("dve", ta[b][:], pre[:], Ks[1][:, sl], ALU.mult, [kre, "Ks"], ["ta%d" % b])
            P.tt("dve", tb_[b][:], pim[:], Ks[0][:, sl], ALU.mult, [kim, "Ks"], ["tb%d" % b])
            P.tt("pool", Ys[1][:, sl], ta[b][:], tb_[b][:], ALU.add, ["ta%d" % b, "tb%d" % b], ["Ys"])
        fft_fwd(P, X, W1, Gc, Gs, A, psA, psU, handler)
        for ri in range(2):
            P.dma("sp", C.Y_d[ri], Ys[ri][:], key="Yo", reads=["Ys"])


def hs_inv(C, l, o):
    nc = C.nc
    with Phase(nc, "hsi") as P:
        E1 = P.sb([128, 256], BF16, "E1"); E2 = P.sb([128, 256], BF16, "E2")
        Vc = P.sb([64, 128, 32], BF16, "Vc"); Vsn = P.sb([64, 128, 32], BF16, "Vsn")
        P.dma("sp", E1[:], C.inp("fE1"), key="itab1", writes=["itabE1"])
        P.dma("sp", E2[:], C.inp("fE2"), key="itab2", writes=["itabE2"])
        Ys = [P.sb([128, 64, 64], BF16, "Ys") for _ in range(2)]
        for ri in range(2):
            P.dma("sp", Ys[ri][:].rearrange("p f c -> p (f c)"), C.Y_d[ri], key="Ys", writes=["Ys"])
        P.dma("sp", Vc[:].rearrange("p a b -> p (a b)"), C.inp("fVc"), key="itab3", writes=["itabVc"])
        P.dma("sp", Vsn[:].rearrange("p a b -> p (a b)"), C.inp("fVsn"), key="itab4", writes=["itabVs"])
        Gx = P.sb([32, 64, 128], BF16, "Gx")
        P.dma("pool", Gx[:], C.hyz_d[1 + o].rearrange("c (a b) -> a c b", b=128), key="Gx", writes=["Gx"])
        Br = P.sb([64, 128, 64], BF16, "Br")
        Bi = P.sb([64, 128, 64], BF16, "Bi")
        X2 = P.sb([32, 64, 128], BF16, "X2")
        ps = [P.ps([128, 512]) for _ in range(4)]
        for c2 in range(32):
            pb = c2 % 4
            for i in range(2):
                c = 2 * c2 + i
                P.mm(ps[pb][0:64, i * 256:(i + 1) * 256], Ys[0][:, :, c], E1[:], True, False, ["Ys", "itabE1"], ["ps%d" % pb])
                P.mm(ps[pb][0:64, i * 256:(i + 1) * 256], Ys[1][:, :, c], E2[:], False, True, ["Ys", "itabE2"], ["ps%d" % pb])
            pv = ps[pb][0:64, :].rearrange("p (i r t) -> p r t i", i=2, r=2)
            P.cp("act", Br[:, :, 2 * c2:2 * c2 + 2], pv[:, 0], ["ps%d" % pb], ["Br%d" % c2])
            P.cp("dve", Bi[:, :, 2 * c2:2 * c2 + 2], pv[:, 1], ["ps%d" % pb], ["Bi%d" % c2])
        for tg in range(16):
            pb = tg % 4
            for ti in range(8):
                t2 = tg * 8 + ti
                sl = slice(ti * 64, (ti + 1) * 64)
                bkeys = ["itabVc", "itabVs"] + ["Br%d" % i for i in range(32)] + ["Bi%d" % i for i in range(32)]
                P.mm(ps[pb][0:32, sl], Vc[:, t2, :], Br[:, t2, :], True, False, bkeys, ["ps%d" % pb])
                P.mm(ps[pb][0:32, sl], Vsn[:, t2, :], Bi[:, t2, :], False, True, bkeys, ["ps%d" % pb])
            pv = ps[pb][0:32, :].rearrange("p (t c) -> p c t", c=64)
            P.tt("dve", X2[:, :, tg * 8:(tg + 1) * 8], pv, Gx[:, :, tg * 8:(tg + 1) * 8], ALU.mult, ["ps%d" % pb, "Gx"], ["X2"])
        dst = C.X2_d if o == 0 else C.payH.ap()
        P.dma("sp", dst, X2[:].rearrange("p c t -> p (c t)"), key="X2", reads=["X2"])


def hs_gather(C, l):
    nc = C.nc
    with Phase(nc, "hsg") as P:
        hyall = P.sb([128, 2, 4096], BF16, "hyall")
        for r in range(4):
            dst = hyall[(r % 2) * 64:(r % 2) * 64 + 64, r // 2, :].rearrange("p (a b) -> p a b", b=128)
            src = C.gatH.ap()[r * 32:(r + 1) * 32, :].rearrange("a (c b) -> c a b", b=128)
            P.dma("sp", dst, src, key="hyall", writes=["hyall"])
        eq = ppa(C, "eq")
        acc = P.sb([128, 2, 1024], F32, "acc")
        for k in range(2):
            P.ts("dve", acc[:, k, :], hyall[:, k, 0:1024], eq[:, 0:1], ALU.mult, ["hyall"], ["acc"])
            for j in range(1, 4):
                P.stt(acc[:, k, :], hyall[:, k, j * 1024:(j + 1) * 1024], eq[:, j:j + 1], acc[:, k, :], ALU.mult, ALU.add,
                      ["hyall", "acc"], ["acc"])
            P.cp("act", C.mixT[:, 6 + k, 0:1024], acc[:, k, :], ["acc"], ["mixH"])


_CACHE = {}


def kernel(**inputs):
    inp = {k: np.asarray(v) for k, v in inputs.items()}
    maps, ppoff = host_prep(inp)
    npp = maps[0]["pp"].shape[1]
    if "nc" not in _CACHE:
        _CACHE["nc"] = build(ppoff, npp)
    nc, C = _CACHE["nc"]
    maps = [{k: v for k, v in m.items() if k in C.declared} for m in maps]
    res = run_bass_kernel_spmd(nc, maps, core_ids=list(range(NCORE)))
    return assemble(res.results)


def assemble(results):
    y_prompt = np.zeros((32, 256, D), np.float32)
    y_sample = np.zeros((2, 4096, D), np.float32)
    new_k = np.zeros((32, DEPTH, 256, 2, 64), np.float32)
    new_v = np.zeros((32, DEPTH, 256, 2, 64), np.float32)
    for i, r in enumerate(results):
        b, q = i // 4, i % 4
        y = np.asarray(r["y_tok"])
        y_prompt[4 * i:4 * i + 4] = y[:1024].reshape(4, 256, D)
        y_sample[b, 1024 * q:1024 * q + 1024] = y[1024:]
        new_k[4 * i:4 * i + 4] = np.asarray(r["new_k"]).reshape(4, DEPTH, 256, 2, 64)
        new_v[4 * i:4 * i + 4] = np.asarray(r["new_v"]).reshape(4, DEPTH, 256, 2, 64)
    return (y_prompt, y_sample, new_k, new_v)
```

```python
import contextlib
import os
import math
import numpy as np
import ml_dtypes
import concourse.bass as bass
import concourse.mybir as mybir
from concourse.bass_utils import run_bass_kernel_spmd

F32 = mybir.dt.float32
BF16 = mybir.dt.bfloat16
ALU = mybir.AluOpType
AF = mybir.ActivationFunctionType

D = 1024
DEPTH = 2
NCORE = 8
EPS = 1e-6
FFN = 2816
NJ = 22
L_S = 4096
L_P = 256
PAYROWS = 1032
NOHY = bool(os.environ.get('NOHY'))
BF16_CONSTS = ['fW1', 'fGc', 'fGs', 'fE1', 'fE2', 'fVc', 'fVsn', 'dftC', 'dftS', 'dftSn', 'idftC', 'idftSn', 'selM', 'onesm', 'blk64', 'ropeRT', 'identb']


class Sched:
    ENGS = ("pe", "act", "dve", "pool", "sp")
    HND = dict(pe="tensor", act="scalar", dve="vector", pool="gpsimd", sp="sync")
    UID = 0
    GLOBAL = {}

    def __init__(self, nc):
        self.nc = nc
        self.ops = []
        self.last_w = {}
        self.readers = {}
        self.dma_keys = {}

    def _deps(self, reads, writes):
        deps = set()
        for r in reads:
            lw = self.last_w.get(r)
            if lw is not None:
                deps.add(lw)
        for w in writes:
            lw = self.last_w.get(w)
            if lw is not None:
                deps.add(lw)
            for rd in self.readers.get(w, ()):
                deps.add(rd)
        return deps

    def _commit(self, oid, reads, writes):
        for w in writes:
            self.last_w[w] = oid
            self.readers[w] = []
        for r in reads:
            if r in writes:
                continue
            self.readers.setdefault(r, []).append(oid)

    def op(self, eng, fn, reads=(), writes=()):
        reads = tuple(reads); writes = tuple(writes)
        writes = writes + tuple(r for r in reads if r.startswith("ps") and r not in writes)
        deps = self._deps(reads, writes)
        oid = len(self.ops)
        self.ops.append(dict(id=oid, eng=eng, fn=fn, deps=deps, kind="c", sig=False))
        self._commit(oid, reads, writes)
        return oid

    def dma(self, queue, out, in_, key, reads=(), writes=(), **kw):
        reads = tuple(reads); writes = tuple(writes)
        deps = self._deps(reads, writes)
        for w in writes:
            lw = self.last_w.get(w)
            if lw is not None and self.ops[lw]["kind"] == "d" and self.ops[lw]["key"] == key and self.ops[lw]["eng"] == queue:
                deps.discard(lw)
        oid = len(self.ops)
        cnt = self.dma_keys.setdefault(key, [0, "d"])
        cnt[0] += 1
        self.ops.append(dict(id=oid, eng=queue, deps=deps, kind="d", key=key,
                             cum=16 * cnt[0], out=out, in_=in_, kw=kw, sig=False))
        self._commit(oid, reads, writes)
        return oid

    def cc(self, fn, key, reads=(), writes=()):
        reads = tuple(reads); writes = tuple(writes)
        deps = self._deps(reads, writes)
        oid = len(self.ops)
        cnt = self.dma_keys.setdefault(key, [0, "cc"])
        cnt[0] += 1
        self.ops.append(dict(id=oid, eng="pool", deps=deps, kind="cc", key=key,
                             cum=cnt[0], fn=fn, sig=False))
        self._commit(oid, reads, writes)
        return oid

    def emit(self):
        nc = self.nc
        ops = self.ops
        for o in ops:
            for d in o["deps"]:
                Dd = ops[d]
                if Dd["kind"] == "c" and Dd["eng"] != o["eng"]:
                    Dd["sig"] = True
        last = {}
        for o in ops:
            if o["kind"] == "c":
                last[o["eng"]] = o
        for o in last.values():
            o["sig"] = True
        cnt = {e: 0 for e in self.ENGS}
        for o in ops:
            if o["kind"] == "c" and o["sig"]:
                cnt[o["eng"]] += 1
                o["sigval"] = cnt[o["eng"]]
        per_eng = {e: [] for e in self.ENGS}
        for o in ops:
            per_eng[o["eng"]].append(o)
        with contextlib.ExitStack() as st:
            G = Sched.GLOBAL
            if G.get("nc") is not nc:
                G.clear()
                G.update(nc=nc, esem={e: nc.alloc_semaphore(name="ge_%s" % e) for e in self.ENGS},
                         ecount={e: 0 for e in self.ENGS}, kpool=[], kcount=[])
            while len(G["kpool"]) < len(self.dma_keys):
                G["kpool"].append(nc.alloc_semaphore(name="gk_%d" % len(G["kpool"])))
                G["kcount"].append(0)
            esem = G["esem"]
            ksem = {}
            kbase = {}
            for i, k in enumerate(self.dma_keys):
                ksem[k] = G["kpool"][i]
                kbase[k] = G["kcount"][i]
            for o in ops:
                if o["kind"] == "c" and o["sig"]:
                    o["sigval"] += G["ecount"][o["eng"]]
                elif o["kind"] in ("d", "cc"):
                    o["cum"] += kbase[o["key"]]
            for e in self.ENGS:
                G["ecount"][e] += cnt[e]
            for i, (k, c) in enumerate(self.dma_keys.items()):
                G["kcount"][i] += (16 if c[1] == "d" else 1) * c[0]
            block = st.enter_context(nc.Block())

            def body_for(ename):
                def body(eng):
                    waited = {}

                    def need(sem, val, tag):
                        if waited.get(tag, 0) >= val:
                            return
                        eng.wait_ge(sem, val)
                        waited[tag] = val

                    for o in per_eng[ename]:
                        for d in sorted(o["deps"]):
                            Dd = ops[d]
                            if Dd["kind"] == "c":
                                if Dd["eng"] == ename:
                                    continue
                                need(esem[Dd["eng"]], Dd["sigval"], "e_" + Dd["eng"])
                            else:
                                need(ksem[Dd["key"]], Dd["cum"], ("k", Dd["key"]))
                        if o["kind"] == "c":
                            ins = o["fn"](eng)
                            if o["sig"]:
                                ins.then_inc(esem[ename], 1)
                        elif o["kind"] == "d":
                            eng.dma_start(out=o["out"], in_=o["in_"], **o["kw"]).then_inc(ksem[o["key"]], 16)
                        else:
                            o["fn"](eng).then_inc(ksem[o["key"]], 1)
                    if ename == "sp":
                        for k, c in self.dma_keys.items():
                            need(ksem[k], kbase[k] + (16 if c[1] == "d" else 1) * c[0], ("k", k))
                        for e2, o2 in last.items():
                            if e2 != ename:
                                need(esem[e2], o2["sigval"], "e_" + e2)
                return body

            for ename in self.ENGS:
                if per_eng[ename] or ename == "sp":
                    getattr(block, self.HND[ename])(body_for(ename))


class Phase:
    UID = 0

    def __init__(self, nc, name):
        self.nc = nc
        self.name = name
        self.st = contextlib.ExitStack()
        self.S = Sched(nc)
        self._n = 0
        self._rr = 0

    def __enter__(self):
        self.st.__enter__()
        return self

    def __exit__(self, *a):
        if a[0] is None:
            self.S.emit()
        return self.st.__exit__(*a)

    def sb(self, shape, dt, name=None):
        self._n += 1
        Phase.UID += 1
        return self.st.enter_context(self.nc.sbuf_tensor("%s_%s%d" % (self.name, name or "t", Phase.UID), list(shape), dt))

    def ps(self, shape=(128, 512), dt=F32):
        self._n += 1
        Phase.UID += 1
        return self.st.enter_context(self.nc.psum_tensor("%s_p%d" % (self.name, Phase.UID), list(shape), dt))

    def mm(self, out, lhsT, rhs, start, stop, reads, writes, **kw):
        return self.S.op("pe", lambda e: e.matmul(out, lhsT=lhsT, rhs=rhs, start=start, stop=stop, **kw), reads, writes)

    def tr(self, out, in_, ident, reads, writes):
        return self.S.op("pe", lambda e: e.transpose(out, in_, ident), reads, writes)

    def act(self, out, in_, func, reads, writes, bias=None, scale=None, eng="act"):
        kw = {}
        if bias is not None:
            kw["bias"] = bias
        if scale is not None:
            kw["scale"] = scale
        return self.S.op("act", lambda e: e.activation(out=out, in_=in_, func=func, **kw), reads, writes)

    def tt(self, eng, out, in0, in1, op, reads, writes):
        return self.S.op(eng, lambda e: e.tensor_tensor(out=out, in0=in0, in1=in1, op=op), reads, writes)

    def ts(self, eng, out, in0, s1, op0, reads, writes, s2=None, op1=None):
        if op1 is None:
            return self.S.op(eng, lambda e: e.tensor_scalar(out=out, in0=in0, scalar1=s1, scalar2=None, op0=op0), reads, writes)
        return self.S.op(eng, lambda e: e.tensor_scalar(out=out, in0=in0, scalar1=s1, scalar2=s2, op0=op0, op1=op1), reads, writes)

    def stt(self, out, in0, scalar, in1, op0, op1, reads, writes):
        return self.S.op("dve", lambda e: e.scalar_tensor_tensor(out=out, in0=in0, scalar=scalar, in1=in1, op0=op0, op1=op1), reads, writes)

    def cp(self, eng, out, in_, reads, writes):
        if eng == "act":
            return self.S.op("act", lambda e: e.activation(out=out, in_=in_, func=AF.Copy), reads, writes)
        return self.S.op(eng, lambda e: e.tensor_copy(out=out, in_=in_), reads, writes)

    def memset(self, eng, ap, val, writes):
        return self.S.op(eng, lambda e: e.memset(ap, val), (), writes)

    def dma(self, q, out, in_, key, reads=(), writes=(), **kw):
        return self.S.dma(q, out, in_, key, reads, writes, **kw)

    def rsqrt(self, out, in_, reads, writes, eps_ap):
        self.act(out, in_, AF.Ln, reads, writes, bias=eps_ap)
        return self.act(out, out, AF.Exp, writes, writes, scale=-0.5)


def _pt(v, n):
    return np.ascontiguousarray(np.asarray(v, np.float32).reshape(n, 128).T)


class Pack:
    def __init__(self):
        self.cols = []
        self.off = {}
        self.n = 0

    def add(self, name, arr):
        arr = np.asarray(arr, np.float32)
        if arr.ndim == 1:
            arr = arr[:, None]
        arr = arr.reshape(arr.shape[0], -1)
        if arr.shape[0] < 128:
            arr = np.concatenate([arr, np.zeros((128 - arr.shape[0], arr.shape[1]), np.float32)], 0)
        self.off[name] = (self.n, arr.shape[1])
        self.cols.append(arr)
        self.n += arr.shape[1]

    def build(self):
        return np.ascontiguousarray(np.concatenate(self.cols, 1))


def _rope_tables(tok):
    rows = tok // 64
    cols = tok % 64
    inv = 10000.0 ** (-np.arange(0, 32, 2, dtype=np.float32) / 32.0)
    cosT = np.zeros((128, len(tok)), np.float32)
    sinT = np.zeros((128, len(tok)), np.float32)
    for p in range(128):
        dd = p % 64
        pos = rows if dd < 32 else cols
        ang = pos.astype(np.float32) * inv[dd % 16]
        cosT[p] = np.cos(ang)
        sinT[p] = np.sin(ang)
    return cosT, sinT


def _consts():
    c = {}
    c["ident"] = np.eye(128, dtype=np.float32)
    c["onesm"] = np.full((128, 128), 1.0 / 1024.0, np.float32)
    b = np.zeros((128, 128), np.float32)
    b[:64, :64] = 1.0 / 64.0
    b[64:, 64:] = 1.0 / 64.0
    c["blk64"] = b
    rt = np.zeros((128, 128), np.float32)
    for m in range(128):
        if (m % 32) < 16:
            rt[m + 16, m] = -1.0
        else:
            rt[m - 16, m] = 1.0
    c["ropeRT"] = rt
    L = 256
    t = np.linspace(0.0, 1.0, L, dtype=np.float32)[:, None]
    w_ang = (2.0 * math.pi * np.arange(L, dtype=np.float32)[:, None] / L).astype(np.float32)
    bands = np.linspace(1e-4, 15, 16, dtype=np.float32)[None, :]
    z = np.concatenate([t, np.cos(bands * w_ang), np.sin(-bands * w_ang)], -1).astype(np.float32)
    c["zT256"] = np.ascontiguousarray(z.T)
    deltas = np.abs(np.linspace(math.log(1e-2) / 1.5, math.log(1e-2) / 0.3, 256, dtype=np.float32))
    dec = np.exp(-t * deltas[None, :]).astype(np.float32)
    c["decayP"] = np.ascontiguousarray(dec.reshape(2, 128, 256).transpose(1, 0, 2))
    tt = np.arange(256, dtype=np.float64)[:, None]
    ff = np.arange(512, dtype=np.float64)[None, :]
    ang = 2.0 * math.pi * tt * ff / 512.0
    cm, sm = np.cos(ang), np.sin(ang)
    lay_t = lambda a: np.ascontiguousarray(a.reshape(2, 128, 512).transpose(1, 0, 2).astype(np.float32))
    c["dftC"] = lay_t(cm); c["dftS"] = lay_t(sm); c["dftSn"] = lay_t(-sm)
    lay_f = lambda a: np.ascontiguousarray(a.T.reshape(4, 128, 256).transpose(1, 0, 2).astype(np.float32))
    c["idftC"] = lay_f(cm / 512.0); c["idftSn"] = lay_f(-sm / 512.0)
    L = 4096
    t = np.linspace(0.0, 1.0, L, dtype=np.float32)[:, None]
    w_ang = (2.0 * math.pi * np.arange(L, dtype=np.float32)[:, None] / L).astype(np.float32)
    z = np.concatenate([t, np.cos(bands * w_ang), np.sin(-bands * w_ang)], -1).astype(np.float32)
    c["zT4096"] = np.ascontiguousarray(z.T)
    t1 = np.arange(32, dtype=np.float64)[:, None]
    f1 = np.arange(64, dtype=np.float64)[None, :]
    a1 = 2.0 * math.pi * t1 * f1 / 64.0
    c["fW1"] = np.concatenate([np.cos(a1), -np.sin(a1), -np.cos(a1)], 1).astype(np.float32)
    t2 = np.arange(128, dtype=np.float64)[:, None, None]
    f1_ = np.arange(64, dtype=np.float64)[None, :, None]
    f2 = np.arange(128, dtype=np.float64)[None, None, :]
    ag = 2.0 * math.pi * t2 * (f1_ + 64.0 * f2) / 8192.0
    c["fGc"] = np.cos(ag).astype(np.float32).reshape(128, 8192)
    c["fGs"] = np.sin(ag).astype(np.float32).reshape(128, 8192)
    f2c = np.arange(128, dtype=np.float64)[:, None]
    t2r = np.arange(128, dtype=np.float64)[None, :]
    ae = 2.0 * math.pi * f2c * t2r / 128.0
    c["fE1"] = np.concatenate([np.cos(ae), np.sin(ae)], 1).astype(np.float32)
    c["fE2"] = np.concatenate([-np.sin(ae), np.cos(ae)], 1).astype(np.float32)
    f1c = np.arange(64, dtype=np.float64)[:, None, None]
    t2m = np.arange(128, dtype=np.float64)[None, :, None]
    t1m = np.arange(32, dtype=np.float64)[None, None, :]
    av = 2.0 * math.pi * f1c * (128.0 * t1m + t2m) / 8192.0
    c["fVc"] = (np.cos(av) / 8192.0).astype(np.float32).reshape(64, 4096)
    c["fVsn"] = (-np.sin(av) / 8192.0).astype(np.float32).reshape(64, 4096)
    c["_deltas"] = deltas
    return c


def host_prep(inp):
    cs = _consts()
    maps = []
    for i in range(NCORE):
        b, q = i // 4, i % 4
        m = {}
        m["x_tok"] = np.ascontiguousarray(np.concatenate(
            [inp["x_prompt"][4 * i:4 * i + 4].reshape(1024, D), inp["x_sample"][b, 1024 * q:1024 * q + 1024]], 0).astype(np.float32))
        cond2 = np.stack([inp["c_ctx"], inp["c"][b]], 0).astype(np.float32)
        m["condT"] = np.ascontiguousarray(cond2.reshape(2, 8, 128).transpose(2, 1, 0))
        m["cache_k"] = np.ascontiguousarray(inp["cache_k"][b].reshape(DEPTH, 256, 128).astype(np.float32))
        m["cache_v"] = np.ascontiguousarray(inp["cache_v"][b].reshape(DEPTH, 256, 128).astype(np.float32))
        for nm in ["w_mod", "w_in", "w_o", "ffn_w_up", "ffn_w_down", "conv_pw_w", "hy_f_w1", "hy_f_w2", "hy_f_w3"]:
            m[nm] = np.ascontiguousarray(inp[nm].astype(np.float32))
        m["hy_f_b3r"] = np.ascontiguousarray(inp["hy_f_b3"].reshape(DEPTH, 1, 1024).astype(np.float32))
        m["hy_biasr"] = np.ascontiguousarray(inp["hy_bias"].reshape(DEPTH, 1, 512).astype(np.float32))
        pk = Pack()
        for l in range(DEPTH):
            pk.add("norm1_%d" % l, _pt(inp["norm1"][l], 8))
            pk.add("norm2_%d" % l, _pt(inp["norm2"][l], 8))
            pk.add("bmod_%d" % l, _pt(inp["b_mod"][l], 48))
            pk.add("qw_%d" % l, np.tile(inp["q_norm"][l], 2))
            pk.add("kw_%d" % l, np.tile(inp["k_norm"][l], 2))
            pk.add("cdw_%d" % l, inp["conv_dw_w"][l].T.reshape(2, 128, 31).transpose(1, 0, 2))
            pk.add("cdb_%d" % l, _pt(inp["conv_dw_b"][l], 2))
            pk.add("gnw_%d" % l, _pt(inp["conv_gn_w"][l], 2))
            pk.add("gnb_%d" % l, _pt(inp["conv_gn_b"][l], 2))
            pk.add("pwb_%d" % l, _pt(inp["conv_pw_b"][l], 2))
            pk.add("fdw_%d" % l, inp["ffn_dw_w"][l].T.reshape(44, 128, 3).transpose(1, 0, 2))
            pk.add("fdb_%d" % l, _pt(inp["ffn_dw_b"][l], 44))
            pk.add("hsw_%d" % l, inp["hy_short_w"][l].T.reshape(6, 128, 3).transpose(1, 0, 2))
            pk.add("hsb_%d" % l, _pt(inp["hy_short_b"][l], 6))
            pk.add("hfr_%d" % l, inp["hy_freq"][l])
            pk.add("hfb1_%d" % l, inp["hy_f_b1"][l])
            pk.add("hfb2_%d" % l, inp["hy_f_b2"][l])
        pk.add("fnorm", _pt(inp["final_norm"], 8))
        eq = np.zeros((128, 4), np.float32); eq[:, q] = 1.0
        el = np.zeros((128, 4), np.float32)
        er = np.zeros((128, 4), np.float32)
        if q > 0:
            el[:, q - 1] = 1.0
        if q < 3:
            er[:, q + 1] = 1.0
        pk.add("eq", eq); pk.add("el", el); pk.add("er", er)
        chs = 64 * q + np.arange(64)
        for l in range(DEPTH):
            hw = inp["hy_short_w"][l]
            pk.add("hswm_%d" % l, np.stack([hw[:, blk * 256 + chs].T for blk in range(3)], 1))
            pk.add("hsbm_%d" % l, np.stack([inp["hy_short_b"][l][blk * 256 + chs] for blk in range(3)], 1))
            b3 = inp["hy_f_b3"][l].reshape(2, 2, 256)
            pk.add("hb3m_%d" % l, np.stack([np.concatenate([b3[o, 0, chs], b3[o, 1, chs]]) for o in range(2)], 1))
            pk.add("hbm_%d" % l, inp["hy_bias"][l][:, chs].T)
        pk.add("eps", np.full((128, 1), EPS, np.float32))
        m["pp"] = pk.build()
        tok = np.arange(1024 * q, 1024 * q + 1024)
        cosT, sinT = _rope_tables(tok)
        m["ropec"] = cosT
        m["ropes"] = sinT
        sel = np.zeros((128, 2, 64), np.float32)
        for mm_ in range(64):
            ch = 64 * q + mm_
            sel[ch % 128, ch // 128, mm_] = 1.0
        m["selM"] = sel
        w3 = inp["hy_f_w3"].reshape(DEPTH, 64, 2, 2, 256)
        m["hy_f_w3m"] = np.ascontiguousarray(w3[:, :, :, :, chs].reshape(DEPTH, 64, 256).astype(np.float32))
        tS = np.linspace(0.0, 1.0, 4096, dtype=np.float32)[None, :]
        dS = np.exp(-tS * cs["_deltas"][chs][:, None]).astype(np.float32)
        m["decayS"] = np.ascontiguousarray(np.concatenate([dS, dS], 0))
        for k, v in cs.items():
            if not k.startswith("_"):
                m[k] = v
        m["identb"] = cs["ident"]
        for k in BF16_CONSTS:
            m[k] = np.ascontiguousarray(m[k]).astype(ml_dtypes.bfloat16)
        maps.append(m)
    return maps, pk.off


class Ctx:
    pass


def build(ppoff, npp, stop_after=None, dbg=None, extra_shapes=None, only=None):
    nc = bass.Bass("TRN2", target_bir_lowering=False)
    C = Ctx()
    C.nc = nc
    C.ppoff = ppoff
    do = lambda n, s, dt=F32: nc.dram_tensor(n, list(s), dt, kind="ExternalOutput").ap()
    C.declared = {}
    shapes = dict(x_tok=[2048, D], condT=[128, 8, 2], cache_k=[DEPTH, 256, 128], cache_v=[DEPTH, 256, 128],
                  w_mod=[DEPTH, D, 6 * D], w_in=[DEPTH, D, 2048], w_o=[DEPTH, D, D], ffn_w_up=[DEPTH, D, 2 * FFN],
                  ffn_w_down=[DEPTH, FFN, D], conv_pw_w=[DEPTH, 256, 256], pp=[128, npp], ropec=[128, 1024],
                  ropes=[128, 1024], ident=[128, 128], identb=[128, 128], onesm=[128, 128], blk64=[128, 128], ropeRT=[128, 128],
                  hy_f_w1=[DEPTH, 33, 64], hy_f_w2=[DEPTH, 64, 64], hy_f_w3=[DEPTH, 64, 1024], hy_f_b3r=[DEPTH, 1, 1024],
                  hy_biasr=[DEPTH, 1, 512], zT256=[33, 256], decayP=[128, 2, 256], dftC=[128, 2, 512], dftS=[128, 2, 512],
                  dftSn=[128, 2, 512], idftC=[128, 4, 256], idftSn=[128, 4, 256],
                  selM=[128, 2, 64], hy_f_w3m=[DEPTH, 64, 256], decayS=[128, 4096], zT4096=[33, 4096], fW1=[32, 192],
                  fGc=[128, 8192], fGs=[128, 8192], fE1=[128, 256], fE2=[128, 256], fVc=[64, 4096], fVsn=[64, 4096])
    shapes.update(extra_shapes or {})

    def inp(name):
        if name not in C.declared:
            C.declared[name] = nc.dram_tensor(name, list(shapes[name]), BF16 if name in BF16_CONSTS else F32,
                                              kind="ExternalInput").ap()
        return C.declared[name]
    C.inp = inp
    C.y_tok = do("y_tok", [2048, D])
    C.new_k = do("new_k", [4, DEPTH, 256, 128])
    C.new_v = do("new_v", [4, DEPTH, 256, 128])
    C.dbg = do("dbg", dbg[1]) if dbg else None
    C.dbgname = dbg[0] if dbg else None
    dt_ = lambda n, s, dt=BF16: nc.dram_tensor(n, list(s), dt).ap()
    C.qT_d = [dt_("qT_d%d" % g, [512, 1024]) for g in range(2)]
    C.uT_d = [dt_("uT_d%d" % g, [256, 1024]) for g in range(2)]
    C.kT_d = dt_("kT_d", [128, 1024])
    C.vtok_d = dt_("vtok_d", [1024, 128])
    C.zhT_d = dt_("zhT_d", [768, 1024])
    C.payA = nc.dram_tensor("payA", [264, 1024], BF16)
    C.gatA = nc.dram_tensor("gatA", [4 * 264, 1024], BF16)
    C.payB = nc.dram_tensor("payB", [384, 1024], BF16)
    C.gatB = nc.dram_tensor("gatB", [4 * 384, 1024], BF16)
    C.payC = nc.dram_tensor("payC", [384, 1024], BF16)
    C.gatC = nc.dram_tensor("gatC", [4 * 384, 1024], BF16)
    C.pay_kT = C.payA.ap()[0:128, :]
    C.pay_V = C.payA.ap()[128:256, :].rearrange("r (a f) -> (r a) f", f=128)
    C.pay_ub = C.payA.ap()[256:264, :].rearrange("r (a f) -> (r a) f", f=32)
    C.pay_zh = lambda zc: (C.payB if zc < 3 else C.payC).ap()[(zc % 3) * 128:(zc % 3) * 128 + 128, :]
    C.gat_kT = lambda r: C.gatA.ap()[r * 264:r * 264 + 128, :]
    C.gat_V = lambda r: C.gatA.ap()[r * 264 + 128:r * 264 + 256, :].rearrange("r (a f) -> (r a) f", f=128)
    C.gat_ub = lambda r: C.gatA.ap()[r * 264 + 256:r * 264 + 264, :].rearrange("r (a f) -> (r a) f", f=32)
    C.gat_zh = lambda r, zc: (C.gatB if zc < 3 else C.gatC).ap()[r * 384 + (zc % 3) * 128:r * 384 + (zc % 3) * 128 + 128, :]
    C.pay3 = nc.dram_tensor("pay3", [128, 1024], BF16)
    C.hyz_d = dt_("hyz_d", [3, 64, 4096], F32)
    C.hyf_d = dt_("hyf_d", [2, 128, 4096], F32)
    C.Ksp_d = dt_("Ksp_d", [2, 2, 128, 4096], BF16)
    C.Y_d = dt_("Y_d", [2, 128, 4096], BF16)
    C.X2_d = dt_("X2_d", [32, 8192], BF16)
    C.payH = nc.dram_tensor("payH", [32, 8192], BF16)
    C.gatH = nc.dram_tensor("gatH", [128, 8192], BF16)
    C.gat3 = nc.dram_tensor("gat3", [4 * 128, 1024], BF16)

    with contextlib.ExitStack() as gst:
        gsb = lambda n, s, dt: gst.enter_context(nc.sbuf_tensor("g_" + n, list(s), dt))
        C.xT = gsb("xT", [128, 8, 2048], F32)
        C.hT = gsb("hT", [128, 8, 1026], BF16)
        C.mixT = C.hT
        C.pp = gsb("pp", [128, npp], F32)
        C.modT = gsb("modT", [128, DEPTH, 48, 2], F32)
        C.A1 = gsb("A1", [128, DEPTH, 8, 2], F32)
        C.A2 = gsb("A2", [128, DEPTH, 8, 2], F32)
        C.ident_f = gsb("ident_f", [128, 128], F32)
        C.ident_b = gsb("ident_b", [128, 128], BF16)
        C.onesm = gsb("onesm", [128, 128], BF16)
        C.blk64 = gsb("blk64", [128, 128], BF16)
        C.ropeRT = gsb("ropeRT", [128, 128], BF16)

        phases = []
        phases.append(("init", lambda: ph_init(C)))
        phases.append(("xin", lambda: ph_xin(C)))
        for l in range(DEPTH):
            for g in (1, 0):
                phases.append(("norm1_%d_%d" % (l, g), lambda l=l, g=g: ph_norm(C, l, g, 1)))
                phases.append(("proj_%d_%d" % (l, g), lambda l=l, g=g: ph_proj(C, l, g)))
            for g in (0, 1):
                if g == 0:
                    phases.append(("attn_%d_%d" % (l, g), lambda l=l, g=g: ph_attn(C, l, g)))
                else:
                    phases.append(("attn_%d_%d" % (l, g), lambda l=l, g=g: ph_attn(C, l, g)))
                phases.append(("conv_%d_%d" % (l, g), lambda l=l, g=g: ph_conv(C, l, g)))
                if NOHY:
                    phases.append(("wo_%d_%d" % (l, g), lambda l=l, g=g: ph_wo(C, l, g, zero_hy=True)))
                else:
                    phases.append(("hy_%d_%d" % (l, g), lambda l=l, g=g: ph_hyena(C, l, g)))
                    phases.append(("wo_%d_%d" % (l, g), lambda l=l, g=g: ph_wo(C, l, g)))
                if g == 1:
                    phases.append(("halo3_%d" % l, lambda l=l: ph_halo3(C, l)))
                phases.append(("ffn_%d_%d" % (l, g), lambda l=l, g=g: ph_ffn(C, l, g)))
        phases.append(("final", lambda: ph_final(C)))
        if only is not None:
            with Phase(nc, "ld") as P:
                P.dma("sp", C.pp[:], C.inp("pp"), key="pp", writes=["pp"])
                P.memset("dve", C.hT[:], 0.0, ["h"])
                P.memset("dve", C.xT[:], 0.0, ["x"])
                P.memset("dve", C.modT[:], 0.0, ["m"])
            only(C)
            phases = []
        for name, fn in phases:
            fn()
            if stop_after == name:
                break
        if C.dbg is not None:
            ph_dbg(C)
    C.ncobj = nc
    return nc, C


def ppa(C, name):
    o, n = C.ppoff[name]
    return C.pp[:, o:o + n]


def ph_dbg(C):
    with Phase(C.nc, "dbg") as P:
        src = getattr(C, C.dbgname)
        if hasattr(src, "ap") and not isinstance(src, bass.AP):
            src = src.ap()
        P.dma("pool", C.dbg, src, key="dbg")


def ph_init(C):
    nc = C.nc
    with Phase(nc, "init") as P:
        P.dma("sp", C.pp[:], C.inp("pp"), key="pp", writes=["pp"])
        P.dma("sp", C.ident_f[:], C.inp("ident"), key="idf", writes=["idf"])
        P.dma("sp", C.ident_b[:], C.inp("identb"), key="idb", writes=["idb"])
        P.dma("sp", C.onesm[:], C.inp("onesm"), key="onesm", writes=["onesm"])
        P.dma("sp", C.blk64[:], C.inp("blk64"), key="blk64", writes=["blk64"])
        P.dma("sp", C.ropeRT[:], C.inp("ropeRT"), key="ropeRT", writes=["ropeRT"])
        cond = P.sb([128, 8, 2], F32)
        scb = P.sb([128, 8, 2], BF16)
        P.dma("sp", cond[:], C.inp("condT"), key="cond", writes=["cond"])
        P.act(scb[:], cond[:], AF.Silu, ["cond"], ["scb"])
        wm = [P.sb([128, 8, 512], BF16, "wm") for _ in range(2)]
        psm = P.ps([128, 512])
        for l in range(DEPTH):
            for blk in range(12):
                s = blk % 2
                src = C.inp("w_mod")[l][:, blk * 512:(blk + 1) * 512].rearrange("(kc p) n -> p kc n", p=128)
                P.dma("pool", wm[s][:], src, key="wm%d" % s, writes=["wm%d" % s])
                for mcl in range(4):
                    mc = blk * 4 + mcl
                    for kc in range(8):
                        P.mm(psm[:, mc * 2:mc * 2 + 2], wm[s][:, kc, mcl * 128:(mcl + 1) * 128], scb[:, kc, :],
                             kc == 0, kc == 7, ["wm%d" % s, "scb"], ["psm"])
            bm = ppa(C, "bmod_%d" % l)
            for j in range(2):
                P.tt("dve", C.modT[:, l, :, j], psm[:, 0:96].rearrange("p (m j) -> p m j", j=2)[:, :, j], bm, ALU.add, ["psm", "pp"], ["modT"])
            n1 = ppa(C, "norm1_%d" % l)
            n2 = ppa(C, "norm2_%d" % l)
            for j in range(2):
                P.stt(C.A1[:, l, :, j], C.modT[:, l, 8:16, j], 1.0, n1, ALU.add, ALU.mult, ["modT", "pp"], ["A1"])
                P.stt(C.A2[:, l, :, j], C.modT[:, l, 32:40, j], 1.0, n2, ALU.add, ALU.mult, ["modT", "pp"], ["A2"])


def ph_xin(C):
    nc = C.nc
    with Phase(nc, "xin") as P:
        xin = [P.sb([128, D], F32, "xin") for _ in range(2)]
        pst = [P.ps([128, 512]) for _ in range(4)]
        n = 0
        P.dma("sp", xin[0][:], C.inp("x_tok")[0:128, :], key="xin0", writes=["xin0"])
        for tt in range(16):
            s = tt % 2
            if tt + 1 < 16:
                P.dma("sp", xin[1 - s][:], C.inp("x_tok")[(tt + 1) * 128:(tt + 2) * 128, :], key="xin%d" % (1 - s),
                      writes=["xin%d" % (1 - s)])
            for hh in range(2):
                pb = n % 4
                n += 1
                for kk in range(4):
                    kc = hh * 4 + kk
                    P.tr(pst[pb][:, kk * 128:(kk + 1) * 128], xin[s][:, kc * 128:(kc + 1) * 128], C.ident_f[:],
                         ["xin%d" % s], ["pst%d" % pb])
                out = C.xT[:, hh * 4:hh * 4 + 4, tt * 128:(tt + 1) * 128]
                src = pst[pb][:].rearrange("p (k t) -> p k t", k=4)
                if n % 2:
                    P.cp("act", out, src, ["pst%d" % pb], ["xw%d_%d" % (tt, hh)])
                else:
                    P.cp("dve", out, src, ["pst%d" % pb], ["xw%d_%d" % (tt, hh)])


def ph_norm(C, l, g, which):
    nc = C.nc
    j = g
    with Phase(nc, "nrm") as P:
        sq = [P.sb([128, 8, 512], BF16, "sq") for _ in range(2)]
        tn = [P.sb([128, 8, 512], F32, "tn") for _ in range(2)]
        rs = [P.sb([128, 512], F32, "rs") for _ in range(2)]
        pss = [P.ps([128, 512]) for _ in range(2)]
        eps = ppa(C, "eps")
        for tb in range(2):
            t0 = g * 1024 + tb * 512
            xk = "xT%d" % (t0 // 512)
            P.act(sq[tb][:], C.xT[:, :, t0:t0 + 512], AF.Square, [xk], ["sq%d" % tb])
            for kc in range(8):
                P.mm(pss[tb][:], C.onesm[:], sq[tb][:, kc, :], kc == 0, kc == 7, ["sq%d" % tb], ["pss%d" % tb])
            P.rsqrt(rs[tb][:], pss[tb][:], ["pss%d" % tb], ["rs%d" % tb], eps)
            if which == 1:
                A, B = C.A1, C.modT[:, l, 0:8, :]
            elif which == 2:
                A, B = C.A2, C.modT[:, l, 24:32, :]
            for kc in range(8):
                P.stt(tn[tb][:, kc, :], C.xT[:, kc, t0:t0 + 512], A[:, l, kc, j:j + 1], rs[tb][:], ALU.mult, ALU.mult,
                      [xk, "rs%d" % tb, "A"], ["tn%d_%d" % (tb, kc)])
                P.act(C.hT[:, kc, tb * 512:(tb + 1) * 512], tn[tb][:, kc, :], AF.Identity, ["tn%d_%d" % (tb, kc)],
                      ["hT%d" % tb], bias=B[:, kc, j:j + 1])


def ph_proj(C, l, g):
    nc = C.nc
    with Phase(nc, "prj") as P:
        if g == 0:
            for i_, (src_, dst_) in enumerate(((C.payA, C.gatA), (C.payB, C.gatB), (C.payC, C.gatC))):
                P.S.cc(lambda e, src_=src_, dst_=dst_: e.collective_compute(
                    "AllGather", ALU.bypass, replica_groups=GROUPS4, ins=[src_.ap().opt()], outs=[dst_.ap().opt()]), key="cc%d" % i_)
        wb = [P.sb([128, 8, 512], BF16, "wb") for _ in range(2)]
        ps = [P.ps([128, 512]) for _ in range(8)]
        sq = [P.sb([128, 512], BF16, "sq") for _ in range(2)]
        rs = [P.sb([128, 512], F32, "rs") for _ in range(2)]
        qn = [P.sb([128, 512], F32, "qn") for _ in range(2)]
        qb = [P.sb([128, 512], BF16, "qb") for _ in range(2)]
        o1 = [P.sb([128, 512], F32, "o1") for _ in range(2)]
        stg = [P.sb([128, 512], BF16, "stg") for _ in range(4)]
        sig = P.sb([128, 2, 1024], BF16, "sig")
        vf = [P.sb([128, 512], F32, "vf") for _ in range(2)]
        tokf = [P.sb([128, 4, 128], F32, "tokf") for _ in range(2)]
        tokb = [P.sb([128, 4, 128], BF16, "tokb") for _ in range(2)]
        eps = ppa(C, "eps")
        if g == 1:
            rc = P.sb([128, 1024], F32, "rc")
            rsn = P.sb([128, 1024], F32, "rsn")
            P.dma("sp", rc[:], C.inp("ropec"), key="rc", writes=["rc"])
            P.dma("sp", rsn[:], C.inp("ropes"), key="rsn", writes=["rsn"])
        W = C.inp("w_in")[l]
        cnt = dict(ps=0, stg=0, w=0, k=0)

        def load_block(cols, permq=False):
            s = cnt["w"] % 2
            cnt["w"] += 1
            key = "wb%d" % s
            if permq:
                for hh in range(2):
                    for c in range(4):
                        h0 = (hh * 4 + c) * 64
                        src = W[:, h0:h0 + 64].rearrange("(kc p) d -> p kc d", p=128)
                        dst = wb[s][:, :, c * 128 + hh * 64:c * 128 + hh * 64 + 64]
                        P.dma("pool", dst, src, key=key, writes=[key])
            else:
                n = cols[1] - cols[0]
                src = W[:, cols[0]:cols[1]].rearrange("(kc p) n -> p kc n", p=128)
                P.dma("pool", wb[s][:, :, 0:n], src, key=key, writes=[key])
            return s

        def matmuls(s, cl):
            res = []
            for tb in range(2):
                pi = cnt["ps"] % 4
                cnt["ps"] += 1
                for kc in range(8):
                    P.mm(ps[pi][:], wb[s][:, kc, cl * 128:(cl + 1) * 128], C.hT[:, kc, tb * 512:(tb + 1) * 512],
                         kc == 0, kc == 7, ["wb%d" % s, "hT%d" % tb], ["ps%d" % pi])
                res.append(pi)
            return res

        def stage_out(src_psum_or_none, dst_dram, producer):
            si = cnt["stg"] % 4
            cnt["stg"] += 1
            producer(stg[si][:], "stg%d" % si)
            P.dma("sp", dst_dram, stg[si][:], key="stg%d" % si, reads=["stg%d" % si])

        def headnorm(pi, tb, wname, rope, dst_dram, fp32_out=None):
            k = cnt["k"] % 2
            cnt["k"] += 1
            pk = "ps%d" % pi
            P.act(sq[k][:], ps[pi][:], AF.Square, [pk], ["sq%d" % k])
            p2 = 4 + (cnt["ps"] % 2)
            P.mm(ps[p2][:], C.blk64[:], sq[k][:], True, True, ["sq%d" % k], ["ps%d" % p2])
            P.rsqrt(rs[k][:], ps[p2][:], ["ps%d" % p2], ["rs%d" % k], eps)
            w_ap = ppa(C, wname)
            if not rope:
                if fp32_out is not None:
                    P.stt(fp32_out, ps[pi][:], w_ap, rs[k][:], ALU.mult, ALU.mult, [pk, "rs%d" % k], ["qn%d" % k])
                    stage_out(None, dst_dram, lambda ap, key: P.cp("act", ap, fp32_out, ["qn%d" % k], [key]))
                else:
                    stage_out(None, dst_dram, lambda ap, key: P.stt(ap, ps[pi][:], w_ap, rs[k][:], ALU.mult, ALU.mult,
                                                                      [pk, "rs%d" % k], [key]))
            else:
                P.stt(qn[k][:], ps[pi][:], w_ap, rs[k][:], ALU.mult, ALU.mult, [pk, "rs%d" % k], ["qn%d" % k])
                P.cp("act", qb[k][:], qn[k][:], ["qn%d" % k], ["qb%d" % k])
                p3 = 6 + (cnt["ps"] % 2)
                P.mm(ps[p3][:], C.ropeRT[:], qb[k][:], True, True, ["qb%d" % k], ["ps%d" % p3])
                P.tt("pool", o1[k][:], qn[k][:], rc[:, tb * 512:(tb + 1) * 512], ALU.mult, ["qn%d" % k, "rc"], ["o1%d" % k])
                P.tt("dve", qn[k][:], ps[p3][:], rsn[:, tb * 512:(tb + 1) * 512], ALU.mult, ["ps%d" % p3, "rsn"], ["qn%d" % k])
                stage_out(None, dst_dram, lambda ap, key: P.tt("dve", ap, qn[k][:], o1[k][:], ALU.add,
                                                                 ["qn%d" % k, "o1%d" % k], [key]))

        def to_tokmajor(src_f32, srckey, tb, out_dram_f32, out_dram_bf, tag="v"):
            k = cnt["k"] % 2
            cnt["k"] += 1
            p3 = 6 + k
            for i in range(4):
                P.tr(ps[p3][:, i * 128:(i + 1) * 128], src_f32[:, i * 128:(i + 1) * 128], C.ident_f[:], [srckey], ["ps%d" % p3])
            pv = ps[p3][:].rearrange("p (i f) -> p i f", i=4)
            if out_dram_f32 is not None and os.environ.get("SKIP_NK", "") != tag:
                P.cp("act", tokf[k][:], pv, ["ps%d" % p3], ["tokf%d" % k])
                for si, dd in enumerate(out_dram_f32):
                    P.dma("sp", dd, tokf[k][:, 2 * si:2 * si + 2, :], key="tokf%d" % k, reads=["tokf%d" % k])
            if out_dram_bf is not None:
                P.cp("dve", tokb[k][:], pv, ["ps%d" % p3], ["tokb%d" % k])
                P.dma("sp", out_dram_bf, tokb[k][:], key="tokb%d" % k, reads=["tokb%d" % k])

        s = load_block(None, permq=True)
        s_nxt = load_block((1024, 1536))
        for c in range(4):
            pis = matmuls(s, c)
            for tb, pi in enumerate(pis):
                headnorm(pi, tb, "qw_%d" % l, g == 1, C.qT_d[g][c * 128:(c + 1) * 128, tb * 512:(tb + 1) * 512])
        s = s_nxt
        s_nxt = load_block((512, 1024))
        for cl in range(4):
            pis = matmuls(s, cl)
            for tb, pi in enumerate(pis):
                pk = "ps%d" % pi
                if cl < 2:
                    P.act(sig[:, cl, tb * 512:(tb + 1) * 512], ps[pi][:], AF.Sigmoid, [pk], ["sig"])
                else:
                    zc = cl - 2
                    if g == 0:
                        dst = C.zhT_d[zc * 128:(zc + 1) * 128, tb * 512:(tb + 1) * 512]
                    else:
                        dst = C.pay_zh(zc)[:, tb * 512:(tb + 1) * 512]
                    stage_out(None, dst, lambda ap, key, pi=pi, pk=pk: P.cp("act", ap, ps[pi][:], [pk], [key]))
        s = s_nxt
        s_nxt = load_block((1536, 2048))
        for cl in range(4):
            pis = matmuls(s, cl)
            for tb, pi in enumerate(pis):
                pk = "ps%d" % pi
                if cl == 0:
                    if g == 0:
                        kk = cnt["k"] % 2
                        headnorm(pi, tb, "kw_%d" % l, False, C.kT_d[:, tb * 512:(tb + 1) * 512], fp32_out=qn[kk][:])
                        for sq_ in range(2):
                            pass
                        dst = [C.new_k[2 * tb + si, l].rearrange("(i p) f -> p i f", p=128) for si in range(2)]
                        to_tokmajor(qn[kk], "qn%d" % kk, tb, dst, None, tag="k")
                    else:
                        headnorm(pi, tb, "kw_%d" % l, True, C.pay_kT[:, tb * 512:(tb + 1) * 512])
                elif cl == 1:
                    kk = cnt["k"] % 2
                    P.cp("act", vf[kk][:], ps[pi][:], [pk], ["vf%d" % kk])
                    if g == 0:
                        dstf = [C.new_v[2 * tb + si, l].rearrange("(i p) f -> p i f", p=128) for si in range(2)]
                        dstb = C.vtok_d[tb * 512:(tb + 1) * 512, :].rearrange("(i p) f -> p i f", p=128)
                        to_tokmajor(vf[kk], "vf%d" % kk, tb, dstf, dstb)
                    else:
                        dstb = C.pay_V[tb * 512:(tb + 1) * 512, :].rearrange("(i p) f -> p i f", p=128)
                        to_tokmajor(vf[kk], "vf%d" % kk, tb, None, dstb)
                else:
                    cc = cl - 2
                    dst = C.uT_d[g][cc * 128:(cc + 1) * 128, tb * 512:(tb + 1) * 512]
                    si_ = cnt["stg"] % 4
                    stage_out(None, dst, lambda ap, key, pi=pi, pk=pk, cc=cc, tb=tb: P.tt(
                        "dve", ap, ps[pi][:], sig[:, cc, tb * 512:(tb + 1) * 512], ALU.mult, [pk, "sig"], [key]))
                    if g == 1:
                        ub = C.pay_ub
                        if tb == 0:
                            P.dma("sp", ub[cc * 128:(cc + 1) * 128, 0:16], stg[si_][:, 0:16], key="stg%d" % si_, reads=["stg%d" % si_])
                        else:
                            P.dma("sp", ub[cc * 128:(cc + 1) * 128, 16:32], stg[si_][:, 496:512], key="stg%d" % si_, reads=["stg%d" % si_])
        s = s_nxt
        for cl in range(4):
            pis = matmuls(s, cl)
            for tb, pi in enumerate(pis):
                pk = "ps%d" % pi
                zc = 2 + cl
                if g == 0:
                    dst = C.zhT_d[zc * 128:(zc + 1) * 128, tb * 512:(tb + 1) * 512]
                else:
                    dst = C.pay_zh(zc)[:, tb * 512:(tb + 1) * 512]
                stage_out(None, dst, lambda ap, key, pi=pi, pk=pk: P.cp("act", ap, ps[pi][:], [pk], [key]))


GROUPS4 = [[0, 1, 2, 3], [4, 5, 6, 7]]


def ph_ag(C, src, dst, name):
    with Phase(C.nc, name) as P:
        P.S.cc(lambda e: e.collective_compute("AllGather", ALU.bypass, replica_groups=GROUPS4,
                                              ins=[src.ap().opt()], outs=[dst.ap().opt()]), key="cc")


def ph_attn(C, l, g, heads=range(8), with_ag=False):
    nc = C.nc
    with Phase(nc, "att") as P:
        nkt = 8 if g == 0 else 34
        psS = [P.ps([128, 512]) for _ in range(6)]
        psO = [P.ps([128, 512]) for _ in range(2)]
        if with_ag:
            for i_, (src_, dst_) in enumerate(((C.payA, C.gatA), (C.payB, C.gatB), (C.payC, C.gatC))):
                P.S.cc(lambda e, src_=src_, dst_=dst_: e.collective_compute(
                    "AllGather", ALU.bypass, replica_groups=GROUPS4, ins=[src_.ap().opt()], outs=[dst_.ap().opt()]), key="cc%d" % i_)
        qT = P.sb([128, 4, 1024], BF16, "qT")
        qz = [P.sb([128, 4, 1024], BF16, "qz") for _ in range(2)]
        kT = P.sb([128, nkt * 128], BF16, "kT")
        va = P.sb([128, nkt, 2, 128], BF16, "va")
        P.dma("sp", qT[:], C.qT_d[g].rearrange("(c p) t -> p c t", p=128), key="qT", writes=["qT"])
        P.memset("dve", qz[0][64:128].rearrange("p a b -> p (a b)"), 0.0, ["qz0"])
        P.memset("dve", qz[1][0:64].rearrange("p a b -> p (a b)"), 0.0, ["qz1"])
        P.cp("dve", qz[0][0:64], qT[0:64], ["qT"], ["qz0b"])
        P.cp("act", qz[1][64:128], qT[64:128], ["qT"], ["qz1b"])
        for t0_ in range(0, nkt, 8):
            t1_ = min(nkt, t0_ + 8)
            P.memset("dve", va[:, t0_:t1_].rearrange("p a b c -> p (a b c)"), 1.0, ["va"])
        if g == 0:
            P.dma("sp", kT[:], C.kT_d, key="kT", writes=["kT"])
            for kv in range(2):
                P.dma("sp", va[:, :, kv, 0:64], C.vtok_d[:, kv * 64:(kv + 1) * 64].rearrange("(i p) d -> p i d", p=128),
                      key="va", writes=["va"])
        else:
            for r in range(4):
                P.dma("sp", kT[:, r * 1024:(r + 1) * 1024], C.gat_kT(r), key="kT", writes=["kT"])
            for r in range(4):
                vreg = C.gat_V(r)
                for kv in range(2):
                    P.dma("sp", va[:, r * 8:(r + 1) * 8, kv, 0:64],
                          vreg[:, kv * 64:(kv + 1) * 64].rearrange("(i p) d -> p i d", p=128), key="va", writes=["va"])
            ck = P.sb([128, 2, 128], F32, "ck")
            P.dma("sp", ck[:], C.inp("cache_k")[l].rearrange("(i p) f -> p i f", p=128), key="ck", writes=["ck"])
            pck = psO[0]
            for i in range(2):
                P.tr(pck[:, i * 128:(i + 1) * 128], ck[:, i, :], C.ident_f[:], ["ck"], ["psO0"])
            P.cp("dve", kT[:, 4096:4352], pck[:, 0:256], ["psO0"], ["kT"])
            for kv in range(2):
                P.dma("pool", va[:, 32:34, kv, 0:64],
                      C.inp("cache_v")[l][:, kv * 64:(kv + 1) * 64].rearrange("(i p) d -> p i d", p=128), key="va2", writes=["va"])
        pT = [P.sb([128, 512], BF16, "pT") for _ in range(6)]
        dn = [P.sb([64, 512], F32, "dn") for _ in range(2)]
        it = dict(s=0, o=0, p=0, b=0)

        NB = 3

        def one(h, qsl, n, ktiles, dst):
            c, kvh = h % 4, h // 4
            r0 = kvh * 64
            oi = it["o"] % 2
            it["o"] += 1
            batches = [ktiles[i:i + NB] for i in range(0, len(ktiles), NB)]

            def emit_s_batch(bi):
                res = []
                par = it["b"] % 2
                it["b"] += 1
                for kt in batches[bi]:
                    si = it["s"] % 6
                    it["s"] += 1
                    pi = it["p"] % 6
                    it["p"] += 1
                    P.mm(psS[si][:, 0:n], kT[:, kt * 128:(kt + 1) * 128], qz[kvh][:, c, qsl], True, True,
                         ["kT", "qz0", "qz1", "qz0b", "qz1b"], ["psS%d" % si])
                    res.append((si, pi, kt))
                for (si, pi, kt) in res:
                    P.act(pT[pi][:, 0:n], psS[si][:, 0:n], AF.Exp, ["psS%d" % si], ["pT%d" % pi, "pTb%d" % par], scale=0.125)
                return res, par
            pend = emit_s_batch(0)
            for bi in range(len(batches)):
                nxt = emit_s_batch(bi + 1) if bi + 1 < len(batches) else None
                res, par = pend
                for j, (si, pi, kt) in enumerate(res):
                    first = (bi == 0 and j == 0)
                    last = (bi == len(batches) - 1 and j == len(res) - 1)
                    P.mm(psO[oi][:, 0:n], va[:, kt, kvh, :], pT[pi][:, 0:n], first, last,
                         ["va", "pT%d" % pi, "pTb%d" % par], ["psO%d" % oi])
                pend = nxt
            P.cp("dve", dn[oi][:, 0:n], psO[oi][64:128, 0:n], ["psO%d" % oi], ["dn%d" % oi])
            P.act(dn[oi][:, 0:n], dn[oi][:, 0:n], AF.Ln, ["dn%d" % oi], ["dn%d" % oi])
            P.act(dn[oi][:, 0:n], dn[oi][:, 0:n], AF.Exp, ["dn%d" % oi], ["dn%d" % oi], scale=-1.0)
            P.tt("dve", dst, psO[oi][0:64, 0:n], dn[oi][:, 0:n], ALU.mult, ["psO%d" % oi, "dn%d" % oi], ["mixA"])

        for h in heads:
            c, kvh = h % 4, h // 4
            r0 = kvh * 64
            if g == 0:
                for sq_ in range(4):
                    one(h, slice(sq_ * 256, (sq_ + 1) * 256), 256, [sq_ * 2, sq_ * 2 + 1],
                        C.mixT[r0:r0 + 64, c, sq_ * 256:(sq_ + 1) * 256])
            else:
                for qb in range(2):
                    one(h, slice(qb * 512, (qb + 1) * 512), 512, list(range(34)),
                        C.mixT[r0:r0 + 64, c, qb * 512:(qb + 1) * 512])


def ph_conv(C, l, g):
    nc = C.nc
    with Phase(nc, "cnv") as P:
        o, n = C.ppoff["cdw_%d" % l]
        dg = P.sb([128, 2, 31, 128], BF16, "dg")
        for cc in range(2):
            for k in range(31):
                P.ts("dve", dg[:, cc, k, :], C.ident_b[:], C.pp[:, o + cc * 31 + k:o + cc * 31 + k + 1], ALU.mult, [], ["dg"])
        cdb = ppa(C, "cdb_%d" % l); gnw = ppa(C, "gnw_%d" % l); gnb = ppa(C, "gnb_%d" % l); pwb = ppa(C, "pwb_%d" % l)
        eps = ppa(C, "eps")
        pww = P.sb([128, 2, 256], BF16, "pww")
        P.dma("pool", pww[:], C.inp("conv_pw_w")[l].rearrange("(kc p) n -> p kc n", p=128), key="pww", writes=["pww"])
        if g == 0:
            up = P.sb([128, 2, 4, 286], BF16, "up")
            P.memset("pool", up[:], 0.0, ["up"])
            for cc in range(2):
                P.dma("sp", up[:, cc, :, 15:271], C.uT_d[0][cc * 128:(cc + 1) * 128, :].rearrange("p (s t) -> p s t", s=4),
                      key="up", writes=["up"])
        else:
            up = P.sb([128, 2, 1054], BF16, "up")
            P.memset("pool", up[:], 0.0, ["up"])
            for cc in range(2):
                P.dma("sp", up[:, cc, 15:1039], C.uT_d[1][cc * 128:(cc + 1) * 128, :], key="up", writes=["up"])
            ubg = P.sb([128, 2, 4, 32], BF16, "ubg")
            for r in range(4):
                ub = C.gat_ub(r)
                for cc in range(2):
                    P.dma("sp", ubg[:, cc, r, :], ub[cc * 128:(cc + 1) * 128, :], key="ubg", writes=["ubg"])
            el = ppa(C, "el"); er = ppa(C, "er")
            for r in range(4):
                for (dst, src, ee) in ((up[:, :, 0:15], ubg[:, :, r, 17:32], el), (up[:, :, 1039:1054], ubg[:, :, r, 0:15], er)):
                    P.stt(dst, src, ee[:, r:r + 1], dst, ALU.mult, ALU.add, ["ubg", "up"], ["up"])
        sbf = P.sb([128, 2, 1024], BF16, "sbf")
        pc = [P.ps([128, 512]) for _ in range(2)]
        pm = [P.ps([128, 512]) for _ in range(4)]
        yf = [P.sb([128, 512], F32, "yf") for _ in range(2)]
        yb = [P.sb([128, 512], BF16, "yb") for _ in range(2)]
        ysq = [P.sb([128, 512], BF16, "ysq") for _ in range(2)]
        msq = [P.sb([128, 512], F32, "msq") for _ in range(2)]
        var = [P.sb([128, 512], F32, "var") for _ in range(2)]
        n_ = 0
        for cc in range(2):
            for blk in range(2):
                b = n_ % 2
                n_ += 1
                if g == 0:
                    for half in range(2):
                        sq_ = blk * 2 + half
                        for k in range(31):
                            P.mm(pc[b][:, half * 256:(half + 1) * 256], dg[:, cc, k, :], up[:, cc, sq_, k:k + 256], k == 0, k == 30,
                                 ["dg", "up"], ["psc%d" % b])
                else:
                    for k in range(31):
                        P.mm(pc[b][:], dg[:, cc, k, :], up[:, cc, blk * 512 + k:blk * 512 + k + 512], k == 0, k == 30,
                             ["dg", "up"], ["psc%d" % b])
                P.act(yf[b][:], pc[b][:], AF.Identity, ["psc%d" % b], ["yf%d" % b], bias=cdb[:, cc:cc + 1])
                P.cp("dve", yb[b][:], yf[b][:], ["yf%d" % b], ["yb%d" % b])
                P.act(ysq[b][:], yf[b][:], AF.Square, ["yf%d" % b], ["ysq%d" % b])
                P.mm(pm[2 * b][:], C.blk64[:], yb[b][:], True, True, ["yb%d" % b], ["psm%d" % (2 * b)])
                P.mm(pm[2 * b + 1][:], C.blk64[:], ysq[b][:], True, True, ["ysq%d" % b], ["psm%d" % (2 * b + 1)])
                P.act(msq[b][:], pm[2 * b][:], AF.Square, ["psm%d" % (2 * b)], ["msq%d" % b])
                P.tt("dve", var[b][:], pm[2 * b + 1][:], msq[b][:], ALU.subtract, ["psm%d" % (2 * b + 1), "msq%d" % b], ["var%d" % b])
                P.rsqrt(var[b][:], var[b][:], ["var%d" % b], ["var%d" % b], eps)
                P.tt("dve", yf[b][:], yf[b][:], pm[2 * b][:], ALU.subtract, ["yf%d" % b, "psm%d" % (2 * b)], ["yf%d" % b])
                P.tt("pool", yf[b][:], yf[b][:], var[b][:], ALU.mult, ["yf%d" % b, "var%d" % b], ["yf%d" % b])
                P.act(sbf[:, cc, blk * 512:(blk + 1) * 512], yf[b][:], AF.Silu, ["yf%d" % b], ["sbf"],
                      scale=gnw[:, cc:cc + 1], bias=gnb[:, cc:cc + 1])
        for m in range(2):
            for blk in range(2):
                b = n_ % 2
                n_ += 1
                for cc in range(2):
                    P.mm(pc[b][:], pww[:, cc, m * 128:(m + 1) * 128], sbf[:, cc, blk * 512:(blk + 1) * 512], cc == 0, cc == 1,
                         ["pww", "sbf"], ["psc%d" % b])
                P.act(C.mixT[:, 4 + m, blk * 512:(blk + 1) * 512], pc[b][:], AF.Identity, ["psc%d" % b], ["mixC"],
                      bias=pwb[:, m:m + 1])


def ph_wo(C, l, g, zero_hy=False, with_norm2=True):
    nc = C.nc
    j = g
    with Phase(nc, "wo") as P:
        wo = P.sb([128, 8, 1024], BF16, "wo")
        W = C.inp("w_o")[l]
        for cb in range(2):
            cs_ = slice(cb * 512, (cb + 1) * 512)
            for c in range(4):
                for hh in range(2):
                    P.dma("pool", wo[hh * 64:(hh + 1) * 64, c, cs_], W[(hh * 4 + c) * 64:(hh * 4 + c) * 64 + 64, cs_],
                          key="wo%d" % cb, writes=["wo%d" % cb])
            P.dma("pool", wo[:, 4:8, cs_], W[512:1024, cs_].rearrange("(kc p) n -> p kc n", p=128), key="wo%d" % cb, writes=["wo%d" % cb])
        if zero_hy:
            P.memset("dve", C.mixT[:, 6:8, :], 0.0, ["mix"])
        ps = [P.ps([128, 512]) for _ in range(4)]
        sq = [P.sb([128, 8, 512], BF16, "sq") for _ in range(2)]
        tn = [P.sb([128, 8, 512], F32, "tn") for _ in range(2)]
        rs = [P.sb([128, 512], F32, "rs") for _ in range(2)]
        pss = [P.ps([128, 512]) for _ in range(2)]
        eps = ppa(C, "eps")
        n_ = 0
        for tb in range(2):
            t0 = g * 1024 + tb * 512
            xk = "xT%d" % (t0 // 512)
            for m in range(8):
                b = n_ % 4
                n_ += 1
                for kc in range(8):
                    P.mm(ps[b][:], wo[:, kc, m * 128:(m + 1) * 128], C.mixT[:, kc, tb * 512:(tb + 1) * 512], kc == 0, kc == 7,
                         ["wo%d" % (m // 4), "mix"], ["ps%d" % b])
                xs = C.xT[:, m, t0:t0 + 512]
                P.stt(xs, ps[b][:], C.modT[:, l, 16 + m, j:j + 1], xs, ALU.mult, ALU.add, ["ps%d" % b], [xk])
            if with_norm2:
                P.act(sq[tb][:], C.xT[:, :, t0:t0 + 512], AF.Square, [xk], ["sq%d" % tb])
                for kc in range(8):
                    P.mm(pss[tb][:], C.onesm[:], sq[tb][:, kc, :], kc == 0, kc == 7, ["sq%d" % tb], ["pss%d" % tb])
                P.rsqrt(rs[tb][:], pss[tb][:], ["pss%d" % tb], ["rs%d" % tb], eps)
                for kc in range(8):
                    P.stt(tn[tb][:, kc, :], C.xT[:, kc, t0:t0 + 512], C.A2[:, l, kc, j:j + 1], rs[tb][:], ALU.mult, ALU.mult,
                          [xk, "rs%d" % tb], ["tn%d_%d" % (tb, kc)])
                    P.act(C.hT[:, kc, tb * 512:(tb + 1) * 512], tn[tb][:, kc, :], AF.Identity, ["tn%d_%d" % (tb, kc)],
                          ["hT%d" % tb], bias=C.modT[:, l, 24 + kc, j:j + 1])


def ph_halo3(C, l):
    nc = C.nc
    with Phase(nc, "h3a") as P:
        hb = P.sb([128, 1024], BF16, "hb")
        P.memset("dve", hb[:], 0.0, ["hb"])
        P.cp("dve", hb[:, 0:8], C.hT[:, :, 0:1].rearrange("p k e -> p (k e)"), ["hb"], ["hb"])
        P.cp("dve", hb[:, 8:16], C.hT[:, :, 1023:1024].rearrange("p k e -> p (k e)"), ["hb"], ["hb"])
        P.dma("sp", C.pay3.ap(), hb[:], key="hb", reads=["hb"])
    ph_ag(C, C.pay3, C.gat3, "ag3")
    with Phase(nc, "h3b") as P:
        hbg = P.sb([128, 4, 16], F32, "hbg")
        for r in range(4):
            P.dma("pool", hbg[:, r, :], C.gat3.ap()[r * 128:(r + 1) * 128, 0:16], key="hbg%d" % r, writes=["hbg"])
        el = ppa(C, "el"); er = ppa(C, "er")
        acc = P.sb([128, 2, 8], F32, "acc")
        P.memset("dve", acc[:], 0.0, ["acc"])
        for r in range(4):
            P.stt(acc[:, 0, :], hbg[:, r, 8:16], el[:, r:r + 1], acc[:, 0, :], ALU.mult, ALU.add, ["hbg", "acc"], ["acc"])
            P.stt(acc[:, 1, :], hbg[:, r, 0:8], er[:, r:r + 1], acc[:, 1, :], ALU.mult, ALU.add, ["hbg", "acc"], ["acc"])
        P.cp("dve", C.hT[:, :, 1024:1025].rearrange("p k e -> p (k e)"), acc[:, 0, :], ["acc"], ["hTh"])
        P.cp("dve", C.hT[:, :, 1025:1026].rearrange("p k e -> p (k e)"), acc[:, 1, :], ["acc"], ["hTh"])


def ph_ffn(C, l, g):
    nc = C.nc
    j = g
    with Phase(nc, "ffn") as P:
        o_w, _ = C.ppoff["fdw_%d" % l]
        o_b, _ = C.ppoff["fdb_%d" % l]
        wsc = lambda ch, k: C.pp[:, o_w + ch * 3 + k:o_w + ch * 3 + k + 1]
        bsc = lambda ch: C.pp[:, o_b + ch:o_b + ch + 1]
        Wu = C.inp("ffn_w_up")[l]
        Wd = C.inp("ffn_w_down")[l]
        gT = P.sb([128, 11, 1024], BF16, "gT")
        wdn = P.sb([128, 11, 1024], BF16, "wdn")
        wupV = [P.sb([128, 8, 512], BF16, "wupV") for _ in range(2)]
        wupG = [P.sb([128, 8, 512], BF16, "wupG") for _ in range(2)]
        cv = [P.sb([128, 1024], F32, "cv") for _ in range(2)]
        cg = [P.sb([128, 1024], F32, "cg") for _ in range(2)]
        sg = [P.sb([128, 1024], F32, "sg") for _ in range(2)]
        pt = [[P.ps([128, 512]) for _ in range(2)] for _ in range(3)]
        ph = [P.ps([128, 512]) for _ in range(2)]
        rot = 0
        for half in range(2):
            for jj in range(11):
                jf = half * 11 + jj
                s = jf % 2
                if jj == 1:
                    P.dma("pool", wdn[:], Wd[half * 1408:(half + 1) * 1408, :].rearrange("(j p) n -> p j n", p=128), key="wdn",
                          writes=["wdn"])
                if jj in (0, 4, 8):
                    bi = half * 3 + jj // 4
                    wb_ = bi % 2

                    def issue_block(bk):
                        h_, q_ = bk // 3, bk % 3
                        j0 = h_ * 11 + q_ * 4
                        nj = 4 if q_ < 2 else 3
                        sl_ = bk % 2
                        P.dma("pool", wupV[sl_][:, :, 0:nj * 128], Wu[:, j0 * 128:(j0 + nj) * 128].rearrange("(kc p) n -> p kc n", p=128),
                              key="wupV%d" % sl_, writes=["wup%d" % sl_])
                        P.dma("pool", wupG[sl_][:, :, 0:nj * 128],
                              Wu[:, FFN + j0 * 128:FFN + (j0 + nj) * 128].rearrange("(kc p) n -> p kc n", p=128),
                              key="wupG%d" % sl_, writes=["wup%d" % sl_])
                    if bi == 0:
                        issue_block(0)
                    if bi + 1 < 6:
                        issue_block(bi + 1)
                wcol = (jj % 4) * 128 if jj < 8 else (jj - 8) * 128
                wkey = "wup%d" % wb_
                tv = rot % 3
                tg = (rot + 1) % 3
                rot += 2
                for (tt_, wsrc) in ((tv, wupV[wb_]), (tg, wupG[wb_])):
                    for tb in range(2):
                        for kc in range(8):
                            P.mm(pt[tt_][tb][:], wsrc[:, kc, wcol:wcol + 128], C.hT[:, kc, tb * 512:(tb + 1) * 512],
                                 kc == 0, kc == 7, [wkey, "hT0", "hT1"], ["pst%d" % tt_])
                if g == 1:
                    for (wsrc, col) in ((wupV[wb_], 0), (wupG[wb_], 2)):
                        for kc in range(8):
                            P.mm(ph[s][:, col:col + 2], wsrc[:, kc, wcol:wcol + 128], C.hT[:, kc, 1024:1026], kc == 0, kc == 7,
                                 [wkey, "hTh"], ["psh%d" % s])
                for (tt_, cbuf, ch, hc) in ((tv, cv[s], jf, 0), (tg, cg[s], NJ + jf, 2)):
                    ck = "c%d_%d" % (hc, s)
                    pk = "pst%d" % tt_
                    for tb in range(2):
                        cb_ = cbuf[:, tb * 512:(tb + 1) * 512]
                        pb_ = pt[tt_][tb][:]
                        P.act(cb_, pb_, AF.Identity, [pk], [ck], scale=wsc(ch, 1), bias=bsc(ch))
                        if g == 0:
                            c3 = cb_.rearrange("p (s t) -> p s t", s=2)
                            p3 = pb_.rearrange("p (s t) -> p s t", s=2)
                            P.stt(c3[:, :, 1:256], p3[:, :, 0:255], wsc(ch, 0), c3[:, :, 1:256], ALU.mult, ALU.add, [pk, ck], [ck])
                            P.stt(c3[:, :, 0:255], p3[:, :, 1:256], wsc(ch, 2), c3[:, :, 0:255], ALU.mult, ALU.add, [pk, ck], [ck])
                        else:
                            P.stt(cb_[:, 1:512], pb_[:, 0:511], wsc(ch, 0), cb_[:, 1:512], ALU.mult, ALU.add, [pk, ck], [ck])
                            P.stt(cb_[:, 0:511], pb_[:, 1:512], wsc(ch, 2), cb_[:, 0:511], ALU.mult, ALU.add, [pk, ck], [ck])
                    if g == 1:
                        P.stt(cbuf[:, 512:513], pt[tt_][0][:, 511:512], wsc(ch, 0), cbuf[:, 512:513], ALU.mult, ALU.add, [pk, ck], [ck])
                        P.stt(cbuf[:, 511:512], pt[tt_][1][:, 0:1], wsc(ch, 2), cbuf[:, 511:512], ALU.mult, ALU.add, [pk, ck], [ck])
                        P.stt(cbuf[:, 0:1], ph[s][:, hc:hc + 1], wsc(ch, 0), cbuf[:, 0:1], ALU.mult, ALU.add, ["psh%d" % s, ck], [ck])
                        P.stt(cbuf[:, 1023:1024], ph[s][:, hc + 1:hc + 2], wsc(ch, 2), cbuf[:, 1023:1024], ALU.mult, ALU.add,
                              ["psh%d" % s, ck], [ck])
                P.act(sg[s][:], cg[s][:], AF.Silu, ["c2_%d" % s], ["sg%d" % s])
                P.tt("dve", gT[:, jj, :], cv[s][:], sg[s][:], ALU.mult, ["c0_%d" % s, "sg%d" % s], ["gT"])
            n_ = 0
            for m in range(8):
                for tb in range(2):
                    tt_ = n_ % 3
                    n_ += 1
                    for jj in range(11):
                        P.mm(pt[tt_][0][:], wdn[:, jj, m * 128:(m + 1) * 128], gT[:, jj, tb * 512:(tb + 1) * 512], jj == 0, jj == 10,
                             ["wdn", "gT"], ["pst%d" % tt_])
                    t0 = g * 1024 + tb * 512
                    xs = C.xT[:, m, t0:t0 + 512]
                    P.stt(xs, pt[tt_][0][:], C.modT[:, l, 40 + m, j:j + 1], xs, ALU.mult, ALU.add, ["pst%d" % tt_],
                          ["xT%d" % (t0 // 512)])


def ph_final(C):
    nc = C.nc
    with Phase(nc, "fin") as P:
        sq = [P.sb([128, 8, 512], BF16, "sq") for _ in range(2)]
        yT = [P.sb([128, 8, 512], F32, "yT") for _ in range(2)]
        rs = [P.sb([128, 512], F32, "rs") for _ in range(2)]
        yo = [P.sb([128, 1024], F32, "yo") for _ in range(2)]
        pss = [P.ps([128, 512]) for _ in range(2)]
        ptr = [P.ps([128, 512]) for _ in range(4)]
        eps = ppa(C, "eps")
        fn = ppa(C, "fnorm")
        n_ = 0
        for tb4 in range(4):
            b = tb4 % 2
            t0 = tb4 * 512
            xk = "xT%d" % tb4
            P.act(sq[b][:], C.xT[:, :, t0:t0 + 512], AF.Square, [xk], ["sq%d" % b])
            for kc in range(8):
                P.mm(pss[b][:], C.onesm[:], sq[b][:, kc, :], kc == 0, kc == 7, ["sq%d" % b], ["pss%d" % b])
            P.rsqrt(rs[b][:], pss[b][:], ["pss%d" % b], ["rs%d" % b], eps)
            for kc in range(8):
                P.stt(yT[b][:, kc, :], C.xT[:, kc, t0:t0 + 512], fn[:, kc:kc + 1], rs[b][:], ALU.mult, ALU.mult,
                      [xk, "rs%d" % b], ["yT%d" % b])
            for ti in range(4):
                yb_ = n_ % 2
                for hh in range(2):
                    pb = n_ % 4 if False else (2 * yb_ + hh)
                    for kk in range(4):
                        kc = hh * 4 + kk
                        P.tr(ptr[pb][:, kk * 128:(kk + 1) * 128], yT[b][:, kc, ti * 128:(ti + 1) * 128], C.ident_f[:],
                             ["yT%d" % b], ["pstr%d" % pb])
                    if hh == 0:
                        P.cp("act", yo[yb_][:, 0:512], ptr[pb][:], ["pstr%d" % pb], ["yo%d" % yb_])
                    else:
                        P.cp("dve", yo[yb_][:, 512:1024], ptr[pb][:], ["pstr%d" % pb], ["yo%d" % yb_])
                r0 = t0 + ti * 128
                P.dma("sp", C.y_tok[r0:r0 + 128, :], yo[yb_][:], key="yo%d" % yb_, reads=["yo%d" % yb_])
                n_ += 1


TWO_PI = 2.0 * math.pi
MAGIC = 12582912.0


def filt_mlp(P, C, l, zT, L, hid2):
    w1 = P.sb([33, 64], F32, "w1")
    w2 = P.sb([64, 64], F32, "w2")
    P.dma("sp", w1[:], C.inp("hy_f_w1")[l], key="w1", writes=["w1"])
    P.dma("sp", w2[:], C.inp("hy_f_w2")[l], key="w2", writes=["w2"])
    fr = ppa(C, "hfr_%d" % l)[0:64, :]
    fb = P.sb([64, 2], F32, "fb")
    P.tt("dve", fb[:, 0:1], ppa(C, "hfb1_%d" % l)[0:64, :], fr, ALU.mult, [], ["fb"])
    P.tt("dve", fb[:, 1:2], ppa(C, "hfb2_%d" % l)[0:64, :], fr, ALU.mult, [], ["fb"])
    hid1 = P.sb([64, L], F32, "hid1")
    a = P.sb([64, 512], F32, "fa")
    t1 = P.sb([64, 512], F32, "ft1")
    pf = P.ps([128, 512])
    for (wt, src, dst, col) in ((w1, zT, hid1, 0), (w2, hid1, hid2, 1)):
        kk = 33 if col == 0 else 64
        for b0 in range(0, L, 512):
            n = min(512, L - b0)
            P.mm(pf[0:64, 0:n], wt[0:kk, :], src[0:kk, b0:b0 + n], True, True, ["w1", "w2", "zT", "hid1"], ["psf"])
            P.ts("dve", a[:, 0:n], pf[0:64, 0:n], fr, ALU.mult, ["psf", "fb"], ["fa"], s2=fb[:, col:col + 1], op1=ALU.add)
            P.ts("dve", t1[:, 0:n], a[:, 0:n], 1.0 / TWO_PI, ALU.mult, ["fa"], ["ft1"], s2=MAGIC, op1=ALU.add)
            P.ts("dve", t1[:, 0:n], t1[:, 0:n], MAGIC, ALU.subtract, ["ft1"], ["ft1"])
            P.stt(a[:, 0:n], t1[:, 0:n], -TWO_PI, a[:, 0:n], ALU.mult, ALU.add, ["ft1", "fa"], ["fa"])
            P.ts("dve", a[:, 0:n], a[:, 0:n], -3.14159, ALU.max, ["fa"], ["fa"], s2=3.14159, op1=ALU.min)
            P.act(dst[:, b0:b0 + n], a[:, 0:n], AF.Sin, ["fa"], ["hid1" if col == 0 else "hid2"])


def ph_hyena(C, l, g):
    if g == 0:
        ph_hyena_p(C, l)
    else:
        ph_hyena_s(C, l)


def ph_hyena_p(C, l):
    nc = C.nc
    with Phase(nc, "hyp") as P:
        dC = P.sb([128, 2, 512], BF16, "dC"); dS = P.sb([128, 2, 512], BF16, "dS"); dSn = P.sb([128, 2, 512], BF16, "dSn")
        iC = P.sb([128, 4, 256], BF16, "iC"); iSn = P.sb([128, 4, 256], BF16, "iSn")
        for t_, nm in ((dC, "dftC"), (dS, "dftS"), (dSn, "dftSn"), (iC, "idftC"), (iSn, "idftSn")):
            P.dma("sp", t_[:], C.inp(nm), key="cst", writes=["cst"])
        zT = P.sb([33, 256], F32, "zT")
        P.dma("sp", zT[:], C.inp("zT256"), key="zT", writes=["zT"])
        hid2 = P.sb([64, 256], F32, "hid2")
        filt_mlp(P, C, l, zT, 256, hid2)
        w3 = P.sb([64, 1024], F32, "w3")
        b3 = P.sb([1, 1024], F32, "b3")
        onesr = P.sb([1, 128], F32, "onesr")
        brow = P.sb([1, 512], F32, "brow")
        dec = P.sb([128, 2, 256], F32, "dec")
        P.dma("sp", w3[:], C.inp("hy_f_w3")[l], key="w3", writes=["w3"])
        P.dma("sp", b3[:], C.inp("hy_f_b3r")[l], key="b3", writes=["b3"])
        P.dma("sp", brow[:], C.inp("hy_biasr")[l], key="brow", writes=["brow"])
        P.dma("sp", dec[:], C.inp("decayP"), key="dec", writes=["dec"])
        P.memset("dve", onesr[:], 1.0, ["onesr"])
        hP = P.sb([128, 2, 1024], F32, "hP")
        pA = [P.ps([128, 512]) for _ in range(2)]
        for ti in range(2):
            for cb in range(2):
                pb = cb
                P.mm(pA[pb][:], hid2[:, ti * 128:(ti + 1) * 128], w3[:, cb * 512:(cb + 1) * 512], True, False, ["hid2", "w3"], ["psA%d" % pb])
                P.mm(pA[pb][:], onesr[:], b3[:, cb * 512:(cb + 1) * 512], False, True, ["onesr", "b3"], ["psA%d" % pb])
                for od in range(2):
                    P.tt("dve", hP[:, ti, cb * 512 + od * 256:cb * 512 + (od + 1) * 256], pA[pb][:, od * 256:(od + 1) * 256], dec[:, ti, :],
                         ALU.mult, ["psA%d" % pb, "dec"], ["hP"])
        hs = P.sb([128, 2, 512], BF16, "hs")
        hd = P.sb([128, 2, 512], BF16, "hd")
        for o in range(2):
            P.memset("dve", hP[0:1, 0, o * 512 + 256:o * 512 + 512], 0.0, ["hP"])
            P.tt("dve", hP[0:1, 0, o * 512:o * 512 + 256], hP[0:1, 0, o * 512:o * 512 + 256], brow[:, o * 256:(o + 1) * 256], ALU.add,
                 ["hP", "brow"], ["hP"])
        for o in range(2):
            fw = hP[:, :, o * 512:o * 512 + 256]
            bw = hP[:, :, o * 512 + 256:o * 512 + 512]
            P.tt("dve", hs[:, :, o * 256:(o + 1) * 256], fw, bw, ALU.add, ["hP"], ["hs"])
            P.tt("dve", hd[:, :, o * 256:(o + 1) * 256], bw, fw, ALU.subtract, ["hP"], ["hd"])
        Kr = P.sb([128, 4, 512], F32, "Kr")
        Ki = P.sb([128, 4, 512], F32, "Ki")
        for fch in range(4):
            for (dst, mat, src, nm) in ((Kr, dC, hs, "hs"), (Ki, dS, hd, "hd")):
                pb = fch % 2
                for ti in range(2):
                    P.mm(pA[pb][:], mat[:, ti, fch * 128:(fch + 1) * 128], src[:, ti, :], ti == 0, ti == 1, ["cst", nm], ["psA%d" % pb])
                P.cp("act", dst[:, fch, :], pA[pb][:], ["psA%d" % pb], ["K"])
        zh = P.sb([128, 6, 1024], BF16, "zh")
        P.dma("sp", zh[:], C.zhT_d.rearrange("(c p) t -> p c t", p=128), key="zh", writes=["zh"])
        o_w, _ = C.ppoff["hsw_%d" % l]
        o_b, _ = C.ppoff["hsb_%d" % l]
        wsc = lambda ch, k: C.pp[:, o_w + ch * 3 + k:o_w + ch * 3 + k + 1]
        zc = P.sb([128, 6, 1024], BF16, "zc")
        ztmp = [P.sb([128, 1024], F32, "ztmp") for _ in range(2)]
        for ch in range(6):
            b = ch % 2
            P.act(ztmp[b][:], zh[:, ch, :], AF.Identity, ["zh"], ["ztmp%d" % b], scale=wsc(ch, 1), bias=C.pp[:, o_b + ch:o_b + ch + 1])
            c3 = ztmp[b][:].rearrange("p (s t) -> p s t", s=4)
            z3 = zh[:, ch, :].rearrange("p (s t) -> p s t", s=4)
            P.stt(c3[:, :, 1:256], z3[:, :, 0:255], wsc(ch, 0), c3[:, :, 1:256], ALU.mult, ALU.add, ["zh", "ztmp%d" % b], ["ztmp%d" % b])
            P.stt(c3[:, :, 0:255], z3[:, :, 1:256], wsc(ch, 2), c3[:, :, 0:255], ALU.mult, ALU.add, ["zh", "ztmp%d" % b], ["ztmp%d" % b])
            P.cp("pool", zc[:, ch, :], ztmp[b][:], ["ztmp%d" % b], ["zc"])
        vtok = P.sb([128, 2, 4, 256], BF16, "vtok")
        Yr = P.sb([128, 4, 1024], BF16, "Yr")
        Yi = P.sb([128, 4, 1024], BF16, "Yi")
        u1 = P.sb([128, 2, 1024], BF16, "u1")
        ta = [P.sb([128, 512], F32, "ta") for _ in range(2)]
        tb_ = [P.sb([128, 512], F32, "tb") for _ in range(2)]
        pT = P.ps([128, 1024], BF16)
        pU = [P.ps([128, 512]) for _ in range(4)]
        for o in range(2):
            src = zc if o == 0 else None
            for cc in range(2):
                for sq_ in range(4):
                    for ti in range(2):
                        col = (sq_ * 2 + ti) * 128
                        inp_ = (zc[:, cc, col:col + 128] if o == 0 else u1[:, cc, col:col + 128])
                        P.tr(pT[:, (sq_ * 2 + ti) * 128:(sq_ * 2 + ti + 1) * 128], inp_, C.ident_b[:], ["zc", "u1"], ["psT"])
                pv = pT[:].rearrange("p (s t c) -> p t s c", s=4, t=2)
                P.cp("dve", vtok[:, :, :, cc * 128:(cc + 1) * 128], pv, ["psT"], ["vtok"])
            n_ = 0
            for fch in range(4):
                for cb in range(2):
                    pr, pi_ = pU[(n_ % 2) * 2], pU[(n_ % 2) * 2 + 1]
                    kr_, ki_ = "psU%d" % ((n_ % 2) * 2), "psU%d" % ((n_ % 2) * 2 + 1)
                    n_ += 1
                    rhs = lambda ti: vtok[:, ti, cb * 2:cb * 2 + 2, :].rearrange("p s c -> p (s c)")
                    for ti in range(2):
                        P.mm(pr[:], dC[:, ti, fch * 128:(fch + 1) * 128], rhs(ti), ti == 0, ti == 1, ["cst", "vtok"], [kr_])
                    for ti in range(2):
                        P.mm(pi_[:], dSn[:, ti, fch * 128:(fch + 1) * 128], rhs(ti), ti == 0, ti == 1, ["cst", "vtok"], [ki_])
                    b = n_ % 2
                    for bh in range(2):
                        sl = slice(bh * 256, (bh + 1) * 256)
                        kr = Kr[:, fch, o * 256:(o + 1) * 256]
                        ki = Ki[:, fch, o * 256:(o + 1) * 256]
                        osl = slice(cb * 512 + bh * 256, cb * 512 + (bh + 1) * 256)
                        P.tt("dve", ta[b][:, sl], pr[:, sl], kr, ALU.mult, [kr_, "K"], ["ta%d" % b])
                        P.tt("dve", tb_[b][:, sl], pi_[:, sl], ki, ALU.mult, [ki_, "K"], ["tb%d" % b])
                        P.tt("pool", Yr[:, fch, osl], ta[b][:, sl], tb_[b][:, sl], ALU.subtract, ["ta%d" % b, "tb%d" % b], ["Y"])
                        P.tt("dve", ta[b][:, sl], pr[:, sl], ki, ALU.mult, [kr_, "K"], ["ta%d" % b])
                        P.tt("dve", tb_[b][:, sl], pi_[:, sl], kr, ALU.mult, [ki_, "K"], ["tb%d" % b])
                        P.tt("pool", Yi[:, fch, osl], ta[b][:, sl], tb_[b][:, sl], ALU.add, ["ta%d" % b, "tb%d" % b], ["Y"])
            n_ = 0
            for sq_ in range(4):
                for cc in range(2):
                    pb = n_ % 4
                    n_ += 1
                    cs = slice(sq_ * 256 + cc * 128, sq_ * 256 + (cc + 1) * 128)
                    for fch in range(4):
                        P.mm(pU[pb][:, 0:256], Yr[:, fch, cs], iC[:, fch, :], fch == 0, False, ["Y", "cst"], ["psU%d" % pb])
                        P.mm(pU[pb][:, 0:256], Yi[:, fch, cs], iSn[:, fch, :], False, fch == 3, ["Y", "cst"], ["psU%d" % pb])
                    gate = zc[:, (2 if o == 0 else 4) + cc, sq_ * 256:(sq_ + 1) * 256]
                    if o == 0:
                        dst = u1[:, cc, sq_ * 256:(sq_ + 1) * 256]
                        P.tt("dve", dst, pU[pb][:, 0:256], gate, ALU.mult, ["psU%d" % pb, "zc"], ["u1"])
                    else:
                        dst = C.mixT[:, 6 + cc, sq_ * 256:(sq_ + 1) * 256]
                        P.tt("dve", dst, pU[pb][:, 0:256], gate, ALU.mult, ["psU%d" % pb, "zc"], ["mixH"])


def ph_hyena_s(C, l):
    hs_select(C, l)
    hs_filter(C, l)
    hs_kspec(C, l)
    for o in range(2):
        hs_fwd(C, l, o)
        hs_inv(C, l, o)
    ph_ag(C, C.payH, C.gatH, "agH")
    hs_gather(C, l)


def hs_select(C, l):
    nc = C.nc
    with Phase(nc, "hsa") as P:
        selb = P.sb([128, 2, 64], BF16, "selb")
        P.dma("sp", selb[:], C.inp("selM"), key="sel", writes=["sel"])
        zgb = [P.sb([128, 2, 4096], BF16, "zgb") for _ in range(2)]
        zraw = P.sb([64, 4098], F32, "zraw")
        cbuf = [P.sb([64, 4096], F32, "cbuf") for _ in range(2)]
        ps = [P.ps([128, 512]) for _ in range(4)]
        o_w, _ = C.ppoff["hswm_%d" % l]
        o_b, _ = C.ppoff["hsbm_%d" % l]
        P.memset("dve", zraw[:, 0:1], 0.0, ["zraw"])
        P.memset("dve", zraw[:, 4097:4098], 0.0, ["zraw"])
        n_ = 0
        for blk in range(3):
            zb = blk % 2
            for cc in range(2):
                for r in range(4):
                    P.dma("sp", zgb[zb][:, cc, r * 1024:(r + 1) * 1024], C.gat_zh(r, 2 * blk + cc), key="zgb%d" % zb, writes=["zgb%d" % zb])
            for tb in range(8):
                pb = n_ % 4
                n_ += 1
                for cc in range(2):
                    P.mm(ps[pb][0:64, :], selb[:, cc, :], zgb[zb][:, cc, tb * 512:(tb + 1) * 512], cc == 0, cc == 1,
                         ["sel", "zgb%d" % zb], ["ps%d" % pb])
                if tb % 2:
                    P.cp("act", zraw[:, 1 + tb * 512:1 + (tb + 1) * 512], ps[pb][0:64, :], ["ps%d" % pb], ["zraw%d" % tb])
                else:
                    P.cp("dve", zraw[:, 1 + tb * 512:1 + (tb + 1) * 512], ps[pb][0:64, :], ["ps%d" % pb], ["zraw%d" % tb])
            w = lambda k: C.pp[0:64, o_w + blk * 3 + k:o_w + blk * 3 + k + 1]
            cb = cbuf[blk % 2]
            ck = "cbuf%d" % (blk % 2)
            zk = ["zraw"] + ["zraw%d" % i for i in range(8)]
            P.act(cb[:], zraw[:, 1:4097], AF.Identity, zk, [ck], scale=w(1), bias=C.pp[0:64, o_b + blk:o_b + blk + 1])
            P.stt(cb[:], zraw[:, 0:4096], w(0), cb[:], ALU.mult, ALU.add, zk + [ck], [ck])
            P.stt(cb[:], zraw[:, 2:4098], w(2), cb[:], ALU.mult, ALU.add, zk + [ck], [ck])
            P.dma("sp", C.hyz_d[blk], cb[:], key=ck, reads=[ck])


def hs_filter(C, l):
    nc = C.nc
    with Phase(nc, "hsf") as P:
        zT = P.sb([33, 4096], F32, "zT")
        P.dma("sp", zT[:], C.inp("zT4096"), key="zT", writes=["zT"])
        hid2 = P.sb([64, 4096], F32, "hid2")
        filt_mlp(P, C, l, zT, 4096, hid2)
        w3 = P.sb([64, 256], F32, "w3")
        P.dma("sp", w3[:], C.inp("hy_f_w3m")[l], key="w3", writes=["w3"])
        dec = P.sb([128, 4096], F32, "dec")
        P.dma("sp", dec[:], C.inp("decayS"), key="dec", writes=["dec"])
        hT = [P.sb([128, 4096], F32, "hT") for _ in range(2)]
        pA = [P.ps([128, 512]) for _ in range(2)]
        b3 = ppa(C, "hb3m_%d" % l)
        bm = ppa(C, "hbm_%d" % l)
        n_ = 0
        for o in range(2):
            for tb in range(8):
                pb = n_ % 2
                n_ += 1
                P.mm(pA[pb][:], w3[:, o * 128:(o + 1) * 128], hid2[:, tb * 512:(tb + 1) * 512], True, True, ["w3", "hid2"], ["psA%d" % pb])
                P.stt(hT[o][:, tb * 512:(tb + 1) * 512], pA[pb][:], b3[:, o:o + 1], dec[:, tb * 512:(tb + 1) * 512], ALU.add, ALU.mult,
                      ["psA%d" % pb, "dec"], ["hT%d" % o])
            P.memset("dve", hT[o][64:128, 0:1], 0.0, ["hT%d" % o])
            P.tt("dve", hT[o][0:64, 0:1], hT[o][0:64, 0:1], bm[0:64, o:o + 1], ALU.add, ["hT%d" % o], ["hT%d" % o])
            P.dma("sp", C.hyf_d[o], hT[o][:], key="hTo%d" % o, reads=["hT%d" % o])


def fft_tables(P, C):
    W1 = P.sb([32, 192], BF16, "W1")
    Gc = P.sb([128, 64, 128], BF16, "Gc")
    Gs = P.sb([128, 64, 128], BF16, "Gs")
    P.dma("sp", W1[:], C.inp("fW1"), key="ftabw", writes=["ftabW"])
    P.dma("sp", Gc[:].rearrange("p a b -> p (a b)"), C.inp("fGc"), key="ftabc", writes=["ftab"])
    P.dma("sp", Gs[:].rearrange("p a b -> p (a b)"), C.inp("fGs"), key="ftabs", writes=["ftabS"])
    return W1, Gc, Gs


def fft_fwd(P, X, W1, Gc, Gs, A, psA, psU, handler):
    for c2 in range(32):
        pb = c2 % 2
        for i in range(2):
            P.mm(psA[pb][:, i * 192:(i + 1) * 192], X[:, 2 * c2 + i, :], W1[:], True, True, ["X", "ftabW"], ["psA%d" % pb])
        src = psA[pb][:, 0:384].rearrange("p (i f) -> p i f", i=2)
        if c2 % 2:
            P.cp("act", A[:, 2 * c2:2 * c2 + 2, :], src, ["psA%d" % pb], ["A%d" % c2])
        else:
            P.cp("dve", A[:, 2 * c2:2 * c2 + 2, :], src, ["psA%d" % pb], ["A%d" % c2])
    for grp in range(8):
        b = grp % 2
        pre, pim = psU[2 * b], psU[2 * b + 1]
        kre, kim = "psU%d" % (2 * b), "psU%d" % (2 * b + 1)
        for fi in range(8):
            f1 = grp * 8 + fi
            sl = slice(fi * 64, (fi + 1) * 64)
            akeys = ["ftab", "ftabS"] + ["A%d" % i for i in range(32)]
            P.mm(pre[:, sl], Gc[:, f1, :], A[:, :, f1], True, False, akeys, [kre])
            P.mm(pre[:, sl], Gs[:, f1, :], A[:, :, 64 + f1], False, True, akeys, [kre])
            P.mm(pim[:, sl], Gc[:, f1, :], A[:, :, 64 + f1], True, False, akeys, [kim])
            P.mm(pim[:, sl], Gs[:, f1, :], A[:, :, 128 + f1], False, True, akeys, [kim])
        handler(grp, pre, pim, kre, kim)


def hs_kspec(C, l):
    nc = C.nc
    with Phase(nc, "hsk") as P:
        _kspec_body(C, l, P)


def _kspec_body(C, l, P):
    X = P.sb([32, 64, 128], BF16, "X")
    P.dma("pool", X[:], C.hyf_d[0][0:64, :].rearrange("c (a b) -> a c b", b=128), key="X", writes=["X"])
    W1, Gc, Gs = fft_tables(P, C)
    A = P.sb([128, 64, 192], BF16, "A")
    KA = [P.sb([128, 4096], F32, "KA") for _ in range(2)]
    Ko = [P.sb([128, 512], BF16, "Ko") for _ in range(4)]
    psA = [P.ps([128, 512]) for _ in range(2)]
    psU = [P.ps([128, 512]) for _ in range(4)]
    cnt = [0]
    for o in range(2):
        for dd in range(2):
            if (o, dd) != (0, 0):
                P.dma("pool", X[:], C.hyf_d[o][dd * 64:(dd + 1) * 64, :].rearrange("c (a b) -> a c b", b=128), key="X", writes=["X"])

            def handler(grp, pre, pim, kre, kim, dd=dd, o=o):
                sl = slice(grp * 512, (grp + 1) * 512)
                if dd == 0:
                    P.cp("act", KA[0][:, sl], pre[:], [kre], ["KAr"])
                    P.cp("dve", KA[1][:, sl], pim[:], [kim], ["KAi"])
                else:
                    for ri, (ps_, kk, op, kak) in enumerate(((pre, kre, ALU.add, "KAr"), (pim, kim, ALU.subtract, "KAi"))):
                        si = cnt[0] % 4
                        cnt[0] += 1
                        P.tt("dve", Ko[si][:], KA[ri][:, sl], ps_[:], op, [kak, kk], ["Ko%d" % si])
                        P.dma("sp", C.Ksp_d[o, ri][:, sl], Ko[si][:], key="Ko%d" % si, reads=["Ko%d" % si])
            fft_fwd(P, X, W1, Gc, Gs, A, psA, psU, handler)


def hs_fwd(C, l, o):
    nc = C.nc
    with Phase(nc, "hsc") as P:
        X = P.sb([32, 64, 128], BF16, "X")
        if o == 0:
            P.dma("pool", X[:], C.hyz_d[0].rearrange("c (a b) -> a c b", b=128), key="X", writes=["X"])
        else:
            P.dma("sp", X[:].rearrange("p c t -> p (c t)"), C.X2_d, key="X", writes=["X"])
        W1, Gc, Gs = fft_tables(P, C)
        A = P.sb([128, 64, 192], BF16, "A")
        Ks = [P.sb([128, 4096], BF16, "Ks") for _ in range(2)]
        for ri in range(2):
            P.dma("sp", Ks[ri][:], C.Ksp_d[o, ri], key="Ks", writes=["Ks"])
        Ys = [P.sb([128, 4096], BF16, "Ys") for _ in range(2)]
        ta = [P.sb([128, 512], F32, "ta") for _ in range(2)]
        tb_ = [P.sb([128, 512], F32, "tb") for _ in range(2)]
        psA = [P.ps([128, 512]) for _ in range(2)]
        psU = [P.ps([128, 512]) for _ in range(4)]

        def handler(grp, pre, pim, kre, kim):
            sl = slice(grp * 512, (grp + 1) * 512)
            b = grp % 2
            P.tt("dve", ta[b][:], pre[:], Ks[0][:, sl], ALU.mult, [kre, "Ks"], ["ta%d" % b])
            P.tt("dve", tb_[b][:], pim[:], Ks[1][:, sl], ALU.mult, [kim, "Ks"], ["tb%d" % b])
            P.tt("pool", Ys[0][:, sl], ta[b][:], tb_[b][:], ALU.subtract, ["ta%d" % b, "tb%d" % b], ["Ys"])
            P.tt("dve", ta[b][:], pre[:], Ks[1][:, sl], ALU.mult, [kre, "Ks"], ["ta%d" % b])
            P.tt("dve", tb_[b][:], pim[:], Ks[0][:, sl], ALU.mult, [kim, "Ks"], ["tb%d" % b])
            P.tt("pool", Ys[1][:, sl], ta[b][:], tb_[b][:], ALU.add, ["ta%d" % b, "tb%d" % b], ["Ys"])
        fft_fwd(P, X, W1, Gc, Gs, A, psA, psU, handler)
        for ri in range(2):
            P.dma("sp", C.Y_d[ri], Ys[ri][:], key="Yo", reads=["Ys"])


def hs_inv(C, l, o):
    nc = C.nc
    with Phase(nc, "hsi") as P:
        E1 = P.sb([128, 256], BF16, "E1"); E2 = P.sb([128, 256], BF16, "E2")
        Vc = P.sb([64, 128, 32], BF16, "Vc"); Vsn = P.sb([64, 128, 32], BF16, "Vsn")
        P.dma("sp", E1[:], C.inp("fE1"), key="itab1", writes=["itabE1"])
        P.dma("sp", E2[:], C.inp("fE2"), key="itab2", writes=["itabE2"])
        Ys = [P.sb([128, 64, 64], BF16, "Ys") for _ in range(2)]
        for ri in range(2):
            P.dma("sp", Ys[ri][:].rearrange("p f c -> p (f c)"), C.Y_d[ri], key="Ys", writes=["Ys"])
        P.dma("sp", Vc[:].rearrange("p a b -> p (a b)"), C.inp("fVc"), key="itab3", writes=["itabVc"])
        P.dma("sp", Vsn[:].rearrange("p a b -> p (a b)"), C.inp("fVsn"), key="itab4", writes=["itabVs"])
        Gx = P.sb([32, 64, 128], BF16, "Gx")
        P.dma("pool", Gx[:], C.hyz_d[1 + o].rearrange("c (a b) -> a c b", b=128), key="Gx", writes=["Gx"])
        Br = P.sb([64, 128, 64], BF16, "Br")
        Bi = P.sb([64, 128, 64], BF16, "Bi")
        X2 = P.sb([32, 64, 128], BF16, "X2")
        ps = [P.ps([128, 512]) for _ in range(4)]
        for c2 in range(32):
            pb = c2 % 4
            for i in range(2):
                c = 2 * c2 + i
                P.mm(ps[pb][0:64, i * 256:(i + 1) * 256], Ys[0][:, :, c], E1[:], True, False, ["Ys", "itabE1"], ["ps%d" % pb])
                P.mm(ps[pb][0:64, i * 256:(i + 1) * 256], Ys[1][:, :, c], E2[:], False, True, ["Ys", "itabE2"], ["ps%d" % pb])
            pv = ps[pb][0:64, :].rearrange("p (i r t) -> p r t i", i=2, r=2)
            P.cp("act", Br[:, :, 2 * c2:2 * c2 + 2], pv[:, 0], ["ps%d" % pb], ["Br%d" % c2])
            P.cp("dve", Bi[:, :, 2 * c2:2 * c2 + 2], pv[:, 1], ["ps%d" % pb], ["Bi%d" % c2])
        for tg in range(16):
            pb = tg % 4
            for ti in range(8):
                t2 = tg * 8 + ti
                sl = slice(ti * 64, (ti + 1) * 64)
                bkeys = ["itabVc", "itabVs"] + ["Br%d" % i for i in range(32)] + ["Bi%d" % i for i in range(32)]
                P.mm(ps[pb][0:32, sl], Vc[:, t2, :], Br[:, t2, :], True, False, bkeys, ["ps%d" % pb])
                P.mm(ps[pb][0:32, sl], Vsn[:, t2, :], Bi[:, t2, :], False, True, bkeys, ["ps%d" % pb])
            pv = ps[pb][0:32, :].rearrange("p (t c) -> p c t", c=64)
            P.tt("dve", X2[:, :, tg * 8:(tg + 1) * 8], pv, Gx[:, :, tg * 8:(tg + 1) * 8], ALU.mult, ["ps%d" % pb, "Gx"], ["X2"])
        dst = C.X2_d if o == 0 else C.payH.ap()
        P.dma("sp", dst, X2[:].rearrange("p c t -> p (c t)"), key="X2", reads=["X2"])


def hs_gather(C, l):
    nc = C.nc
    with Phase(nc, "hsg") as P:
        hyall = P.sb([128, 2, 4096], BF16, "hyall")
        for r in range(4):
            dst = hyall[(r % 2) * 64:(r % 2) * 64 + 64, r // 2, :].rearrange("p (a b) -> p a b", b=128)
            src = C.gatH.ap()[r * 32:(r + 1) * 32, :].rearrange("a (c b) -> c a b", b=128)
            P.dma("sp", dst, src, key="hyall", writes=["hyall"])
        eq = ppa(C, "eq")
        acc = P.sb([128, 2, 1024], F32, "acc")
        for k in range(2):
            P.ts("dve", acc[:, k, :], hyall[:, k, 0:1024], eq[:, 0:1], ALU.mult, ["hyall"], ["acc"])
            for j in range(1, 4):
                P.stt(acc[:, k, :], hyall[:, k, j * 1024:(j + 1) * 1024], eq[:, j:j + 1], acc[:, k, :], ALU.mult, ALU.add,
                      ["hyall", "acc"], ["acc"])
            P.cp("act", C.mixT[:, 6 + k, 0:1024], acc[:, k, :], ["acc"], ["mixH"])


_CACHE = {}


def kernel(**inputs):
    inp = {k: np.asarray(v) for k, v in inputs.items()}
    maps, ppoff = host_prep(inp)
    npp = maps[0]["pp"].shape[1]
    if "nc" not in _CACHE:
        _CACHE["nc"] = build(ppoff, npp)
    nc, C = _CACHE["nc"]
    maps = [{k: v for k, v in m.items() if k in C.declared} for m in maps]
    res = run_bass_kernel_spmd(nc, maps, core_ids=list(range(NCORE)))
    return assemble(res.results)


def assemble(results):
    y_prompt = np.zeros((32, 256, D), np.float32)
    y_sample = np.zeros((2, 4096, D), np.float32)
    new_k = np.zeros((32, DEPTH, 256, 2, 64), np.float32)
    new_v = np.zeros((32, DEPTH, 256, 2, 64), np.float32)
    for i, r in enumerate(results):
        b, q = i // 4, i % 4
        y = np.asarray(r["y_tok"])
        y_prompt[4 * i:4 * i + 4] = y[:1024].reshape(4, 256, D)
        y_sample[b, 1024 * q:1024 * q + 1024] = y[1024:]
        new_k[4 * i:4 * i + 4] = np.asarray(r["new_k"]).reshape(4, DEPTH, 256, 2, 64)
        new_v[4 * i:4 * i + 4] = np.asarray(r["new_v"]).reshape(4, DEPTH, 256, 2, 64)
    return (y_prompt, y_sample, new_k, new_v)
```

```python
import contextlib
import os
import math
import numpy as np
import ml_dtypes
import concourse.bass as bass
import concourse.mybir as mybir
from concourse.bass_utils import run_bass_kernel_spmd

F32 = mybir.dt.float32
BF16 = mybir.dt.bfloat16
ALU = mybir.AluOpType
AF = mybir.ActivationFunctionType

D = 1024
DEPTH = 2
NCORE = 8
EPS = 1e-6
FFN = 2816
NJ = 22
L_S = 4096
L_P = 256
PAYROWS = 1032
NOHY = bool(os.environ.get('NOHY'))
BF16_CONSTS = ['dftAlt', 'fW1', 'fGc', 'fGs', 'fE1', 'fE2', 'fVc', 'fVsn', 'dftC', 'dftS', 'dftSn', 'idftC', 'idftSn', 'selM', 'onesm', 'blk64', 'ropeRT', 'identb']


class Sched:
    ENGS = ("pe", "act", "dve", "pool", "sp")
    HND = dict(pe="tensor", act="scalar", dve="vector", pool="gpsimd", sp="sync")
    UID = 0
    GLOBAL = {}

    def __init__(self, nc):
        self.nc = nc
        self.ops = []
        self.last_w = {}
        self.readers = {}
        self.dma_keys = {}

    def _deps(self, reads, writes):
        deps = set()
        for r in reads:
            lw = self.last_w.get(r)
            if lw is not None:
                deps.add(lw)
        for w in writes:
            lw = self.last_w.get(w)
            if lw is not None:
                deps.add(lw)
            for rd in self.readers.get(w, ()):
                deps.add(rd)
        return deps

    def _commit(self, oid, reads, writes):
        for w in writes:
            self.last_w[w] = oid
            self.readers[w] = []
        for r in reads:
            if r in writes:
                continue
            self.readers.setdefault(r, []).append(oid)

    def op(self, eng, fn, reads=(), writes=()):
        reads = tuple(reads); writes = tuple(writes)
        writes = writes + tuple(r for r in reads if r.startswith("ps") and r not in writes)
        deps = self._deps(reads, writes)
        oid = len(self.ops)
        self.ops.append(dict(id=oid, eng=eng, fn=fn, deps=deps, kind="c", sig=False))
        self._commit(oid, reads, writes)
        return oid

    def dma(self, queue, out, in_, key, reads=(), writes=(), **kw):
        reads = tuple(reads); writes = tuple(writes)
        deps = self._deps(reads, writes)
        for w in writes:
            lw = self.last_w.get(w)
            if lw is not None and self.ops[lw]["kind"] == "d" and self.ops[lw]["key"] == key and self.ops[lw]["eng"] == queue:
                deps.discard(lw)
        oid = len(self.ops)
        cnt = self.dma_keys.setdefault(key, [0, "d"])
        cnt[0] += 1
        self.ops.append(dict(id=oid, eng=queue, deps=deps, kind="d", key=key,
                             cum=16 * cnt[0], out=out, in_=in_, kw=kw, sig=False))
        self._commit(oid, reads, writes)
        return oid

    def cc(self, fn, key, reads=(), writes=()):
        reads = tuple(reads); writes = tuple(writes)
        deps = self._deps(reads, writes)
        oid = len(self.ops)
        cnt = self.dma_keys.setdefault(key, [0, "cc"])
        cnt[0] += 1
        self.ops.append(dict(id=oid, eng="pool", deps=deps, kind="cc", key=key,
                             cum=cnt[0], fn=fn, sig=False))
        self._commit(oid, reads, writes)
        return oid

    def emit(self):
        nc = self.nc
        ops = self.ops
        for o in ops:
            for d in o["deps"]:
                Dd = ops[d]
                if Dd["kind"] == "c" and Dd["eng"] != o["eng"]:
                    Dd["sig"] = True
        last = {}
        for o in ops:
            if o["kind"] == "c":
                last[o["eng"]] = o
        for o in last.values():
            o["sig"] = True
        cnt = {e: 0 for e in self.ENGS}
        for o in ops:
            if o["kind"] == "c" and o["sig"]:
                cnt[o["eng"]] += 1
                o["sigval"] = cnt[o["eng"]]
        per_eng = {e: [] for e in self.ENGS}
        for o in ops:
            per_eng[o["eng"]].append(o)
        with contextlib.ExitStack() as st:
            G = Sched.GLOBAL
            if G.get("nc") is not nc:
                G.clear()
                G.update(nc=nc, esem={e: nc.alloc_semaphore(name="ge_%s" % e) for e in self.ENGS},
                         ecount={e: 0 for e in self.ENGS}, kpool=[], kcount=[])
            while len(G["kpool"]) < len(self.dma_keys):
                G["kpool"].append(nc.alloc_semaphore(name="gk_%d" % len(G["kpool"])))
                G["kcount"].append(0)
            esem = G["esem"]
            ksem = {}
            kbase = {}
            for i, k in enumerate(self.dma_keys):
                ksem[k] = G["kpool"][i]
                kbase[k] = G["kcount"][i]
            for o in ops:
                if o["kind"] == "c" and o["sig"]:
                    o["sigval"] += G["ecount"][o["eng"]]
                elif o["kind"] in ("d", "cc"):
                    o["cum"] += kbase[o["key"]]
            for e in self.ENGS:
                G["ecount"][e] += cnt[e]
            for i, (k, c) in enumerate(self.dma_keys.items()):
                G["kcount"][i] += (16 if c[1] == "d" else 1) * c[0]
            block = st.enter_context(nc.Block())

            def body_for(ename):
                def body(eng):
                    waited = {}

                    def need(sem, val, tag):
                        if waited.get(tag, 0) >= val:
                            return
                        eng.wait_ge(sem, val)
                        waited[tag] = val

                    for o in per_eng[ename]:
                        for d in sorted(o["deps"]):
                            Dd = ops[d]
                            if Dd["kind"] == "c":
                                if Dd["eng"] == ename:
                                    continue
                                need(esem[Dd["eng"]], Dd["sigval"], "e_" + Dd["eng"])
                            else:
                                need(ksem[Dd["key"]], Dd["cum"], ("k", Dd["key"]))
                        if o["kind"] == "c":
                            ins = o["fn"](eng)
                            if o["sig"]:
                                ins.then_inc(esem[ename], 1)
                        elif o["kind"] == "d":
                            eng.dma_start(out=o["out"], in_=o["in_"], **o["kw"]).then_inc(ksem[o["key"]], 16)
                        else:
                            o["fn"](eng).then_inc(ksem[o["key"]], 1)
                    if ename == "sp":
                        for k, c in self.dma_keys.items():
                            need(ksem[k], kbase[k] + (16 if c[1] == "d" else 1) * c[0], ("k", k))
                        for e2, o2 in last.items():
                            if e2 != ename:
                                need(esem[e2], o2["sigval"], "e_" + e2)
                return body

            for ename in self.ENGS:
                if per_eng[ename] or ename == "sp":
                    getattr(block, self.HND[ename])(body_for(ename))


class Phase:
    UID = 0

    def __init__(self, nc, name):
        self.nc = nc
        self.name = name
        self.st = contextlib.ExitStack()
        self.S = Sched(nc)
        self._n = 0
        self._rr = 0

    def __enter__(self):
        self.st.__enter__()
        return self

    def __exit__(self, *a):
        if a[0] is None:
            self.S.emit()
        return self.st.__exit__(*a)

    def sb(self, shape, dt, name=None):
        self._n += 1
        Phase.UID += 1
        return self.st.enter_context(self.nc.sbuf_tensor("%s_%s%d" % (self.name, name or "t", Phase.UID), list(shape), dt))

    def ps(self, shape=(128, 512), dt=F32):
        self._n += 1
        Phase.UID += 1
        return self.st.enter_context(self.nc.psum_tensor("%s_p%d" % (self.name, Phase.UID), list(shape), dt))

    def mm(self, out, lhsT, rhs, start, stop, reads, writes, **kw):
        return self.S.op("pe", lambda e: e.matmul(out, lhsT=lhsT, rhs=rhs, start=start, stop=stop, **kw), reads, writes)

    def tr(self, out, in_, ident, reads, writes):
        return self.S.op("pe", lambda e: e.transpose(out, in_, ident), reads, writes)

    def act(self, out, in_, func, reads, writes, bias=None, scale=None, eng="act"):
        kw = {}
        if bias is not None:
            kw["bias"] = bias
        if scale is not None:
            kw["scale"] = scale
        return self.S.op("act", lambda e: e.activation(out=out, in_=in_, func=func, **kw), reads, writes)

    def tt(self, eng, out, in0, in1, op, reads, writes):
        return self.S.op(eng, lambda e: e.tensor_tensor(out=out, in0=in0, in1=in1, op=op), reads, writes)

    def ts(self, eng, out, in0, s1, op0, reads, writes, s2=None, op1=None):
        if op1 is None:
            return self.S.op(eng, lambda e: e.tensor_scalar(out=out, in0=in0, scalar1=s1, scalar2=None, op0=op0), reads, writes)
        return self.S.op(eng, lambda e: e.tensor_scalar(out=out, in0=in0, scalar1=s1, scalar2=s2, op0=op0, op1=op1), reads, writes)

    def stt(self, out, in0, scalar, in1, op0, op1, reads, writes):
        return self.S.op("dve", lambda e: e.scalar_tensor_tensor(out=out, in0=in0, scalar=scalar, in1=in1, op0=op0, op1=op1), reads, writes)

    def cp(self, eng, out, in_, reads, writes):
        if eng == "act":
            return self.S.op("act", lambda e: e.activation(out=out, in_=in_, func=AF.Copy), reads, writes)
        return self.S.op(eng, lambda e: e.tensor_copy(out=out, in_=in_), reads, writes)

    def memset(self, eng, ap, val, writes):
        return self.S.op(eng, lambda e: e.memset(ap, val), (), writes)

    def dma(self, q, out, in_, key, reads=(), writes=(), **kw):
        return self.S.dma(q, out, in_, key, reads, writes, **kw)

    def rsqrt(self, out, in_, reads, writes, eps_ap):
        self.act(out, in_, AF.Ln, reads, writes, bias=eps_ap)
        return self.act(out, out, AF.Exp, writes, writes, scale=-0.5)


def _pt(v, n):
    return np.ascontiguousarray(np.asarray(v, np.float32).reshape(n, 128).T)


class Pack:
    def __init__(self):
        self.cols = []
        self.off = {}
        self.n = 0

    def add(self, name, arr):
        arr = np.asarray(arr, np.float32)
        if arr.ndim == 1:
            arr = arr[:, None]
        arr = arr.reshape(arr.shape[0], -1)
        if arr.shape[0] < 128:
            arr = np.concatenate([arr, np.zeros((128 - arr.shape[0], arr.shape[1]), np.float32)], 0)
        self.off[name] = (self.n, arr.shape[1])
        self.cols.append(arr)
        self.n += arr.shape[1]

    def build(self):
        return np.ascontiguousarray(np.concatenate(self.cols, 1))


def _rope_tables(tok):
    rows = tok // 64
    cols = tok % 64
    inv = 10000.0 ** (-np.arange(0, 32, 2, dtype=np.float32) / 32.0)
    cosT = np.zeros((128, len(tok)), np.float32)
    sinT = np.zeros((128, len(tok)), np.float32)
    for p in range(128):
        dd = p % 64
        pos = rows if dd < 32 else cols
        ang = pos.astype(np.float32) * inv[dd % 16]
        cosT[p] = np.cos(ang)
        sinT[p] = np.sin(ang)
    return cosT, sinT


def _consts():
    c = {}
    c["ident"] = np.eye(128, dtype=np.float32)
    c["onesm"] = np.full((128, 128), 1.0 / 1024.0, np.float32)
    b = np.zeros((128, 128), np.float32)
    b[:64, :64] = 1.0 / 64.0
    b[64:, 64:] = 1.0 / 64.0
    c["blk64"] = b
    rt = np.zeros((128, 128), np.float32)
    for m in range(128):
        if (m % 32) < 16:
            rt[m + 16, m] = -1.0
        else:
            rt[m - 16, m] = 1.0
    c["ropeRT"] = rt
    L = 256
    t = np.linspace(0.0, 1.0, L, dtype=np.float32)[:, None]
    w_ang = (2.0 * math.pi * np.arange(L, dtype=np.float32)[:, None] / L).astype(np.float32)
    bands = np.linspace(1e-4, 15, 16, dtype=np.float32)[None, :]
    z = np.concatenate([t, np.cos(bands * w_ang), np.sin(-bands * w_ang)], -1).astype(np.float32)
    c["zT256"] = np.ascontiguousarray(z.T)
    deltas = np.abs(np.linspace(math.log(1e-2) / 1.5, math.log(1e-2) / 0.3, 256, dtype=np.float32))
    dec = np.exp(-t * deltas[None, :]).astype(np.float32)
    c["decayP"] = np.ascontiguousarray(dec.reshape(2, 128, 256).transpose(1, 0, 2))
    tt = np.arange(256, dtype=np.float64)[:, None]
    ff = np.arange(256, dtype=np.float64)[None, :]
    ang = 2.0 * math.pi * tt * ff / 512.0
    cm, sm = np.cos(ang), np.sin(ang)
    alt = np.where(np.arange(256) % 2 == 0, 1.0, -1.0)
    sm_n = -sm.copy(); sm_n[:, 0] = alt
    lay_t = lambda a: np.ascontiguousarray(a.reshape(2, 128, 256).transpose(1, 0, 2).astype(np.float32))
    c["dftC"] = lay_t(cm); c["dftS"] = lay_t(sm); c["dftSn"] = lay_t(sm_n)
    c["dftAlt"] = np.ascontiguousarray(alt.reshape(2, 128).T.astype(np.float32))
    wgt = np.full((256, 1), 2.0); wgt[0] = 1.0
    ic = (cm * wgt.T / 512.0)
    isn = (-sm * wgt.T / 512.0); isn[:, 0] = alt / 512.0
    lay_f = lambda a: np.ascontiguousarray(a.T.reshape(2, 128, 256).transpose(1, 0, 2).astype(np.float32))
    c["idftC"] = lay_f(ic); c["idftSn"] = lay_f(isn)
    L = 4096
    t = np.linspace(0.0, 1.0, L, dtype=np.float32)[:, None]
    w_ang = (2.0 * math.pi * np.arange(L, dtype=np.float32)[:, None] / L).astype(np.float32)
    z = np.concatenate([t, np.cos(bands * w_ang), np.sin(-bands * w_ang)], -1).astype(np.float32)
    c["zT4096"] = np.ascontiguousarray(z.T)
    t1 = np.arange(32, dtype=np.float64)[:, None]
    f1 = np.arange(64, dtype=np.float64)[None, :]
    a1 = 2.0 * math.pi * t1 * f1 / 64.0
    c["fW1"] = np.concatenate([np.cos(a1), -np.sin(a1), -np.cos(a1)], 1).astype(np.float32)
    t2 = np.arange(128, dtype=np.float64)[:, None, None]
    f1_ = np.arange(64, dtype=np.float64)[None, :, None]
    f2 = np.arange(128, dtype=np.float64)[None, None, :]
    ag = 2.0 * math.pi * t2 * (f1_ + 64.0 * f2) / 8192.0
    c["fGc"] = np.cos(ag).astype(np.float32).reshape(128, 8192)
    c["fGs"] = np.sin(ag).astype(np.float32).reshape(128, 8192)
    f2c = np.arange(128, dtype=np.float64)[:, None]
    t2r = np.arange(128, dtype=np.float64)[None, :]
    ae = 2.0 * math.pi * f2c * t2r / 128.0
    c["fE1"] = np.concatenate([np.cos(ae), np.sin(ae)], 1).astype(np.float32)
    c["fE2"] = np.concatenate([-np.sin(ae), np.cos(ae)], 1).astype(np.float32)
    f1c = np.arange(64, dtype=np.float64)[:, None, None]
    t2m = np.arange(128, dtype=np.float64)[None, :, None]
    t1m = np.arange(32, dtype=np.float64)[None, None, :]
    av = 2.0 * math.pi * f1c * (128.0 * t1m + t2m) / 8192.0
    c["fVc"] = (np.cos(av) / 8192.0).astype(np.float32).reshape(64, 4096)
    c["fVsn"] = (-np.sin(av) / 8192.0).astype(np.float32).reshape(64, 4096)
    c["_deltas"] = deltas
    return c


def host_prep(inp):
    cs = _consts()
    maps = []
    for i in range(NCORE):
        b, q = i // 4, i % 4
        m = {}
        m["x_tok"] = np.ascontiguousarray(np.concatenate(
            [inp["x_prompt"][4 * i:4 * i + 4].reshape(1024, D), inp["x_sample"][b, 1024 * q:1024 * q + 1024]], 0).astype(np.float32))
        cond2 = np.stack([inp["c_ctx"], inp["c"][b]], 0).astype(np.float32)
        m["condT"] = np.ascontiguousarray(cond2.reshape(2, 8, 128).transpose(2, 1, 0))
        m["cache_k"] = np.ascontiguousarray(inp["cache_k"][b].reshape(DEPTH, 256, 128).astype(np.float32))
        m["cache_v"] = np.ascontiguousarray(inp["cache_v"][b].reshape(DEPTH, 256, 128).astype(np.float32))
        for nm in ["w_mod", "w_in", "w_o", "ffn_w_up", "ffn_w_down", "conv_pw_w", "hy_f_w1", "hy_f_w2", "hy_f_w3"]:
            m[nm] = np.ascontiguousarray(inp[nm].astype(np.float32))
        m["hy_f_b3r"] = np.ascontiguousarray(inp["hy_f_b3"].reshape(DEPTH, 1, 1024).astype(np.float32))
        m["hy_biasr"] = np.ascontiguousarray(inp["hy_bias"].reshape(DEPTH, 1, 512).astype(np.float32))
        pk = Pack()
        for l in range(DEPTH):
            pk.add("norm1_%d" % l, _pt(inp["norm1"][l], 8))
            pk.add("norm2_%d" % l, _pt(inp["norm2"][l], 8))
            pk.add("bmod_%d" % l, _pt(inp["b_mod"][l], 48))
            pk.add("qw_%d" % l, np.tile(inp["q_norm"][l], 2))
            pk.add("kw_%d" % l, np.tile(inp["k_norm"][l], 2))
            pk.add("cdw_%d" % l, inp["conv_dw_w"][l].T.reshape(2, 128, 31).transpose(1, 0, 2))
            pk.add("cdb_%d" % l, _pt(inp["conv_dw_b"][l], 2))
            pk.add("gnw_%d" % l, _pt(inp["conv_gn_w"][l], 2))
            pk.add("gnb_%d" % l, _pt(inp["conv_gn_b"][l], 2))
            pk.add("pwb_%d" % l, _pt(inp["conv_pw_b"][l], 2))
            pk.add("fdw_%d" % l, inp["ffn_dw_w"][l].T.reshape(44, 128, 3).transpose(1, 0, 2))
            pk.add("fdb_%d" % l, _pt(inp["ffn_dw_b"][l], 44))
            pk.add("hsw_%d" % l, inp["hy_short_w"][l].T.reshape(6, 128, 3).transpose(1, 0, 2))
            pk.add("hsb_%d" % l, _pt(inp["hy_short_b"][l], 6))
            pk.add("hfr_%d" % l, inp["hy_freq"][l])
            pk.add("hfb1_%d" % l, inp["hy_f_b1"][l])
            pk.add("hfb2_%d" % l, inp["hy_f_b2"][l])
        pk.add("fnorm", _pt(inp["final_norm"], 8))
        eq = np.zeros((128, 4), np.float32); eq[:, q] = 1.0
        el = np.zeros((128, 4), np.float32)
        er = np.zeros((128, 4), np.float32)
        if q > 0:
            el[:, q - 1] = 1.0
        if q < 3:
            er[:, q + 1] = 1.0
        pk.add("eq", eq); pk.add("el", el); pk.add("er", er)
        chs = 64 * q + np.arange(64)
        for l in range(DEPTH):
            hw = inp["hy_short_w"][l]
            pk.add("hswm_%d" % l, np.stack([hw[:, blk * 256 + chs].T for blk in range(3)], 1))
            pk.add("hsbm_%d" % l, np.stack([inp["hy_short_b"][l][blk * 256 + chs] for blk in range(3)], 1))
            b3 = inp["hy_f_b3"][l].reshape(2, 2, 256)
            pk.add("hb3m_%d" % l, np.stack([np.concatenate([b3[o, 0, chs], b3[o, 1, chs]]) for o in range(2)], 1))
            pk.add("hbm_%d" % l, inp["hy_bias"][l][:, chs].T)
        pk.add("eps", np.full((128, 1), EPS, np.float32))
        m["pp"] = pk.build()
        tok = np.arange(1024 * q, 1024 * q + 1024)
        cosT, sinT = _rope_tables(tok)
        m["ropec"] = cosT
        m["ropes"] = sinT
        sel = np.zeros((128, 2, 64), np.float32)
        for mm_ in range(64):
            ch = 64 * q + mm_
            sel[ch % 128, ch // 128, mm_] = 1.0
        m["selM"] = sel
        w3 = inp["hy_f_w3"].reshape(DEPTH, 64, 2, 2, 256)
        m["hy_f_w3m"] = np.ascontiguousarray(w3[:, :, :, :, chs].reshape(DEPTH, 64, 256).astype(np.float32))
        tS = np.linspace(0.0, 1.0, 4096, dtype=np.float32)[None, :]
        dS = np.exp(-tS * cs["_deltas"][chs][:, None]).astype(np.float32)
        m["decayS"] = np.ascontiguousarray(np.concatenate([dS, dS], 0))
        for k, v in cs.items():
            if not k.startswith("_"):
                m[k] = v
        m["identb"] = cs["ident"]
        for k in BF16_CONSTS:
            m[k] = np.ascontiguousarray(m[k]).astype(ml_dtypes.bfloat16)
        maps.append(m)
    return maps, pk.off


class Ctx:
    pass


def build(ppoff, npp, stop_after=None, dbg=None, extra_shapes=None, only=None):
    nc = bass.Bass("TRN2", target_bir_lowering=False)
    C = Ctx()
    C.nc = nc
    C.ppoff = ppoff
    do = lambda n, s, dt=F32: nc.dram_tensor(n, list(s), dt, kind="ExternalOutput").ap()
    C.declared = {}
    shapes = dict(x_tok=[2048, D], condT=[128, 8, 2], cache_k=[DEPTH, 256, 128], cache_v=[DEPTH, 256, 128],
                  w_mod=[DEPTH, D, 6 * D], w_in=[DEPTH, D, 2048], w_o=[DEPTH, D, D], ffn_w_up=[DEPTH, D, 2 * FFN],
                  ffn_w_down=[DEPTH, FFN, D], conv_pw_w=[DEPTH, 256, 256], pp=[128, npp], ropec=[128, 1024],
                  ropes=[128, 1024], ident=[128, 128], identb=[128, 128], onesm=[128, 128], blk64=[128, 128], ropeRT=[128, 128],
                  hy_f_w1=[DEPTH, 33, 64], hy_f_w2=[DEPTH, 64, 64], hy_f_w3=[DEPTH, 64, 1024], hy_f_b3r=[DEPTH, 1, 1024],
                  hy_biasr=[DEPTH, 1, 512], zT256=[33, 256], decayP=[128, 2, 256], dftC=[128, 2, 256], dftS=[128, 2, 256],
                  dftSn=[128, 2, 256], idftC=[128, 2, 256], idftSn=[128, 2, 256], dftAlt=[128, 2],
                  selM=[128, 2, 64], hy_f_w3m=[DEPTH, 64, 256], decayS=[128, 4096], zT4096=[33, 4096], fW1=[32, 192],
                  fGc=[128, 8192], fGs=[128, 8192], fE1=[128, 256], fE2=[128, 256], fVc=[64, 4096], fVsn=[64, 4096])
    shapes.update(extra_shapes or {})

    def inp(name):
        if name not in C.declared:
            C.declared[name] = nc.dram_tensor(name, list(shapes[name]), BF16 if name in BF16_CONSTS else F32,
                                              kind="ExternalInput").ap()
        return C.declared[name]
    C.inp = inp
    C.y_tok = do("y_tok", [2048, D])
    C.new_k = do("new_k", [4, DEPTH, 256, 128])
    C.new_v = do("new_v", [4, DEPTH, 256, 128])
    C.dbg = do("dbg", dbg[1]) if dbg else None
    C.dbgname = dbg[0] if dbg else None
    dt_ = lambda n, s, dt=BF16: nc.dram_tensor(n, list(s), dt).ap()
    C.qT_d = [dt_("qT_d%d" % g, [512, 1024]) for g in range(2)]
    C.uT_d = [dt_("uT_d%d" % g, [256, 1024]) for g in range(2)]
    C.kT_d = dt_("kT_d", [128, 1024])
    C.vtok_d = dt_("vtok_d", [1024, 128])
    C.zhT_d = dt_("zhT_d", [768, 1024])
    C.payA = nc.dram_tensor("payA", [264, 1024], BF16)
    C.gatA = nc.dram_tensor("gatA", [4 * 264, 1024], BF16)
    C.payB = nc.dram_tensor("payB", [384, 1024], BF16)
    C.gatB = nc.dram_tensor("gatB", [4 * 384, 1024], BF16)
    C.payC = nc.dram_tensor("payC", [384, 1024], BF16)
    C.gatC = nc.dram_tensor("gatC", [4 * 384, 1024], BF16)
    C.pay_kT = C.payA.ap()[0:128, :]
    C.pay_V = C.payA.ap()[128:256, :].rearrange("r (a f) -> (r a) f", f=128)
    C.pay_ub = C.payA.ap()[256:264, :].rearrange("r (a f) -> (r a) f", f=32)
    C.pay_zh = lambda zc: (C.payB if zc < 3 else C.payC).ap()[(zc % 3) * 128:(zc % 3) * 128 + 128, :]
    C.gat_kT = lambda r: C.gatA.ap()[r * 264:r * 264 + 128, :]
    C.gat_V = lambda r: C.gatA.ap()[r * 264 + 128:r * 264 + 256, :].rearrange("r (a f) -> (r a) f", f=128)
    C.gat_ub = lambda r: C.gatA.ap()[r * 264 + 256:r * 264 + 264, :].rearrange("r (a f) -> (r a) f", f=32)
    C.gat_zh = lambda r, zc: (C.gatB if zc < 3 else C.gatC).ap()[r * 384 + (zc % 3) * 128:r * 384 + (zc % 3) * 128 + 128, :]
    C.pay3 = nc.dram_tensor("pay3", [128, 1024], BF16)
    C.hyz_d = dt_("hyz_d", [3, 64, 4096], F32)
    C.hyf_d = dt_("hyf_d", [2, 128, 4096], F32)
    C.Ksp_d = dt_("Ksp_d", [2, 2, 128, 4096], BF16)
    C.Y_d = dt_("Y_d", [2, 128, 4096], BF16)
    C.X2_d = dt_("X2_d", [32, 8192], BF16)
    C.payH = nc.dram_tensor("payH", [32, 8192], BF16)
    C.gatH = nc.dram_tensor("gatH", [128, 8192], BF16)
    C.gat3 = nc.dram_tensor("gat3", [4 * 128, 1024], BF16)

    with contextlib.ExitStack() as gst:
        gsb = lambda n, s, dt: gst.enter_context(nc.sbuf_tensor("g_" + n, list(s), dt))
        C.xT = gsb("xT", [128, 8, 2048], F32)
        C.hT = gsb("hT", [128, 8, 1026], BF16)
        C.mixT = C.hT
        C.pp = gsb("pp", [128, npp], F32)
        C.modT = gsb("modT", [128, DEPTH, 48, 2], F32)
        C.A1 = gsb("A1", [128, DEPTH, 8, 2], F32)
        C.A2 = gsb("A2", [128, DEPTH, 8, 2], F32)
        C.ident_f = gsb("ident_f", [128, 128], F32)
        C.ident_b = gsb("ident_b", [128, 128], BF16)
        C.onesm = gsb("onesm", [128, 128], BF16)
        C.blk64 = gsb("blk64", [128, 128], BF16)
        C.ropeRT = gsb("ropeRT", [128, 128], BF16)

        phases = []
        phases.append(("init", lambda: ph_init(C)))
        phases.append(("xin", lambda: ph_xin(C)))
        for l in range(DEPTH):
            for g in (1, 0):
                phases.append(("norm1_%d_%d" % (l, g), lambda l=l, g=g: ph_norm(C, l, g, 1)))
                phases.append(("proj_%d_%d" % (l, g), lambda l=l, g=g: ph_proj(C, l, g)))
            for g in (0, 1):
                if g == 0:
                    phases.append(("attn_%d_%d" % (l, g), lambda l=l, g=g: ph_attn(C, l, g)))
                else:
                    phases.append(("attn_%d_%d" % (l, g), lambda l=l, g=g: ph_attn(C, l, g)))
                phases.append(("conv_%d_%d" % (l, g), lambda l=l, g=g: ph_conv(C, l, g)))
                if NOHY:
                    phases.append(("wo_%d_%d" % (l, g), lambda l=l, g=g: ph_wo(C, l, g, zero_hy=True)))
                else:
                    phases.append(("hy_%d_%d" % (l, g), lambda l=l, g=g: ph_hyena(C, l, g)))
                    phases.append(("wo_%d_%d" % (l, g), lambda l=l, g=g: ph_wo(C, l, g)))
                if g == 1:
                    phases.append(("halo3_%d" % l, lambda l=l: ph_halo3(C, l)))
                phases.append(("ffn_%d_%d" % (l, g), lambda l=l, g=g: ph_ffn(C, l, g)))
        phases.append(("final", lambda: ph_final(C)))
        if only is not None:
            with Phase(nc, "ld") as P:
                P.dma("sp", C.pp[:], C.inp("pp"), key="pp", writes=["pp"])
                P.memset("dve", C.hT[:], 0.0, ["h"])
                P.memset("dve", C.xT[:], 0.0, ["x"])
                P.memset("dve", C.modT[:], 0.0, ["m"])
            only(C)
            phases = []
        for name, fn in phases:
            fn()
            if stop_after == name:
                break
        if C.dbg is not None:
            ph_dbg(C)
    C.ncobj = nc
    return nc, C


def ppa(C, name):
    o, n = C.ppoff[name]
    return C.pp[:, o:o + n]


def ph_dbg(C):
    with Phase(C.nc, "dbg") as P:
        src = getattr(C, C.dbgname)
        if hasattr(src, "ap") and not isinstance(src, bass.AP):
            src = src.ap()
        P.dma("pool", C.dbg, src, key="dbg")


def ph_init(C):
    nc = C.nc
    with Phase(nc, "init") as P:
        P.dma("sp", C.pp[:], C.inp("pp"), key="pp", writes=["pp"])
        P.dma("sp", C.ident_f[:], C.inp("ident"), key="idf", writes=["idf"])
        P.dma("sp", C.ident_b[:], C.inp("identb"), key="idb", writes=["idb"])
        P.dma("sp", C.onesm[:], C.inp("onesm"), key="onesm", writes=["onesm"])
        P.dma("sp", C.blk64[:], C.inp("blk64"), key="blk64", writes=["blk64"])
        P.dma("sp", C.ropeRT[:], C.inp("ropeRT"), key="ropeRT", writes=["ropeRT"])
        cond = P.sb([128, 8, 2], F32)
        scb = P.sb([128, 8, 2], BF16)
        P.dma("sp", cond[:], C.inp("condT"), key="cond", writes=["cond"])
        P.act(scb[:], cond[:], AF.Silu, ["cond"], ["scb"])
        wm = [P.sb([128, 8, 512], BF16, "wm") for _ in range(2)]
        psm = P.ps([128, 512])
        for l in range(DEPTH):
            for blk in range(12):
                s = blk % 2
                src = C.inp("w_mod")[l][:, blk * 512:(blk + 1) * 512].rearrange("(kc p) n -> p kc n", p=128)
                P.dma("pool", wm[s][:], src, key="wm%d" % s, writes=["wm%d" % s])
                for mcl in range(4):
                    mc = blk * 4 + mcl
                    for kc in range(8):
                        P.mm(psm[:, mc * 2:mc * 2 + 2], wm[s][:, kc, mcl * 128:(mcl + 1) * 128], scb[:, kc, :],
                             kc == 0, kc == 7, ["wm%d" % s, "scb"], ["psm"])
            bm = ppa(C, "bmod_%d" % l)
            for j in range(2):
                P.tt("dve", C.modT[:, l, :, j], psm[:, 0:96].rearrange("p (m j) -> p m j", j=2)[:, :, j], bm, ALU.add, ["psm", "pp"], ["modT"])
            n1 = ppa(C, "norm1_%d" % l)
            n2 = ppa(C, "norm2_%d" % l)
            for j in range(2):
                P.stt(C.A1[:, l, :, j], C.modT[:, l, 8:16, j], 1.0, n1, ALU.add, ALU.mult, ["modT", "pp"], ["A1"])
                P.stt(C.A2[:, l, :, j], C.modT[:, l, 32:40, j], 1.0, n2, ALU.add, ALU.mult, ["modT", "pp"], ["A2"])


def ph_xin(C):
    nc = C.nc
    with Phase(nc, "xin") as P:
        xin = [P.sb([128, D], F32, "xin") for _ in range(2)]
        pst = [P.ps([128, 512]) for _ in range(4)]
        n = 0
        P.dma("sp", xin[0][:], C.inp("x_tok")[0:128, :], key="xin0", writes=["xin0"])
        for tt in range(16):
            s = tt % 2
            if tt + 1 < 16:
                P.dma("sp", xin[1 - s][:], C.inp("x_tok")[(tt + 1) * 128:(tt + 2) * 128, :], key="xin%d" % (1 - s),
                      writes=["xin%d" % (1 - s)])
            for hh in range(2):
                pb = n % 4
                n += 1
                for kk in range(4):
                    kc = hh * 4 + kk
                    P.tr(pst[pb][:, kk * 128:(kk + 1) * 128], xin[s][:, kc * 128:(kc + 1) * 128], C.ident_f[:],
                         ["xin%d" % s], ["pst%d" % pb])
                out = C.xT[:, hh * 4:hh * 4 + 4, tt * 128:(tt + 1) * 128]
                src = pst[pb][:].rearrange("p (k t) -> p k t", k=4)
                if n % 2:
                    P.cp("act", out, src, ["pst%d" % pb], ["xw%d_%d" % (tt, hh)])
                else:
                    P.cp("dve", out, src, ["pst%d" % pb], ["xw%d_%d" % (tt, hh)])


def ph_norm(C, l, g, which):
    nc = C.nc
    j = g
    with Phase(nc, "nrm") as P:
        sq = [P.sb([128, 8, 512], BF16, "sq") for _ in range(2)]
        tn = [P.sb([128, 8, 512], F32, "tn") for _ in range(2)]
        rs = [P.sb([128, 512], F32, "rs") for _ in range(2)]
        pss = [P.ps([128, 512]) for _ in range(2)]
        eps = ppa(C, "eps")
        for tb in range(2):
            t0 = g * 1024 + tb * 512
            xk = "xT%d" % (t0 // 512)
            P.act(sq[tb][:], C.xT[:, :, t0:t0 + 512], AF.Square, [xk], ["sq%d" % tb])
            for kc in range(8):
                P.mm(pss[tb][:], C.onesm[:], sq[tb][:, kc, :], kc == 0, kc == 7, ["sq%d" % tb], ["pss%d" % tb])
            P.rsqrt(rs[tb][:], pss[tb][:], ["pss%d" % tb], ["rs%d" % tb], eps)
            if which == 1:
                A, B = C.A1, C.modT[:, l, 0:8, :]
            elif which == 2:
                A, B = C.A2, C.modT[:, l, 24:32, :]
            for kc in range(8):
                P.stt(tn[tb][:, kc, :], C.xT[:, kc, t0:t0 + 512], A[:, l, kc, j:j + 1], rs[tb][:], ALU.mult, ALU.mult,
                      [xk, "rs%d" % tb, "A"], ["tn%d_%d" % (tb, kc)])
                P.act(C.hT[:, kc, tb * 512:(tb + 1) * 512], tn[tb][:, kc, :], AF.Identity, ["tn%d_%d" % (tb, kc)],
                      ["hT%d" % tb], bias=B[:, kc, j:j + 1])


def ph_proj(C, l, g):
    nc = C.nc
    with Phase(nc, "prj") as P:
        if g == 0:
            for i_, (src_, dst_) in enumerate(((C.payA, C.gatA), (C.payB, C.gatB), (C.payC, C.gatC))):
                P.S.cc(lambda e, src_=src_, dst_=dst_: e.collective_compute(
                    "AllGather", ALU.bypass, replica_groups=GROUPS4, ins=[src_.ap().opt()], outs=[dst_.ap().opt()]), key="cc%d" % i_)
        wb = [P.sb([128, 8, 512], BF16, "wb") for _ in range(2)]
        ps = [P.ps([128, 512]) for _ in range(8)]
        sq = [P.sb([128, 512], BF16, "sq") for _ in range(2)]
        rs = [P.sb([128, 512], F32, "rs") for _ in range(2)]
        qn = [P.sb([128, 512], F32, "qn") for _ in range(2)]
        qb = [P.sb([128, 512], BF16, "qb") for _ in range(2)]
        o1 = [P.sb([128, 512], F32, "o1") for _ in range(2)]
        stg = [P.sb([128, 512], BF16, "stg") for _ in range(4)]
        sig = P.sb([128, 2, 1024], BF16, "sig")
        vf = [P.sb([128, 512], F32, "vf") for _ in range(2)]
        tokf = [P.sb([128, 4, 128], F32, "tokf") for _ in range(2)]
        tokb = [P.sb([128, 4, 128], BF16, "tokb") for _ in range(2)]
        eps = ppa(C, "eps")
        if g == 1:
            rc = P.sb([128, 1024], F32, "rc")
            rsn = P.sb([128, 1024], F32, "rsn")
            P.dma("sp", rc[:], C.inp("ropec"), key="rc", writes=["rc"])
            P.dma("sp", rsn[:], C.inp("ropes"), key="rsn", writes=["rsn"])
        W = C.inp("w_in")[l]
        cnt = dict(ps=0, stg=0, w=0, k=0)

        def load_block(cols, permq=False):
            s = cnt["w"] % 2
            cnt["w"] += 1
            key = "wb%d" % s
            if permq:
                for hh in range(2):
                    for c in range(4):
                        h0 = (hh * 4 + c) * 64
                        src = W[:, h0:h0 + 64].rearrange("(kc p) d -> p kc d", p=128)
                        dst = wb[s][:, :, c * 128 + hh * 64:c * 128 + hh * 64 + 64]
                        P.dma("pool", dst, src, key=key, writes=[key])
            else:
                n = cols[1] - cols[0]
                src = W[:, cols[0]:cols[1]].rearrange("(kc p) n -> p kc n", p=128)
                P.dma("pool", wb[s][:, :, 0:n], src, key=key, writes=[key])
            return s

        def matmuls(s, cl):
            res = []
            for tb in range(2):
                pi = cnt["ps"] % 4
                cnt["ps"] += 1
                for kc in range(8):
                    P.mm(ps[pi][:], wb[s][:, kc, cl * 128:(cl + 1) * 128], C.hT[:, kc, tb * 512:(tb + 1) * 512],
                         kc == 0, kc == 7, ["wb%d" % s, "hT%d" % tb], ["ps%d" % pi])
                res.append(pi)
            return res

        def stage_out(src_psum_or_none, dst_dram, producer):
            si = cnt["stg"] % 4
            cnt["stg"] += 1
            producer(stg[si][:], "stg%d" % si)
            P.dma("sp", dst_dram, stg[si][:], key="stg%d" % si, reads=["stg%d" % si])

        def headnorm(pi, tb, wname, rope, dst_dram, fp32_out=None):
            k = cnt["k"] % 2
            cnt["k"] += 1
            pk = "ps%d" % pi
            P.act(sq[k][:], ps[pi][:], AF.Square, [pk], ["sq%d" % k])
            p2 = 4 + (cnt["ps"] % 2)
            P.mm(ps[p2][:], C.blk64[:], sq[k][:], True, True, ["sq%d" % k], ["ps%d" % p2])
            P.rsqrt(rs[k][:], ps[p2][:], ["ps%d" % p2], ["rs%d" % k], eps)
            w_ap = ppa(C, wname)
            if not rope:
                if fp32_out is not None:
                    P.stt(fp32_out, ps[pi][:], w_ap, rs[k][:], ALU.mult, ALU.mult, [pk, "rs%d" % k], ["qn%d" % k])
                    stage_out(None, dst_dram, lambda ap, key: P.cp("act", ap, fp32_out, ["qn%d" % k], [key]))
                else:
                    stage_out(None, dst_dram, lambda ap, key: P.stt(ap, ps[pi][:], w_ap, rs[k][:], ALU.mult, ALU.mult,
                                                                      [pk, "rs%d" % k], [key]))
            else:
                P.stt(qn[k][:], ps[pi][:], w_ap, rs[k][:], ALU.mult, ALU.mult, [pk, "rs%d" % k], ["qn%d" % k])
                P.cp("act", qb[k][:], qn[k][:], ["qn%d" % k], ["qb%d" % k])
                p3 = 6 + (cnt["ps"] % 2)
                P.mm(ps[p3][:], C.ropeRT[:], qb[k][:], True, True, ["qb%d" % k], ["ps%d" % p3])
                P.tt("pool", o1[k][:], qn[k][:], rc[:, tb * 512:(tb + 1) * 512], ALU.mult, ["qn%d" % k, "rc"], ["o1%d" % k])
                P.tt("dve", qn[k][:], ps[p3][:], rsn[:, tb * 512:(tb + 1) * 512], ALU.mult, ["ps%d" % p3, "rsn"], ["qn%d" % k])
                stage_out(None, dst_dram, lambda ap, key: P.tt("dve", ap, qn[k][:], o1[k][:], ALU.add,
                                                                 ["qn%d" % k, "o1%d" % k], [key]))

        def to_tokmajor(src_f32, srckey, tb, out_dram_f32, out_dram_bf, tag="v"):
            k = cnt["k"] % 2
            cnt["k"] += 1
            p3 = 6 + k
            for i in range(4):
                P.tr(ps[p3][:, i * 128:(i + 1) * 128], src_f32[:, i * 128:(i + 1) * 128], C.ident_f[:], [srckey], ["ps%d" % p3])
            pv = ps[p3][:].rearrange("p (i f) -> p i f", i=4)
            if out_dram_f32 is not None and os.environ.get("SKIP_NK", "") != tag:
                P.cp("act", tokf[k][:], pv, ["ps%d" % p3], ["tokf%d" % k])
                for si, dd in enumerate(out_dram_f32):
                    P.dma("sp", dd, tokf[k][:, 2 * si:2 * si + 2, :], key="tokf%d" % k, reads=["tokf%d" % k])
            if out_dram_bf is not None:
                P.cp("dve", tokb[k][:], pv, ["ps%d" % p3], ["tokb%d" % k])
                P.dma("sp", out_dram_bf, tokb[k][:], key="tokb%d" % k, reads=["tokb%d" % k])

        s = load_block(None, permq=True)
        s_nxt = load_block((1024, 1536))
        for c in range(4):
            pis = matmuls(s, c)
            for tb, pi in enumerate(pis):
                headnorm(pi, tb, "qw_%d" % l, g == 1, C.qT_d[g][c * 128:(c + 1) * 128, tb * 512:(tb + 1) * 512])
        s = s_nxt
        s_nxt = load_block((512, 1024))
        for cl in range(4):
            pis = matmuls(s, cl)
            for tb, pi in enumerate(pis):
                pk = "ps%d" % pi
                if cl < 2:
                    P.act(sig[:, cl, tb * 512:(tb + 1) * 512], ps[pi][:], AF.Sigmoid, [pk], ["sig"])
                else:
                    zc = cl - 2
                    if g == 0:
                        dst = C.zhT_d[zc * 128:(zc + 1) * 128, tb * 512:(tb + 1) * 512]
                    else:
                        dst = C.pay_zh(zc)[:, tb * 512:(tb + 1) * 512]
                    stage_out(None, dst, lambda ap, key, pi=pi, pk=pk: P.cp("act", ap, ps[pi][:], [pk], [key]))
        s = s_nxt
        s_nxt = load_block((1536, 2048))
        for cl in range(4):
            pis = matmuls(s, cl)
            for tb, pi in enumerate(pis):
                pk = "ps%d" % pi
                if cl == 0:
                    if g == 0:
                        kk = cnt["k"] % 2
                        headnorm(pi, tb, "kw_%d" % l, False, C.kT_d[:, tb * 512:(tb + 1) * 512], fp32_out=qn[kk][:])
                        for sq_ in range(2):
                            pass
                        dst = [C.new_k[2 * tb + si, l].rearrange("(i p) f -> p i f", p=128) for si in range(2)]
                        to_tokmajor(qn[kk], "qn%d" % kk, tb, dst, None, tag="k")
                    else:
                        headnorm(pi, tb, "kw_%d" % l, True, C.pay_kT[:, tb * 512:(tb + 1) * 512])
                elif cl == 1:
                    kk = cnt["k"] % 2
                    P.cp("act", vf[kk][:], ps[pi][:], [pk], ["vf%d" % kk])
                    if g == 0:
                        dstf = [C.new_v[2 * tb + si, l].rearrange("(i p) f -> p i f", p=128) for si in range(2)]
                        dstb = C.vtok_d[tb * 512:(tb + 1) * 512, :].rearrange("(i p) f -> p i f", p=128)
                        to_tokmajor(vf[kk], "vf%d" % kk, tb, dstf, dstb)
                    else:
                        dstb = C.pay_V[tb * 512:(tb + 1) * 512, :].rearrange("(i p) f -> p i f", p=128)
                        to_tokmajor(vf[kk], "vf%d" % kk, tb, None, dstb)
                else:
                    cc = cl - 2
                    dst = C.uT_d[g][cc * 128:(cc + 1) * 128, tb * 512:(tb + 1) * 512]
                    si_ = cnt["stg"] % 4
                    stage_out(None, dst, lambda ap, key, pi=pi, pk=pk, cc=cc, tb=tb: P.tt(
                        "dve", ap, ps[pi][:], sig[:, cc, tb * 512:(tb + 1) * 512], ALU.mult, [pk, "sig"], [key]))
                    if g == 1:
                        ub = C.pay_ub
                        if tb == 0:
                            P.dma("sp", ub[cc * 128:(cc + 1) * 128, 0:16], stg[si_][:, 0:16], key="stg%d" % si_, reads=["stg%d" % si_])
                        else:
                            P.dma("sp", ub[cc * 128:(cc + 1) * 128, 16:32], stg[si_][:, 496:512], key="stg%d" % si_, reads=["stg%d" % si_])
        s = s_nxt
        for cl in range(4):
            pis = matmuls(s, cl)
            for tb, pi in enumerate(pis):
                pk = "ps%d" % pi
                zc = 2 + cl
                if g == 0:
                    dst = C.zhT_d[zc * 128:(zc + 1) * 128, tb * 512:(tb + 1) * 512]
                else:
                    dst = C.pay_zh(zc)[:, tb * 512:(tb + 1) * 512]
                stage_out(None, dst, lambda ap, key, pi=pi, pk=pk: P.cp("act", ap, ps[pi][:], [pk], [key]))


GROUPS4 = [[0, 1, 2, 3], [4, 5, 6, 7]]


def ph_ag(C, src, dst, name):
    with Phase(C.nc, name) as P:
        P.S.cc(lambda e: e.collective_compute("AllGather", ALU.bypass, replica_groups=GROUPS4,
                                              ins=[src.ap().opt()], outs=[dst.ap().opt()]), key="cc")


def ph_attn(C, l, g, heads=range(8), with_ag=False):
    nc = C.nc
    with Phase(nc, "att") as P:
        nkt = 8 if g == 0 else 34
        psS = [P.ps([128, 512]) for _ in range(6)]
        psO = [P.ps([128, 512]) for _ in range(2)]
        if with_ag:
            for i_, (src_, dst_) in enumerate(((C.payA, C.gatA), (C.payB, C.gatB), (C.payC, C.gatC))):
                P.S.cc(lambda e, src_=src_, dst_=dst_: e.collective_compute(
                    "AllGather", ALU.bypass, replica_groups=GROUPS4, ins=[src_.ap().opt()], outs=[dst_.ap().opt()]), key="cc%d" % i_)
        qT = P.sb([128, 4, 1024], BF16, "qT")
        qz = [P.sb([128, 4, 1024], BF16, "qz") for _ in range(2)]
        kT = P.sb([128, nkt * 128], BF16, "kT")
        va = P.sb([128, nkt, 2, 128], BF16, "va")
        P.dma("sp", qT[:], C.qT_d[g].rearrange("(c p) t -> p c t", p=128), key="qT", writes=["qT"])
        P.memset("dve", qz[0][64:128].rearrange("p a b -> p (a b)"), 0.0, ["qz0"])
        P.memset("dve", qz[1][0:64].rearrange("p a b -> p (a b)"), 0.0, ["qz1"])
        P.cp("dve", qz[0][0:64], qT[0:64], ["qT"], ["qz0b"])
        P.cp("act", qz[1][64:128], qT[64:128], ["qT"], ["qz1b"])
        for t0_ in range(0, nkt, 8):
            t1_ = min(nkt, t0_ + 8)
            P.memset("dve", va[:, t0_:t1_].rearrange("p a b c -> p (a b c)"), 1.0, ["va"])
        if g == 0:
            P.dma("sp", kT[:], C.kT_d, key="kT", writes=["kT"])
            for kv in range(2):
                P.dma("sp", va[:, :, kv, 0:64], C.vtok_d[:, kv * 64:(kv + 1) * 64].rearrange("(i p) d -> p i d", p=128),
                      key="va", writes=["va"])
        else:
            for r in range(4):
                P.dma("sp", kT[:, r * 1024:(r + 1) * 1024], C.gat_kT(r), key="kT", writes=["kT"])
            for r in range(4):
                vreg = C.gat_V(r)
                for kv in range(2):
                    P.dma("sp", va[:, r * 8:(r + 1) * 8, kv, 0:64],
                          vreg[:, kv * 64:(kv + 1) * 64].rearrange("(i p) d -> p i d", p=128), key="va", writes=["va"])
            ck = P.sb([128, 2, 128], F32, "ck")
            P.dma("sp", ck[:], C.inp("cache_k")[l].rearrange("(i p) f -> p i f", p=128), key="ck", writes=["ck"])
            pck = psO[0]
            for i in range(2):
                P.tr(pck[:, i * 128:(i + 1) * 128], ck[:, i, :], C.ident_f[:], ["ck"], ["psO0"])
            P.cp("dve", kT[:, 4096:4352], pck[:, 0:256], ["psO0"], ["kT"])
            for kv in range(2):
                P.dma("pool", va[:, 32:34, kv, 0:64],
                      C.inp("cache_v")[l][:, kv * 64:(kv + 1) * 64].rearrange("(i p) d -> p i d", p=128), key="va2", writes=["va"])
        pT = [P.sb([128, 512], BF16, "pT") for _ in range(6)]
        dn = [P.sb([64, 512], F32, "dn") for _ in range(2)]
        it = dict(s=0, o=0, p=0, b=0)

        NB = 3

        def one(h, qsl, n, ktiles, dst):
            c, kvh = h % 4, h // 4
            r0 = kvh * 64
            oi = it["o"] % 2
            it["o"] += 1
            batches = [ktiles[i:i + NB] for i in range(0, len(ktiles), NB)]

            def emit_s_batch(bi):
                res = []
                par = it["b"] % 2
                it["b"] += 1
                for kt in batches[bi]:
                    si = it["s"] % 6
                    it["s"] += 1
                    pi = it["p"] % 6
                    it["p"] += 1
                    P.mm(psS[si][:, 0:n], kT[:, kt * 128:(kt + 1) * 128], qz[kvh][:, c, qsl], True, True,
                         ["kT", "qz0", "qz1", "qz0b", "qz1b"], ["psS%d" % si])
                    res.append((si, pi, kt))
                for (si, pi, kt) in res:
                    P.act(pT[pi][:, 0:n], psS[si][:, 0:n], AF.Exp, ["psS%d" % si], ["pT%d" % pi, "pTb%d" % par], scale=0.125)
                return res, par
            pend = emit_s_batch(0)
            for bi in range(len(batches)):
                nxt = emit_s_batch(bi + 1) if bi + 1 < len(batches) else None
                res, par = pend
                for j, (si, pi, kt) in enumerate(res):
                    first = (bi == 0 and j == 0)
                    last = (bi == len(batches) - 1 and j == len(res) - 1)
                    P.mm(psO[oi][:, 0:n], va[:, kt, kvh, :], pT[pi][:, 0:n], first, last,
                         ["va", "pT%d" % pi, "pTb%d" % par], ["psO%d" % oi])
                pend = nxt
            P.cp("dve", dn[oi][:, 0:n], psO[oi][64:128, 0:n], ["psO%d" % oi], ["dn%d" % oi])
            P.act(dn[oi][:, 0:n], dn[oi][:, 0:n], AF.Ln, ["dn%d" % oi], ["dn%d" % oi])
            P.act(dn[oi][:, 0:n], dn[oi][:, 0:n], AF.Exp, ["dn%d" % oi], ["dn%d" % oi], scale=-1.0)
            P.tt("dve", dst, psO[oi][0:64, 0:n], dn[oi][:, 0:n], ALU.mult, ["psO%d" % oi, "dn%d" % oi], ["mixA"])

        for h in heads:
            c, kvh = h % 4, h // 4
            r0 = kvh * 64
            if g == 0:
                for sq_ in range(4):
                    one(h, slice(sq_ * 256, (sq_ + 1) * 256), 256, [sq_ * 2, sq_ * 2 + 1],
                        C.mixT[r0:r0 + 64, c, sq_ * 256:(sq_ + 1) * 256])
            else:
                for qb in range(2):
                    one(h, slice(qb * 512, (qb + 1) * 512), 512, list(range(34)),
                        C.mixT[r0:r0 + 64, c, qb * 512:(qb + 1) * 512])


def ph_conv(C, l, g):
    nc = C.nc
    with Phase(nc, "cnv") as P:
        o, n = C.ppoff["cdw_%d" % l]
        dg = P.sb([128, 2, 31, 128], BF16, "dg")
        for cc in range(2):
            for k in range(31):
                P.ts("dve", dg[:, cc, k, :], C.ident_b[:], C.pp[:, o + cc * 31 + k:o + cc * 31 + k + 1], ALU.mult, [], ["dg"])
        cdb = ppa(C, "cdb_%d" % l); gnw = ppa(C, "gnw_%d" % l); gnb = ppa(C, "gnb_%d" % l); pwb = ppa(C, "pwb_%d" % l)
        eps = ppa(C, "eps")
        pww = P.sb([128, 2, 256], BF16, "pww")
        P.dma("pool", pww[:], C.inp("conv_pw_w")[l].rearrange("(kc p) n -> p kc n", p=128), key="pww", writes=["pww"])
        if g == 0:
            up = P.sb([128, 2, 4, 286], BF16, "up")
            P.memset("pool", up[:], 0.0, ["up"])
            for cc in range(2):
                P.dma("sp", up[:, cc, :, 15:271], C.uT_d[0][cc * 128:(cc + 1) * 128, :].rearrange("p (s t) -> p s t", s=4),
                      key="up", writes=["up"])
        else:
            up = P.sb([128, 2, 1054], BF16, "up")
            P.memset("pool", up[:], 0.0, ["up"])
            for cc in range(2):
                P.dma("sp", up[:, cc, 15:1039], C.uT_d[1][cc * 128:(cc + 1) * 128, :], key="up", writes=["up"])
            ubg = P.sb([128, 2, 4, 32], BF16, "ubg")
            for r in range(4):
                ub = C.gat_ub(r)
                for cc in range(2):
                    P.dma("sp", ubg[:, cc, r, :], ub[cc * 128:(cc + 1) * 128, :], key="ubg", writes=["ubg"])
            el = ppa(C, "el"); er = ppa(C, "er")
            for r in range(4):
                for (dst, src, ee) in ((up[:, :, 0:15], ubg[:, :, r, 17:32], el), (up[:, :, 1039:1054], ubg[:, :, r, 0:15], er)):
                    P.stt(dst, src, ee[:, r:r + 1], dst, ALU.mult, ALU.add, ["ubg", "up"], ["up"])
        sbf = P.sb([128, 2, 1024], BF16, "sbf")
        pc = [P.ps([128, 512]) for _ in range(2)]
        pm = [P.ps([128, 512]) for _ in range(4)]
        yf = [P.sb([128, 512], F32, "yf") for _ in range(2)]
        yb = [P.sb([128, 512], BF16, "yb") for _ in range(2)]
        ysq = [P.sb([128, 512], BF16, "ysq") for _ in range(2)]
        msq = [P.sb([128, 512], F32, "msq") for _ in range(2)]
        var = [P.sb([128, 512], F32, "var") for _ in range(2)]
        n_ = 0
        for cc in range(2):
            for blk in range(2):
                b = n_ % 2
                n_ += 1
                if g == 0:
                    for half in range(2):
                        sq_ = blk * 2 + half
                        for k in range(31):
                            P.mm(pc[b][:, half * 256:(half + 1) * 256], dg[:, cc, k, :], up[:, cc, sq_, k:k + 256], k == 0, k == 30,
                                 ["dg", "up"], ["psc%d" % b])
                else:
                    for k in range(31):
                        P.mm(pc[b][:], dg[:, cc, k, :], up[:, cc, blk * 512 + k:blk * 512 + k + 512], k == 0, k == 30,
                             ["dg", "up"], ["psc%d" % b])
                P.act(yf[b][:], pc[b][:], AF.Identity, ["psc%d" % b], ["yf%d" % b], bias=cdb[:, cc:cc + 1])
                P.cp("dve", yb[b][:], yf[b][:], ["yf%d" % b], ["yb%d" % b])
                P.act(ysq[b][:], yf[b][:], AF.Square, ["yf%d" % b], ["ysq%d" % b])
                P.mm(pm[2 * b][:], C.blk64[:], yb[b][:], True, True, ["yb%d" % b], ["psm%d" % (2 * b)])
                P.mm(pm[2 * b + 1][:], C.blk64[:], ysq[b][:], True, True, ["ysq%d" % b], ["psm%d" % (2 * b + 1)])
                P.act(msq[b][:], pm[2 * b][:], AF.Square, ["psm%d" % (2 * b)], ["msq%d" % b])
                P.tt("dve", var[b][:], pm[2 * b + 1][:], msq[b][:], ALU.subtract, ["psm%d" % (2 * b + 1), "msq%d" % b], ["var%d" % b])
                P.rsqrt(var[b][:], var[b][:], ["var%d" % b], ["var%d" % b], eps)
                P.tt("dve", yf[b][:], yf[b][:], pm[2 * b][:], ALU.subtract, ["yf%d" % b, "psm%d" % (2 * b)], ["yf%d" % b])
                P.tt("pool", yf[b][:], yf[b][:], var[b][:], ALU.mult, ["yf%d" % b, "var%d" % b], ["yf%d" % b])
                P.act(sbf[:, cc, blk * 512:(blk + 1) * 512], yf[b][:], AF.Silu, ["yf%d" % b], ["sbf"],
                      scale=gnw[:, cc:cc + 1], bias=gnb[:, cc:cc + 1])
        for m in range(2):
            for blk in range(2):
                b = n_ % 2
                n_ += 1
                for cc in range(2):
                    P.mm(pc[b][:], pww[:, cc, m * 128:(m + 1) * 128], sbf[:, cc, blk * 512:(blk + 1) * 512], cc == 0, cc == 1,
                         ["pww", "sbf"], ["psc%d" % b])
                P.act(C.mixT[:, 4 + m, blk * 512:(blk + 1) * 512], pc[b][:], AF.Identity, ["psc%d" % b], ["mixC"],
                      bias=pwb[:, m:m + 1])


def ph_wo(C, l, g, zero_hy=False, with_norm2=True):
    nc = C.nc
    j = g
    with Phase(nc, "wo") as P:
        wo = P.sb([128, 8, 1024], BF16, "wo")
        W = C.inp("w_o")[l]
        for cb in range(2):
            cs_ = slice(cb * 512, (cb + 1) * 512)
            for c in range(4):
                for hh in range(2):
                    P.dma("pool", wo[hh * 64:(hh + 1) * 64, c, cs_], W[(hh * 4 + c) * 64:(hh * 4 + c) * 64 + 64, cs_],
                          key="wo%d" % cb, writes=["wo%d" % cb])
            P.dma("pool", wo[:, 4:8, cs_], W[512:1024, cs_].rearrange("(kc p) n -> p kc n", p=128), key="wo%d" % cb, writes=["wo%d" % cb])
        if zero_hy:
            P.memset("dve", C.mixT[:, 6:8, :], 0.0, ["mix"])
        ps = [P.ps([128, 512]) for _ in range(4)]
        sq = [P.sb([128, 8, 512], BF16, "sq") for _ in range(2)]
        tn = [P.sb([128, 8, 512], F32, "tn") for _ in range(2)]
        rs = [P.sb([128, 512], F32, "rs") for _ in range(2)]
        pss = [P.ps([128, 512]) for _ in range(2)]
        eps = ppa(C, "eps")
        n_ = 0
        for tb in range(2):
            t0 = g * 1024 + tb * 512
            xk = "xT%d" % (t0 // 512)
            for m in range(8):
                b = n_ % 4
                n_ += 1
                for kc in range(8):
                    P.mm(ps[b][:], wo[:, kc, m * 128:(m + 1) * 128], C.mixT[:, kc, tb * 512:(tb + 1) * 512], kc == 0, kc == 7,
                         ["wo%d" % (m // 4), "mix"], ["ps%d" % b])
                xs = C.xT[:, m, t0:t0 + 512]
                P.stt(xs, ps[b][:], C.modT[:, l, 16 + m, j:j + 1], xs, ALU.mult, ALU.add, ["ps%d" % b], [xk])
            if with_norm2:
                P.act(sq[tb][:], C.xT[:, :, t0:t0 + 512], AF.Square, [xk], ["sq%d" % tb])
                for kc in range(8):
                    P.mm(pss[tb][:], C.onesm[:], sq[tb][:, kc, :], kc == 0, kc == 7, ["sq%d" % tb], ["pss%d" % tb])
                P.rsqrt(rs[tb][:], pss[tb][:], ["pss%d" % tb], ["rs%d" % tb], eps)
                for kc in range(8):
                    P.stt(tn[tb][:, kc, :], C.xT[:, kc, t0:t0 + 512], C.A2[:, l, kc, j:j + 1], rs[tb][:], ALU.mult, ALU.mult,
                          [xk, "rs%d" % tb], ["tn%d_%d" % (tb, kc)])
                    P.act(C.hT[:, kc, tb * 512:(tb + 1) * 512], tn[tb][:, kc, :], AF.Identity, ["tn%d_%d" % (tb, kc)],
                          ["hT%d" % tb], bias=C.modT[:, l, 24 + kc, j:j + 1])


def ph_halo3(C, l):
    nc = C.nc
    with Phase(nc, "h3a") as P:
        hb = P.sb([128, 1024], BF16, "hb")
        P.memset("dve", hb[:], 0.0, ["hb"])
        P.cp("dve", hb[:, 0:8], C.hT[:, :, 0:1].rearrange("p k e -> p (k e)"), ["hb"], ["hb"])
        P.cp("dve", hb[:, 8:16], C.hT[:, :, 1023:1024].rearrange("p k e -> p (k e)"), ["hb"], ["hb"])
        P.dma("sp", C.pay3.ap(), hb[:], key="hb", reads=["hb"])
    ph_ag(C, C.pay3, C.gat3, "ag3")
    with Phase(nc, "h3b") as P:
        hbg = P.sb([128, 4, 16], F32, "hbg")
        for r in range(4):
            P.dma("pool", hbg[:, r, :], C.gat3.ap()[r * 128:(r + 1) * 128, 0:16], key="hbg%d" % r, writes=["hbg"])
        el = ppa(C, "el"); er = ppa(C, "er")
        acc = P.sb([128, 2, 8], F32, "acc")
        P.memset("dve", acc[:], 0.0, ["acc"])
        for r in range(4):
            P.stt(acc[:, 0, :], hbg[:, r, 8:16], el[:, r:r + 1], acc[:, 0, :], ALU.mult, ALU.add, ["hbg", "acc"], ["acc"])
            P.stt(acc[:, 1, :], hbg[:, r, 0:8], er[:, r:r + 1], acc[:, 1, :], ALU.mult, ALU.add, ["hbg", "acc"], ["acc"])
        P.cp("dve", C.hT[:, :, 1024:1025].rearrange("p k e -> p (k e)"), acc[:, 0, :], ["acc"], ["hTh"])
        P.cp("dve", C.hT[:, :, 1025:1026].rearrange("p k e -> p (k e)"), acc[:, 1, :], ["acc"], ["hTh"])


def ph_ffn(C, l, g):
    nc = C.nc
    j = g
    with Phase(nc, "ffn") as P:
        o_w, _ = C.ppoff["fdw_%d" % l]
        o_b, _ = C.ppoff["fdb_%d" % l]
        wsc = lambda ch, k: C.pp[:, o_w + ch * 3 + k:o_w + ch * 3 + k + 1]
        bsc = lambda ch: C.pp[:, o_b + ch:o_b + ch + 1]
        Wu = C.inp("ffn_w_up")[l]
        Wd = C.inp("ffn_w_down")[l]
        gT = P.sb([128, 11, 1024], BF16, "gT")
        wdn = P.sb([128, 11, 1024], BF16, "wdn")
        wupV = [P.sb([128, 8, 512], BF16, "wupV") for _ in range(2)]
        wupG = [P.sb([128, 8, 512], BF16, "wupG") for _ in range(2)]
        cv = [P.sb([128, 1024], F32, "cv") for _ in range(2)]
        cg = [P.sb([128, 1024], F32, "cg") for _ in range(2)]
        sg = [P.sb([128, 1024], F32, "sg") for _ in range(2)]
        pt = [[P.ps([128, 512]) for _ in range(2)] for _ in range(3)]
        ph = [P.ps([128, 512]) for _ in range(2)]
        rot = 0
        for half in range(2):
            for jj in range(11):
                jf = half * 11 + jj
                s = jf % 2
                if jj == 1:
                    P.dma("pool", wdn[:], Wd[half * 1408:(half + 1) * 1408, :].rearrange("(j p) n -> p j n", p=128), key="wdn",
                          writes=["wdn"])
                if jj in (0, 4, 8):
                    bi = half * 3 + jj // 4
                    wb_ = bi % 2

                    def issue_block(bk):
                        h_, q_ = bk // 3, bk % 3
                        j0 = h_ * 11 + q_ * 4
                        nj = 4 if q_ < 2 else 3
                        sl_ = bk % 2
                        P.dma("pool", wupV[sl_][:, :, 0:nj * 128], Wu[:, j0 * 128:(j0 + nj) * 128].rearrange("(kc p) n -> p kc n", p=128),
                              key="wupV%d" % sl_, writes=["wup%d" % sl_])
                        P.dma("pool", wupG[sl_][:, :, 0:nj * 128],
                              Wu[:, FFN + j0 * 128:FFN + (j0 + nj) * 128].rearrange("(kc p) n -> p kc n", p=128),
                              key="wupG%d" % sl_, writes=["wup%d" % sl_])
                    if bi == 0:
                        issue_block(0)
                    if bi + 1 < 6:
                        issue_block(bi + 1)
                wcol = (jj % 4) * 128 if jj < 8 else (jj - 8) * 128
                wkey = "wup%d" % wb_
                tv = rot % 3
                tg = (rot + 1) % 3
                rot += 2
                for (tt_, wsrc) in ((tv, wupV[wb_]), (tg, wupG[wb_])):
                    for tb in range(2):
                        for kc in range(8):
                            P.mm(pt[tt_][tb][:], wsrc[:, kc, wcol:wcol + 128], C.hT[:, kc, tb * 512:(tb + 1) * 512],
                                 kc == 0, kc == 7, [wkey, "hT0", "hT1"], ["pst%d" % tt_])
                if g == 1:
                    for (wsrc, col) in ((wupV[wb_], 0), (wupG[wb_], 2)):
                        for kc in range(8):
                            P.mm(ph[s][:, col:col + 2], wsrc[:, kc, wcol:wcol + 128], C.hT[:, kc, 1024:1026], kc == 0, kc == 7,
                                 [wkey, "hTh"], ["psh%d" % s])
                for (tt_, cbuf, ch, hc) in ((tv, cv[s], jf, 0), (tg, cg[s], NJ + jf, 2)):
                    ck = "c%d_%d" % (hc, s)
                    pk = "pst%d" % tt_
                    for tb in range(2):
                        cb_ = cbuf[:, tb * 512:(tb + 1) * 512]
                        pb_ = pt[tt_][tb][:]
                        P.act(cb_, pb_, AF.Identity, [pk], [ck], scale=wsc(ch, 1), bias=bsc(ch))
                        if g == 0:
                            c3 = cb_.rearrange("p (s t) -> p s t", s=2)
                            p3 = pb_.rearrange("p (s t) -> p s t", s=2)
                            P.stt(c3[:, :, 1:256], p3[:, :, 0:255], wsc(ch, 0), c3[:, :, 1:256], ALU.mult, ALU.add, [pk, ck], [ck])
                            P.stt(c3[:, :, 0:255], p3[:, :, 1:256], wsc(ch, 2), c3[:, :, 0:255], ALU.mult, ALU.add, [pk, ck], [ck])
                        else:
                            P.stt(cb_[:, 1:512], pb_[:, 0:511], wsc(ch, 0), cb_[:, 1:512], ALU.mult, ALU.add, [pk, ck], [ck])
                            P.stt(cb_[:, 0:511], pb_[:, 1:512], wsc(ch, 2), cb_[:, 0:511], ALU.mult, ALU.add, [pk, ck], [ck])
                    if g == 1:
                        P.stt(cbuf[:, 512:513], pt[tt_][0][:, 511:512], wsc(ch, 0), cbuf[:, 512:513], ALU.mult, ALU.add, [pk, ck], [ck])
                        P.stt(cbuf[:, 511:512], pt[tt_][1][:, 0:1], wsc(ch, 2), cbuf[:, 511:512], ALU.mult, ALU.add, [pk, ck], [ck])
                        P.stt(cbuf[:, 0:1], ph[s][:, hc:hc + 1], wsc(ch, 0), cbuf[:, 0:1], ALU.mult, ALU.add, ["psh%d" % s, ck], [ck])
                        P.stt(cbuf[:, 1023:1024], ph[s][:, hc + 1:hc + 2], wsc(ch, 2), cbuf[:, 1023:1024], ALU.mult, ALU.add,
                              ["psh%d" % s, ck], [ck])
                P.act(sg[s][:], cg[s][:], AF.Silu, ["c2_%d" % s], ["sg%d" % s])
                P.tt("dve", gT[:, jj, :], cv[s][:], sg[s][:], ALU.mult, ["c0_%d" % s, "sg%d" % s], ["gT"])
            n_ = 0
            for m in range(8):
                for tb in range(2):
                    tt_ = n_ % 3
                    n_ += 1
                    for jj in range(11):
                        P.mm(pt[tt_][0][:], wdn[:, jj, m * 128:(m + 1) * 128], gT[:, jj, tb * 512:(tb + 1) * 512], jj == 0, jj == 10,
                             ["wdn", "gT"], ["pst%d" % tt_])
                    t0 = g * 1024 + tb * 512
                    xs = C.xT[:, m, t0:t0 + 512]
                    P.stt(xs, pt[tt_][0][:], C.modT[:, l, 40 + m, j:j + 1], xs, ALU.mult, ALU.add, ["pst%d" % tt_],
                          ["xT%d" % (t0 // 512)])


def ph_final(C):
    nc = C.nc
    with Phase(nc, "fin") as P:
        sq = [P.sb([128, 8, 512], BF16, "sq") for _ in range(2)]
        yT = [P.sb([128, 8, 512], F32, "yT") for _ in range(2)]
        rs = [P.sb([128, 512], F32, "rs") for _ in range(2)]
        yo = [P.sb([128, 1024], F32, "yo") for _ in range(2)]
        pss = [P.ps([128, 512]) for _ in range(2)]
        ptr = [P.ps([128, 512]) for _ in range(4)]
        eps = ppa(C, "eps")
        fn = ppa(C, "fnorm")
        n_ = 0
        for tb4 in range(4):
            b = tb4 % 2
            t0 = tb4 * 512
            xk = "xT%d" % tb4
            P.act(sq[b][:], C.xT[:, :, t0:t0 + 512], AF.Square, [xk], ["sq%d" % b])
            for kc in range(8):
                P.mm(pss[b][:], C.onesm[:], sq[b][:, kc, :], kc == 0, kc == 7, ["sq%d" % b], ["pss%d" % b])
            P.rsqrt(rs[b][:], pss[b][:], ["pss%d" % b], ["rs%d" % b], eps)
            for kc in range(8):
                P.stt(yT[b][:, kc, :], C.xT[:, kc, t0:t0 + 512], fn[:, kc:kc + 1], rs[b][:], ALU.mult, ALU.mult,
                      [xk, "rs%d" % b], ["yT%d" % b])
            for ti in range(4):
                yb_ = n_ % 2
                for hh in range(2):
                    pb = n_ % 4 if False else (2 * yb_ + hh)
                    for kk in range(4):
                        kc = hh * 4 + kk
                        P.tr(ptr[pb][:, kk * 128:(kk + 1) * 128], yT[b][:, kc, ti * 128:(ti + 1) * 128], C.ident_f[:],
                             ["yT%d" % b], ["pstr%d" % pb])
                    if hh == 0:
                        P.cp("act", yo[yb_][:, 0:512], ptr[pb][:], ["pstr%d" % pb], ["yo%d" % yb_])
                    else:
                        P.cp("dve", yo[yb_][:, 512:1024], ptr[pb][:], ["pstr%d" % pb], ["yo%d" % yb_])
                r0 = t0 + ti * 128
                P.dma("sp", C.y_tok[r0:r0 + 128, :], yo[yb_][:], key="yo%d" % yb_, reads=["yo%d" % yb_])
                n_ += 1


TWO_PI = 2.0 * math.pi
MAGIC = 12582912.0


def filt_mlp(P, C, l, zT, L, hid2):
    w1 = P.sb([33, 64], F32, "w1")
    w2 = P.sb([64, 64], F32, "w2")
    P.dma("sp", w1[:], C.inp("hy_f_w1")[l], key="w1", writes=["w1"])
    P.dma("sp", w2[:], C.inp("hy_f_w2")[l], key="w2", writes=["w2"])
    fr = ppa(C, "hfr_%d" % l)[0:64, :]
    fb = P.sb([64, 2], F32, "fb")
    P.tt("dve", fb[:, 0:1], ppa(C, "hfb1_%d" % l)[0:64, :], fr, ALU.mult, [], ["fb"])
    P.tt("dve", fb[:, 1:2], ppa(C, "hfb2_%d" % l)[0:64, :], fr, ALU.mult, [], ["fb"])
    hid1 = P.sb([64, L], F32, "hid1")
    a = P.sb([64, 512], F32, "fa")
    t1 = P.sb([64, 512], F32, "ft1")
    pf = P.ps([128, 512])
    for (wt, src, dst, col) in ((w1, zT, hid1, 0), (w2, hid1, hid2, 1)):
        kk = 33 if col == 0 else 64
        for b0 in range(0, L, 512):
            n = min(512, L - b0)
            P.mm(pf[0:64, 0:n], wt[0:kk, :], src[0:kk, b0:b0 + n], True, True, ["w1", "w2", "zT", "hid1"], ["psf"])
            P.ts("dve", a[:, 0:n], pf[0:64, 0:n], fr, ALU.mult, ["psf", "fb"], ["fa"], s2=fb[:, col:col + 1], op1=ALU.add)
            P.ts("dve", t1[:, 0:n], a[:, 0:n], 1.0 / TWO_PI, ALU.mult, ["fa"], ["ft1"], s2=MAGIC, op1=ALU.add)
            P.ts("dve", t1[:, 0:n], t1[:, 0:n], MAGIC, ALU.subtract, ["ft1"], ["ft1"])
            P.stt(a[:, 0:n], t1[:, 0:n], -TWO_PI, a[:, 0:n], ALU.mult, ALU.add, ["ft1", "fa"], ["fa"])
            P.ts("dve", a[:, 0:n], a[:, 0:n], -3.14159, ALU.max, ["fa"], ["fa"], s2=3.14159, op1=ALU.min)
            P.act(dst[:, b0:b0 + n], a[:, 0:n], AF.Sin, ["fa"], ["hid1" if col == 0 else "hid2"])


def ph_hyena(C, l, g):
    if g == 0:
        ph_hyena_p(C, l)
    else:
        ph_hyena_s(C, l)


def ph_hyena_p(C, l):
    nc = C.nc
    with Phase(nc, "hyp") as P:
        dC = P.sb([128, 2, 256], BF16, "dC"); dS = P.sb([128, 2, 256], BF16, "dS"); dSn = P.sb([128, 2, 256], BF16, "dSn")
        iC = P.sb([128, 2, 256], BF16, "iC"); iSn = P.sb([128, 2, 256], BF16, "iSn")
        dAlt = P.sb([128, 2], BF16, "dAlt")
        for t_, nm in ((dC, "dftC"), (dS, "dftS"), (dSn, "dftSn"), (iC, "idftC"), (iSn, "idftSn"), (dAlt, "dftAlt")):
            P.dma("sp", t_[:], C.inp(nm), key="cst", writes=["cst"])
        zT = P.sb([33, 256], F32, "zT")
        P.dma("sp", zT[:], C.inp("zT256"), key="zT", writes=["zT"])
        hid2 = P.sb([64, 256], F32, "hid2")
        filt_mlp(P, C, l, zT, 256, hid2)
        w3 = P.sb([64, 1024], F32, "w3")
        b3 = P.sb([1, 1024], F32, "b3")
        onesr = P.sb([1, 128], F32, "onesr")
        brow = P.sb([1, 512], F32, "brow")
        dec = P.sb([128, 2, 256], F32, "dec")
        P.dma("sp", w3[:], C.inp("hy_f_w3")[l], key="w3", writes=["w3"])
        P.dma("sp", b3[:], C.inp("hy_f_b3r")[l], key="b3", writes=["b3"])
        P.dma("sp", brow[:], C.inp("hy_biasr")[l], key="brow", writes=["brow"])
        P.dma("sp", dec[:], C.inp("decayP"), key="dec", writes=["dec"])
        P.memset("dve", onesr[:], 1.0, ["onesr"])
        hP = P.sb([128, 2, 1024], F32, "hP")
        pA = [P.ps([128, 512]) for _ in range(2)]
        for ti in range(2):
            for cb in range(2):
                pb = cb
                P.mm(pA[pb][:], hid2[:, ti * 128:(ti + 1) * 128], w3[:, cb * 512:(cb + 1) * 512], True, False, ["hid2", "w3"], ["psA%d" % pb])
                P.mm(pA[pb][:], onesr[:], b3[:, cb * 512:(cb + 1) * 512], False, True, ["onesr", "b3"], ["psA%d" % pb])
                for od in range(2):
                    P.tt("dve", hP[:, ti, cb * 512 + od * 256:cb * 512 + (od + 1) * 256], pA[pb][:, od * 256:(od + 1) * 256], dec[:, ti, :],
                         ALU.mult, ["psA%d" % pb, "dec"], ["hP"])
        hs = P.sb([128, 2, 512], BF16, "hs")
        hd = P.sb([128, 2, 512], BF16, "hd")
        for o in range(2):
            P.memset("dve", hP[0:1, 0, o * 512 + 256:o * 512 + 512], 0.0, ["hP"])
            P.tt("dve", hP[0:1, 0, o * 512:o * 512 + 256], hP[0:1, 0, o * 512:o * 512 + 256], brow[:, o * 256:(o + 1) * 256], ALU.add,
                 ["hP", "brow"], ["hP"])
        for o in range(2):
            fw = hP[:, :, o * 512:o * 512 + 256]
            bw = hP[:, :, o * 512 + 256:o * 512 + 512]
            P.tt("dve", hs[:, :, o * 256:(o + 1) * 256], fw, bw, ALU.add, ["hP"], ["hs"])
            P.tt("dve", hd[:, :, o * 256:(o + 1) * 256], bw, fw, ALU.subtract, ["hP"], ["hd"])
        Kr = P.sb([128, 2, 512], F32, "Kr")
        Ki = P.sb([128, 2, 512], F32, "Ki")
        for fch in range(2):
            for (dst, mat, src, nm) in ((Kr, dC, hs, "hs"), (Ki, dS, hd, "hd")):
                pb = fch % 2
                for ti in range(2):
                    P.mm(pA[pb][:], mat[:, ti, fch * 128:(fch + 1) * 128], src[:, ti, :], ti == 0, ti == 1, ["cst", nm], ["psA%d" % pb])
                P.cp("act", dst[:, fch, :], pA[pb][:], ["psA%d" % pb], ["K"])
        for ti in range(2):
            P.mm(pA[0][0:1, :], dAlt[:, ti:ti + 1], hs[:, ti, :], ti == 0, ti == 1, ["cst", "hs"], ["psA0"])
        P.cp("act", Ki[0:1, 0, :], pA[0][0:1, :], ["psA0", "K"], ["K"])
        zh = P.sb([128, 6, 1024], BF16, "zh")
        P.dma("sp", zh[:], C.zhT_d.rearrange("(c p) t -> p c t", p=128), key="zh", writes=["zh"])
        o_w, _ = C.ppoff["hsw_%d" % l]
        o_b, _ = C.ppoff["hsb_%d" % l]
        wsc = lambda ch, k: C.pp[:, o_w + ch * 3 + k:o_w + ch * 3 + k + 1]
        zc = P.sb([128, 6, 1024], BF16, "zc")
        ztmp = [P.sb([128, 1024], F32, "ztmp") for _ in range(2)]
        for ch in range(6):
            b = ch % 2
            P.act(ztmp[b][:], zh[:, ch, :], AF.Identity, ["zh"], ["ztmp%d" % b], scale=wsc(ch, 1), bias=C.pp[:, o_b + ch:o_b + ch + 1])
            c3 = ztmp[b][:].rearrange("p (s t) -> p s t", s=4)
            z3 = zh[:, ch, :].rearrange("p (s t) -> p s t", s=4)
            P.stt(c3[:, :, 1:256], z3[:, :, 0:255], wsc(ch, 0), c3[:, :, 1:256], ALU.mult, ALU.add, ["zh", "ztmp%d" % b], ["ztmp%d" % b])
            P.stt(c3[:, :, 0:255], z3[:, :, 1:256], wsc(ch, 2), c3[:, :, 0:255], ALU.mult, ALU.add, ["zh", "ztmp%d" % b], ["ztmp%d" % b])
            P.cp("pool", zc[:, ch, :], ztmp[b][:], ["ztmp%d" % b], ["zc"])
        vtok = P.sb([128, 2, 4, 256], BF16, "vtok")
        Yr = P.sb([128, 2, 1024], BF16, "Yr")
        Yi = P.sb([128, 2, 1024], BF16, "Yi")
        r0b = [P.sb([1, 256], F32, "r0b") for _ in range(2)]
        u1 = P.sb([128, 2, 1024], BF16, "u1")
        ta = [P.sb([128, 512], F32, "ta") for _ in range(2)]
        tb_ = [P.sb([128, 512], F32, "tb") for _ in range(2)]
        pT = P.ps([128, 1024], BF16)
        pU = [P.ps([128, 512]) for _ in range(4)]
        for o in range(2):
            src = zc if o == 0 else None
            for cc in range(2):
                for sq_ in range(4):
                    for ti in range(2):
                        col = (sq_ * 2 + ti) * 128
                        inp_ = (zc[:, cc, col:col + 128] if o == 0 else u1[:, cc, col:col + 128])
                        P.tr(pT[:, (sq_ * 2 + ti) * 128:(sq_ * 2 + ti + 1) * 128], inp_, C.ident_b[:], ["zc", "u1"], ["psT"])
                pv = pT[:].rearrange("p (s t c) -> p t s c", s=4, t=2)
                P.cp("dve", vtok[:, :, :, cc * 128:(cc + 1) * 128], pv, ["psT"], ["vtok"])
            n_ = 0
            for fch in range(2):
                for cb in range(2):
                    pr, pi_ = pU[(n_ % 2) * 2], pU[(n_ % 2) * 2 + 1]
                    kr_, ki_ = "psU%d" % ((n_ % 2) * 2), "psU%d" % ((n_ % 2) * 2 + 1)
                    n_ += 1
                    rhs = lambda ti: vtok[:, ti, cb * 2:cb * 2 + 2, :].rearrange("p s c -> p (s c)")
                    for ti in range(2):
                        P.mm(pr[:], dC[:, ti, fch * 128:(fch + 1) * 128], rhs(ti), ti == 0, ti == 1, ["cst", "vtok"], [kr_])
                    for ti in range(2):
                        P.mm(pi_[:], dSn[:, ti, fch * 128:(fch + 1) * 128], rhs(ti), ti == 0, ti == 1, ["cst", "vtok"], [ki_])
                    b = n_ % 2
                    for bh in range(2):
                        sl = slice(bh * 256, (bh + 1) * 256)
                        kr = Kr[:, fch, o * 256:(o + 1) * 256]
                        ki = Ki[:, fch, o * 256:(o + 1) * 256]
                        osl = slice(cb * 512 + bh * 256, cb * 512 + (bh + 1) * 256)
                        P.tt("dve", ta[b][:, sl], pr[:, sl], kr, ALU.mult, [kr_, "K"], ["ta%d" % b])
                        P.tt("dve", tb_[b][:, sl], pi_[:, sl], ki, ALU.mult, [ki_, "K"], ["tb%d" % b])
                        P.tt("pool", Yr[:, fch, osl], ta[b][:, sl], tb_[b][:, sl], ALU.subtract, ["ta%d" % b, "tb%d" % b], ["Y"])
                        if fch == 0:
                            P.cp("pool", Yr[0:1, 0, osl], ta[b][0:1, sl], ["ta%d" % b, "Y"], ["Y"])
                            P.cp("pool", r0b[bh][:], tb_[b][0:1, sl], ["tb%d" % b], ["r0b%d" % bh])
                        P.tt("dve", ta[b][:, sl], pr[:, sl], ki, ALU.mult, [kr_, "K"], ["ta%d" % b])
                        P.tt("dve", tb_[b][:, sl], pi_[:, sl], kr, ALU.mult, [ki_, "K"], ["tb%d" % b])
                        P.tt("pool", Yi[:, fch, osl], ta[b][:, sl], tb_[b][:, sl], ALU.add, ["ta%d" % b, "tb%d" % b], ["Y"])
                        if fch == 0:
                            P.cp("pool", Yi[0:1, 0, osl], r0b[bh][:], ["r0b%d" % bh, "Y"], ["Y"])
            n_ = 0
            for sq_ in range(4):
                for cc in range(2):
                    pb = n_ % 4
                    n_ += 1
                    cs = slice(sq_ * 256 + cc * 128, sq_ * 256 + (cc + 1) * 128)
                    for fch in range(2):
                        P.mm(pU[pb][:, 0:256], Yr[:, fch, cs], iC[:, fch, :], fch == 0, False, ["Y", "cst"], ["psU%d" % pb])
                        P.mm(pU[pb][:, 0:256], Yi[:, fch, cs], iSn[:, fch, :], False, fch == 1, ["Y", "cst"], ["psU%d" % pb])
                    gate = zc[:, (2 if o == 0 else 4) + cc, sq_ * 256:(sq_ + 1) * 256]
                    if o == 0:
                        dst = u1[:, cc, sq_ * 256:(sq_ + 1) * 256]
                        P.tt("dve", dst, pU[pb][:, 0:256], gate, ALU.mult, ["psU%d" % pb, "zc"], ["u1"])
                    else:
                        dst = C.mixT[:, 6 + cc, sq_ * 256:(sq_ + 1) * 256]
                        P.tt("dve", dst, pU[pb][:, 0:256], gate, ALU.mult, ["psU%d" % pb, "zc"], ["mixH"])


def ph_hyena_s(C, l):
    hs_select(C, l)
    hs_filter(C, l)
    hs_kspec(C, l)
    for o in range(2):
        hs_fwd(C, l, o)
        hs_inv(C, l, o)
    ph_ag(C, C.payH, C.gatH, "agH")
    hs_gather(C, l)


def hs_select(C, l):
    nc = C.nc
    with Phase(nc, "hsa") as P:
        selb = P.sb([128, 2, 64], BF16, "selb")
        P.dma("sp", selb[:], C.inp("selM"), key="sel", writes=["sel"])
        zgb = [P.sb([128, 2, 4096], BF16, "zgb") for _ in range(2)]
        zraw = P.sb([64, 4098], F32, "zraw")
        cbuf = [P.sb([64, 4096], F32, "cbuf") for _ in range(2)]
        ps = [P.ps([128, 512]) for _ in range(4)]
        o_w, _ = C.ppoff["hswm_%d" % l]
        o_b, _ = C.ppoff["hsbm_%d" % l]
        P.memset("dve", zraw[:, 0:1], 0.0, ["zraw"])
        P.memset("dve", zraw[:, 4097:4098], 0.0, ["zraw"])
        n_ = 0
        for blk in range(3):
            zb = blk % 2
            for cc in range(2):
                for r in range(4):
                    P.dma("sp", zgb[zb][:, cc, r * 1024:(r + 1) * 1024], C.gat_zh(r, 2 * blk + cc), key="zgb%d" % zb, writes=["zgb%d" % zb])
            for tb in range(8):
                pb = n_ % 4
                n_ += 1
                for cc in range(2):
                    P.mm(ps[pb][0:64, :], selb[:, cc, :], zgb[zb][:, cc, tb * 512:(tb + 1) * 512], cc == 0, cc == 1,
                         ["sel", "zgb%d" % zb], ["ps%d" % pb])
                if tb % 2:
                    P.cp("act", zraw[:, 1 + tb * 512:1 + (tb + 1) * 512], ps[pb][0:64, :], ["ps%d" % pb], ["zraw%d" % tb])
                else:
                    P.cp("dve", zraw[:, 1 + tb * 512:1 + (tb + 1) * 512], ps[pb][0:64, :], ["ps%d" % pb], ["zraw%d" % tb])
            w = lambda k: C.pp[0:64, o_w + blk * 3 + k:o_w + blk * 3 + k + 1]
            cb = cbuf[blk % 2]
            ck = "cbuf%d" % (blk % 2)
            zk = ["zraw"] + ["zraw%d" % i for i in range(8)]
            P.act(cb[:], zraw[:, 1:4097], AF.Identity, zk, [ck], scale=w(1), bias=C.pp[0:64, o_b + blk:o_b + blk + 1])
            P.stt(cb[:], zraw[:, 0:4096], w(0), cb[:], ALU.mult, ALU.add, zk + [ck], [ck])
            P.stt(cb[:], zraw[:, 2:4098], w(2), cb[:], ALU.mult, ALU.add, zk + [ck], [ck])
            P.dma("sp", C.hyz_d[blk], cb[:], key=ck, reads=[ck])


def hs_filter(C, l):
    nc = C.nc
    with Phase(nc, "hsf") as P:
        zT = P.sb([33, 4096], F32, "zT")
        P.dma("sp", zT[:], C.inp("zT4096"), key="zT", writes=["zT"])
        hid2 = P.sb([64, 4096], F32, "hid2")
        filt_mlp(P, C, l, zT, 4096, hid2)
        w3 = P.sb([64, 256], F32, "w3")
        P.dma("sp", w3[:], C.inp("hy_f_w3m")[l], key="w3", writes=["w3"])
        dec = P.sb([128, 4096], F32, "dec")
        P.dma("sp", dec[:], C.inp("decayS"), key="dec", writes=["dec"])
        hT = [P.sb([128, 4096], F32, "hT") for _ in range(2)]
        pA = [P.ps([128, 512]) for _ in range(2)]
        b3 = ppa(C, "hb3m_%d" % l)
        bm = ppa(C, "hbm_%d" % l)
        n_ = 0
        for o in range(2):
            for tb in range(8):
                pb = n_ % 2
                n_ += 1
                P.mm(pA[pb][:], w3[:, o * 128:(o + 1) * 128], hid2[:, tb * 512:(tb + 1) * 512], True, True, ["w3", "hid2"], ["psA%d" % pb])
                P.stt(hT[o][:, tb * 512:(tb + 1) * 512], pA[pb][:], b3[:, o:o + 1], dec[:, tb * 512:(tb + 1) * 512], ALU.add, ALU.mult,
                      ["psA%d" % pb, "dec"], ["hT%d" % o])
            P.memset("dve", hT[o][64:128, 0:1], 0.0, ["hT%d" % o])
            P.tt("dve", hT[o][0:64, 0:1], hT[o][0:64, 0:1], bm[0:64, o:o + 1], ALU.add, ["hT%d" % o], ["hT%d" % o])
            P.dma("sp", C.hyf_d[o], hT[o][:], key="hTo%d" % o, reads=["hT%d" % o])


def fft_tables(P, C):
    W1 = P.sb([32, 192], BF16, "W1")
    Gc = P.sb([128, 64, 128], BF16, "Gc")
    Gs = P.sb([128, 64, 128], BF16, "Gs")
    P.dma("sp", W1[:], C.inp("fW1"), key="ftabw", writes=["ftabW"])
    P.dma("sp", Gc[:].rearrange("p a b -> p (a b)"), C.inp("fGc"), key="ftabc", writes=["ftab"])
    P.dma("sp", Gs[:].rearrange("p a b -> p (a b)"), C.inp("fGs"), key="ftabs", writes=["ftabS"])
    return W1, Gc, Gs


def fft_fwd(P, X, W1, Gc, Gs, A, psA, psU, handler):
    for c2 in range(32):
        pb = c2 % 2
        for i in range(2):
            P.mm(psA[pb][:, i * 192:(i + 1) * 192], X[:, 2 * c2 + i, :], W1[:], True, True, ["X", "ftabW"], ["psA%d" % pb])
        src = psA[pb][:, 0:384].rearrange("p (i f) -> p i f", i=2)
        if c2 % 2:
            P.cp("act", A[:, 2 * c2:2 * c2 + 2, :], src, ["psA%d" % pb], ["A%d" % c2])
        else:
            P.cp("dve", A[:, 2 * c2:2 * c2 + 2, :], src, ["psA%d" % pb], ["A%d" % c2])
    for grp in range(8):
        b = grp % 2
        pre, pim = psU[2 * b], psU[2 * b + 1]
        kre, kim = "psU%d" % (2 * b), "psU%d" % (2 * b + 1)
        for fi in range(8):
            f1 = grp * 8 + fi
            sl = slice(fi * 64, (fi + 1) * 64)
            akeys = ["ftab", "ftabS"] + ["A%d" % i for i in range(32)]
            P.mm(pre[:, sl], Gc[:, f1, :], A[:, :, f1], True, False, akeys, [kre])
            P.mm(pre[:, sl], Gs[:, f1, :], A[:, :, 64 + f1], False, True, akeys, [kre])
            P.mm(pim[:, sl], Gc[:, f1, :], A[:, :, 64 + f1], True, False, akeys, [kim])
            P.mm(pim[:, sl], Gs[:, f1, :], A[:, :, 128 + f1], False, True, akeys, [kim])
        handler(grp, pre, pim, kre, kim)


def hs_kspec(C, l):
    nc = C.nc
    with Phase(nc, "hsk") as P:
        _kspec_body(C, l, P)


def _kspec_body(C, l, P):
    X = P.sb([32, 64, 128], BF16, "X")
    P.dma("pool", X[:], C.hyf_d[0][0:64, :].rearrange("c (a b) -> a c b", b=128), key="X", writes=["X"])
    W1, Gc, Gs = fft_tables(P, C)
    A = P.sb([128, 64, 192], BF16, "A")
    KA = [P.sb([128, 4096], F32, "KA") for _ in range(2)]
    Ko = [P.sb([128, 512], BF16, "Ko") for _ in range(4)]
    psA = [P.ps([128, 512]) for _ in range(2)]
    psU = [P.ps([128, 512]) for _ in range(4)]
    cnt = [0]
    for o in range(2):
        for dd in range(2):
            if (o, dd) != (0, 0):
                P.dma("pool", X[:], C.hyf_d[o][dd * 64:(dd + 1) * 64, :].rearrange("c (a b) -> a c b", b=128), key="X", writes=["X"])

            def handler(grp, pre, pim, kre, kim, dd=dd, o=o):
                sl = slice(grp * 512, (grp + 1) * 512)
                if dd == 0:
                    P.cp("act", KA[0][:, sl], pre[:], [kre], ["KAr"])
                    P.cp("dve", KA[1][:, sl], pim[:], [kim], ["KAi"])
                else:
                    for ri, (ps_, kk, op, kak) in enumerate(((pre, kre, ALU.add, "KAr"), (pim, kim, ALU.subtract, "KAi"))):
                        si = cnt[0] % 4
                        cnt[0] += 1
                        P.tt("dve", Ko[si][:], KA[ri][:, sl], ps_[:], op, [kak, kk], ["Ko%d" % si])
                        P.dma("sp", C.Ksp_d[o, ri][:, sl], Ko[si][:], key="Ko%d" % si, reads=["Ko%d" % si])
            fft_fwd(P, X, W1, Gc, Gs, A, psA, psU, handler)


def hs_fwd(C, l, o):
    nc = C.nc
    with Phase(nc, "hsc") as P:
        X = P.sb([32, 64, 128], BF16, "X")
        if o == 0:
            P.dma("pool", X[:], C.hyz_d[0].rearrange("c (a b) -> a c b", b=128), key="X", writes=["X"])
        else:
            P.dma("sp", X[:].rearrange("p c t -> p (c t)"), C.X2_d, key="X", writes=["X"])
        W1, Gc, Gs = fft_tables(P, C)
        A = P.sb([128, 64, 192], BF16, "A")
        Ks = [P.sb([128, 4096], BF16, "Ks") for _ in range(2)]
        for ri in range(2):
            P.dma("sp", Ks[ri][:], C.Ksp_d[o, ri], key="Ks", writes=["Ks"])
        Ys = [P.sb([128, 4096], BF16, "Ys") for _ in range(2)]
        ta = [P.sb([128, 512], F32, "ta") for _ in range(2)]
        tb_ = [P.sb([128, 512], F32, "tb") for _ in range(2)]
        psA = [P.ps([128, 512]) for _ in range(2)]
        psU = [P.ps([128, 512]) for _ in range(4)]

        def handler(grp, pre, pim, kre, kim):
            sl = slice(grp * 512, (grp + 1) * 512)
            b = grp % 2
            P.tt("dve", ta[b][:], pre[:], Ks[0][:, sl], ALU.mult, [kre, "Ks"], ["ta%d" % b])
            P.tt("dve", tb_[b][:], pim[:], Ks[1][:, sl], ALU.mult, [kim, "Ks"], ["tb%d" % b])
            P.tt("pool", Ys[0][:, sl], ta[b][:], tb_[b][:], ALU.subtract, ["ta%d" % b, "tb%d" % b], ["Ys"])
            P.tt("dve", ta[b][:], pre[:], Ks[1][:, sl], ALU.mult, [kre, "Ks"], ["ta%d" % b])
            P.tt("dve", tb_[b][:], pim[:], Ks[0][:, sl], ALU.mult, [kim, "Ks"], ["tb%d" % b])
            P.tt("pool", Ys[1][:, sl], ta[b][:], tb_[b][:], ALU.add, ["ta%d" % b, "tb%d" % b], ["Ys"])
        fft_fwd(P, X, W1, Gc, Gs, A, psA, psU, handler)
        for ri in range(2):
            P.dma("sp", C.Y_d[ri], Ys[ri][:], key="Yo", reads=["Ys"])


def hs_inv(C, l, o):
    nc = C.nc
    with Phase(nc, "hsi") as P:
        E1 = P.sb([128, 256], BF16, "E1"); E2 = P.sb([128, 256], BF16, "E2")
        Vc = P.sb([64, 128, 32], BF16, "Vc"); Vsn = P.sb([64, 128, 32], BF16, "Vsn")
        P.dma("sp", E1[:], C.inp("fE1"), key="itab1", writes=["itabE1"])
        P.dma("sp", E2[:], C.inp("fE2"), key="itab2", writes=["itabE2"])
        Ys = [P.sb([128, 64, 64], BF16, "Ys") for _ in range(2)]
        for ri in range(2):
            P.dma("sp", Ys[ri][:].rearrange("p f c -> p (f c)"), C.Y_d[ri], key="Ys", writes=["Ys"])
        P.dma("sp", Vc[:].rearrange("p a b -> p (a b)"), C.inp("fVc"), key="itab3", writes=["itabVc"])
        P.dma("sp", Vsn[:].rearrange("p a b -> p (a b)"), C.inp("fVsn"), key="itab4", writes=["itabVs"])
        Gx = P.sb([32, 64, 128], BF16, "Gx")
        P.dma("pool", Gx[:], C.hyz_d[1 + o].rearrange("c (a b) -> a c b", b=128), key="Gx", writes=["Gx"])
        Br = P.sb([64, 128, 64], BF16, "Br")
        Bi = P.sb([64, 128, 64], BF16, "Bi")
        X2 = P.sb([32, 64, 128], BF16, "X2")
        ps = [P.ps([128, 512]) for _ in range(4)]
        for c2 in range(32):
            pb = c2 % 4
            for i in range(2):
                c = 2 * c2 + i
                P.mm(ps[pb][0:64, i * 256:(i + 1) * 256], Ys[0][:, :, c], E1[:], True, False, ["Ys", "itabE1"], ["ps%d" % pb])
                P.mm(ps[pb][0:64, i * 256:(i + 1) * 256], Ys[1][:, :, c], E2[:], False, True, ["Ys", "itabE2"], ["ps%d" % pb])
            pv = ps[pb][0:64, :].rearrange("p (i r t) -> p r t i", i=2, r=2)
            P.cp("act", Br[:, :, 2 * c2:2 * c2 + 2], pv[:, 0], ["ps%d" % pb], ["Br%d" % c2])
            P.cp("dve", Bi[:, :, 2 * c2:2 * c2 + 2], pv[:, 1], ["ps%d" % pb], ["Bi%d" % c2])
        for tg in range(16):
            pb = tg % 4
            for ti in range(8):
                t2 = tg * 8 + ti
                sl = slice(ti * 64, (ti + 1) * 64)
                bkeys = ["itabVc", "itabVs"] + ["Br%d" % i for i in range(32)] + ["Bi%d" % i for i in range(32)]
                P.mm(ps[pb][0:32, sl], Vc[:, t2, :], Br[:, t2, :], True, False, bkeys, ["ps%d" % pb])
                P.mm(ps[pb][0:32, sl], Vsn[:, t2, :], Bi[:, t2, :], False, True, bkeys, ["ps%d" % pb])
            pv = ps[pb][0:32, :].rearrange("p (t c) -> p c t", c=64)
            P.tt("dve", X2[:, :, tg * 8:(tg + 1) * 8], pv, Gx[:, :, tg * 8:(tg + 1) * 8], ALU.mult, ["ps%d" % pb, "Gx"], ["X2"])
        dst = C.X2_d if o == 0 else C.payH.ap()
        P.dma("sp", dst, X2[:].rearrange("p c t -> p (c t)"), key="X2", reads=["X2"])


def hs_gather(C, l):
    nc = C.nc
    with Phase(nc, "hsg") as P:
        hyall = P.sb([128, 2, 4096], BF16, "hyall")
        for r in range(4):
            dst = hyall[(r % 2) * 64:(r % 2) * 64 + 64, r // 2, :].rearrange("p (a b) -> p a b", b=128)
            src = C.gatH.ap()[r * 32:(r + 1) * 32, :].rearrange("a (c b) -> c a b", b=128)
            P.dma("sp", dst, src, key="hyall", writes=["hyall"])
        eq = ppa(C, "eq")
        acc = P.sb([128, 2, 1024], F32, "acc")
        for k in range(2):
            P.ts("dve", acc[:, k, :], hyall[:, k, 0:1024], eq[:, 0:1], ALU.mult, ["hyall"], ["acc"])
            for j in range(1, 4):
                P.stt(acc[:, k, :], hyall[:, k, j * 1024:(j + 1) * 1024], eq[:, j:j + 1], acc[:, k, :], ALU.mult, ALU.add,
                      ["hyall", "acc"], ["acc"])
            P.cp("act", C.mixT[:, 6 + k, 0:1024], acc[:, k, :], ["acc"], ["mixH"])


_CACHE = {}


def kernel(**inputs):
    inp = {k: np.asarray(v) for k, v in inputs.items()}
    maps, ppoff = host_prep(inp)
    npp = maps[0]["pp"].shape[1]
    if "nc" not in _CACHE:
        _CACHE["nc"] = build(ppoff, npp)
    nc, C = _CACHE["nc"]
    maps = [{k: v for k, v in m.items() if k in C.declared} for m in maps]
    res = run_bass_kernel_spmd(nc, maps, core_ids=list(range(NCORE)))
    return assemble(res.results)


def assemble(results):
    y_prompt = np.zeros((32, 256, D), np.float32)
    y_sample = np.zeros((2, 4096, D), np.float32)
    new_k = np.zeros((32, DEPTH, 256, 2, 64), np.float32)
    new_v = np.zeros((32, DEPTH, 256, 2, 64), np.float32)
    for i, r in enumerate(results):
        b, q = i // 4, i % 4
        y = np.asarray(r["y_tok"])
        y_prompt[4 * i:4 * i + 4] = y[:1024].reshape(4, 256, D)
        y_sample[b, 1024 * q:1024 * q + 1024] = y[1024:]
        new_k[4 * i:4 * i + 4] = np.asarray(r["new_k"]).reshape(4, DEPTH, 256, 2, 64)
        new_v[4 * i:4 * i + 4] = np.asarray(r["new_v"]).reshape(4, DEPTH, 256, 2, 64)
    return (y_prompt, y_sample, new_k, new_v)
```
